# Optimizing a Trainium2 kernel written in Bass

```python
import math
import jax, jax.numpy as jnp
from jax import lax
import numpy as np

D_MODEL = 1024
BATCH = 8
SEQ = 2048
DEPTH = 2

N_META = 16
A_HEADS = 8
A_KV_HEADS = 2
A_GROUP = A_HEADS // A_KV_HEADS
HEAD_DIM = 64
WINDOW = 128
BLOCK = 128
A_WIDTH = A_HEADS * HEAD_DIM
KV_WIDTH = A_KV_HEADS * HEAD_DIM
B_WIDTH = D_MODEL // 2
CONV_WIDTH = 3
IN_WIDTH = A_WIDTH + 2 * KV_WIDTH + 3 * B_WIDTH
MIX_WIDTH = A_WIDTH + B_WIDTH
SPLITS = (A_WIDTH, A_WIDTH + KV_WIDTH, A_WIDTH + 2 * KV_WIDTH,
          A_WIDTH + 2 * KV_WIDTH + B_WIDTH, A_WIDTH + 2 * KV_WIDTH + 2 * B_WIDTH)
POOL_SIZES = (2, 4, 8, 16)
N_POOL_GROUPS = 4
POOL_GROUP_DIM = D_MODEL // N_POOL_GROUPS
REL_BUCKETS = 32
REL_MAX_DIST = 128
D_FF = 2816
HALF_STEP = 0.5
RMS_EPS = 1e-6
N_EVEN = (DEPTH + 1) // 2
N_ODD = DEPTH // 2

kernel_name = "hybrid_swa_conv_pool_macaron"


def rmsnorm(x, g):
    xf = x.astype(jnp.float32)
    y = xf * lax.rsqrt(jnp.mean(xf * xf, axis=-1, keepdims=True) + RMS_EPS)
    return (y * g.astype(jnp.float32)).astype(x.dtype)


def swiglu(x, w1, w2):
    gate, up = jnp.split(x @ w1, 2, axis=-1)
    return (jax.nn.silu(gate) * up) @ w2


def t5_bucket(dist):
    n = jnp.maximum(dist, 0)
    max_exact = REL_BUCKETS // 2
    nf = jnp.maximum(n, 1).astype(jnp.float32)
    large = max_exact + (jnp.log(nf / max_exact) / math.log(REL_MAX_DIST / max_exact)
                         * (REL_BUCKETS - max_exact)).astype(jnp.int32)
    large = jnp.minimum(large, REL_BUCKETS - 1)
    return jnp.where(n < max_exact, n, large)


def sliding_window_attention(q, k, v, sinks, rel_bias):
    Bsz, L = q.shape[0], q.shape[1]
    pad = BLOCK - N_META
    nb = (L + pad) // BLOCK
    q = q * (HEAD_DIM ** -0.5)
    padr = lambda t: jnp.pad(t, ((0, 0), (pad, 0), (0, 0)))
    qb = padr(q).reshape(Bsz, nb, BLOCK, A_KV_HEADS, A_GROUP, HEAD_DIM)
    kb = padr(k).reshape(Bsz, nb, BLOCK, A_KV_HEADS, HEAD_DIM)
    vb = padr(v).reshape(Bsz, nb, BLOCK, A_KV_HEADS, HEAD_DIM)

    def band(t):
        prev = jnp.pad(t[:, :-1], ((0, 0), (1, 0), (0, 0), (0, 0), (0, 0)))
        return jnp.concatenate([prev, t], axis=2)

    k_band, v_band = band(kb), band(vb)
    k_meta = k[:, :N_META].reshape(Bsz, N_META, A_KV_HEADS, HEAD_DIM)
    v_meta = v[:, :N_META].reshape(Bsz, N_META, A_KV_HEADS, HEAD_DIM)

    blk = jnp.arange(nb)[:, None, None]
    r = jnp.arange(BLOCK)[None, :, None]
    s = jnp.arange(2 * BLOCK)[None, None, :]
    dist_band = BLOCK + r[0] - s[0]
    k_pos = (blk - 1) * BLOCK + s
    band_ok = (dist_band >= 0) & (dist_band < WINDOW) & (k_pos >= BLOCK)
    bias_band = jnp.moveaxis(rel_bias[t5_bucket(dist_band)].astype(jnp.float32), -1, 0)
    bias_band = bias_band.reshape(A_KV_HEADS, A_GROUP, 1, BLOCK, 2 * BLOCK)
    q_pos = blk * BLOCK + r
    dist_meta = q_pos - (pad + jnp.arange(N_META)[None, None, :])
    meta_ok = dist_meta >= 0
    bias_meta = jnp.moveaxis(rel_bias[t5_bucket(dist_meta)].astype(jnp.float32), -1, 0)
    bias_meta = bias_meta.reshape(A_KV_HEADS, A_GROUP, nb, BLOCK, N_META)

    neg = jnp.float32(-jnp.inf)
    lg_band = jnp.einsum('bnqkgd,bnskd->bkgnqs', qb, k_band).astype(jnp.float32) + bias_band
    lg_band = jnp.where(band_ok, lg_band, neg)
    lg_meta = jnp.einsum('bnqkgd,bmkd->bkgnqm', qb, k_meta).astype(jnp.float32) + bias_meta
    lg_meta = jnp.where(meta_ok, lg_meta, neg)
    sink = jnp.broadcast_to(sinks.astype(jnp.float32).reshape(1, A_KV_HEADS, A_GROUP, 1, 1, 1),
                            lg_band.shape[:-1] + (1,))
    probs = jax.nn.softmax(jnp.concatenate([lg_band, lg_meta, sink], axis=-1), axis=-1)
    p_band = probs[..., :2 * BLOCK].astype(v.dtype)
    p_meta = probs[..., 2 * BLOCK:2 * BLOCK + N_META].astype(v.dtype)
    out = (jnp.einsum('bkgnqs,bnskd->bnqkgd', p_band, v_band)
           + jnp.einsum('bkgnqm,bmkd->bnqkgd', p_meta, v_meta))
    return out.reshape(Bsz, nb * BLOCK, A_WIDTH)[:, pad:]


def short_conv(b_gate, c_gate, x_in, conv_w):
    u = c_gate * x_in
    L = u.shape[1]
    up = jnp.pad(u, ((0, 0), (CONV_WIDTH - 1, 0), (0, 0)))
    y = conv_w[0] * u
    for j in range(1, CONV_WIDTH):
        y = y + conv_w[j] * up[:, CONV_WIDTH - 1 - j:CONV_WIDTH - 1 - j + L]
    return b_gate * y


def attn_conv_mixer(u, w_in, w_out, sinks, conv_w, rel_bias):
    z = u @ w_in
    q, k, v, b_gate, c_gate, x_in = jnp.split(z, SPLITS, axis=-1)
    y_attn = sliding_window_attention(q, k, v, sinks, rel_bias)
    y_conv = short_conv(b_gate, c_gate, x_in, conv_w)
    return jnp.concatenate([y_attn, y_conv], axis=-1) @ w_out


def pool_mixer(u, pool_w, pool_scale):
    Bsz, L, _ = u.shape
    xg = u.astype(jnp.float32).reshape(Bsz, L, N_POOL_GROUPS, POOL_GROUP_DIM)
    cs = jnp.pad(jnp.cumsum(xg, axis=1), ((0, 0), (1, 0), (0, 0), (0, 0)))
    t = jnp.arange(L)
    pooled = []
    for gi, w in enumerate(POOL_SIZES):
        start = jnp.pad(cs[:, :L + 1 - w, gi], ((0, 0), (w - 1, 0), (0, 0)))
        cnt = jnp.minimum(t + 1, w).astype(jnp.float32)[None, :, None]
        pooled.append((cs[:, 1:, gi] - start) / cnt)
    d = jnp.stack(pooled, axis=2) - xg
    y = jnp.einsum('blgc,gcd->blgd', d, pool_w.astype(jnp.float32)).reshape(Bsz, L, D_MODEL)
    return (y * pool_scale.astype(jnp.float32)).astype(u.dtype)


def setup_inputs(seed: int = 0) -> dict:
    key = jax.random.key(seed)
    ks = jax.random.split(key, 12)
    nrm = lambda k, shape, s: jax.random.normal(k, shape, jnp.float32) * s
    return {
        "x": nrm(ks[0], (BATCH, SEQ, D_MODEL), 1.0),
        "meta_tokens": nrm(ks[1], (N_META, D_MODEL), 1.0),
        "rel_bias": nrm(ks[2], (REL_BUCKETS, A_HEADS), 0.5),
        "norm_g": 1.0 + nrm(ks[3], (DEPTH, 6, D_MODEL), 0.1),
        "ffn_w1": nrm(ks[4], (DEPTH, 2, D_MODEL, 2 * D_FF), D_MODEL ** -0.5),
        "ffn_w2": nrm(ks[5], (DEPTH, 2, D_FF, D_MODEL), D_FF ** -0.5),
        "mix_w_in": nrm(ks[6], (N_EVEN, D_MODEL, IN_WIDTH), D_MODEL ** -0.5),
        "mix_w_out": nrm(ks[7], (N_EVEN, MIX_WIDTH, D_MODEL), MIX_WIDTH ** -0.5),
        "attn_sinks": nrm(ks[8], (N_EVEN, A_HEADS), 0.5),
        "conv_w": nrm(ks[9], (N_EVEN, CONV_WIDTH, B_WIDTH), CONV_WIDTH ** -0.5),
        "pool_w": nrm(ks[10], (N_ODD, N_POOL_GROUPS, POOL_GROUP_DIM, POOL_GROUP_DIM), POOL_GROUP_DIM ** -0.5),
        "pool_scale": 1.0 + nrm(ks[11], (N_ODD, D_MODEL), 0.1),
    }


def reference(x, meta_tokens, rel_bias, norm_g, ffn_w1, ffn_w2, mix_w_in, mix_w_out,
              attn_sinks, conv_w, pool_w, pool_scale):
    Bsz = x.shape[0]
    meta = jnp.broadcast_to(meta_tokens.astype(x.dtype)[None], (Bsz, N_META, D_MODEL))
    h = jnp.concatenate([meta, x], axis=1)
    for layer in range(DEPTH):
        g = norm_g[layer]
        h = h + HALF_STEP * rmsnorm(swiglu(rmsnorm(h, g[0]), ffn_w1[layer, 0], ffn_w2[layer, 0]), g[1])
        u = rmsnorm(h, g[2])
        if layer % 2 == 0:
            e = layer // 2
            mix = attn_conv_mixer(u, mix_w_in[e], mix_w_out[e], attn_sinks[e], conv_w[e], rel_bias)
        else:
            o = layer // 2
            mix = pool_mixer(u, pool_w[o], pool_scale[o])
        h = h + rmsnorm(mix, g[3])
        h = h + HALF_STEP * rmsnorm(swiglu(rmsnorm(h, g[4]), ffn_w1[layer, 1], ffn_w2[layer, 1]), g[5])
    return h[:, N_META:]
```

```python
import contextlib
import numpy as np
import concourse.bass as bass
import concourse.mybir as mybir
from concourse.bass_utils import run_bass_kernel_spmd

F32 = mybir.dt.float32
BF16 = mybir.dt.bfloat16
AF = mybir.ActivationFunctionType
ALU = mybir.AluOpType

D = 1024
KC = 8
NMETA = 16
SEQ = 2048
T = NMETA + SEQ
DFF = 2816
FC = 22
EPS = 1e-6
TT = [(0, 16), (16, 512), (528, 512), (1040, 512), (1552, 512)]
PASSES = [[0, 1, 2], [3, 4]]
PLO = [0, 1040]
PW = 1040
POOL_SIZES = (2, 4, 8, 16)
SLAB = 2048
NS = 2
NB = 3


class _Op:
    __slots__ = ("eng", "fn", "reads", "writes", "dma", "deps", "raw", "signal", "count", "idx")


class Sched:
    def __init__(self, same_engine_sync=True):
        self.ops = []
        self.last_w = {}
        self.readers = {}
        self.same_engine_sync = same_engine_sync

    def add(self, eng, fn, reads=(), writes=(), dma=None):
        op = _Op()
        op.eng = eng
        op.fn = fn
        op.dma = dma
        op.reads = tuple(reads)
        op.writes = tuple(writes)
        op.signal = False
        op.count = None
        op.idx = len(self.ops)
        deps = set()
        for r in op.reads:
            w = self.last_w.get(r)
            if w is not None:
                deps.add(w)
        op.raw = set(deps)
        for w_ in op.writes:
            w = self.last_w.get(w_)
            if w is not None:
                deps.add(w)
            for rd in self.readers.get(w_, ()):
                deps.add(rd)
        deps.discard(op.idx)
        for r in op.reads:
            self.readers.setdefault(r, []).append(op.idx)
        for w_ in op.writes:
            self.last_w[w_] = op.idx
            self.readers[w_] = []
        op.deps = deps
        self.ops.append(op)
        return op

    @staticmethod
    def _key(p):
        return ("dma", p.dma) if p.dma is not None else ("eng", p.eng)

    def finalize(self, force_signal=()):
        ops = self.ops
        for op in ops:
            best = {}
            for d in op.deps:
                p = ops[d]
                key = self._key(p)
                if p.dma is None and p.eng == op.eng and op.dma is None:
                    if p.eng == "pe" or not self.same_engine_sync:
                        continue
                if key not in best or best[key] < d:
                    best[key] = d
            op.deps = set(best.values())
            for d in op.deps:
                ops[d].signal = True
        for p in force_signal:
            p.signal = True
        cnt = {}
        for op in ops:
            if op.signal:
                key = self._key(op)
                cnt[key] = cnt.get(key, 0) + (16 if op.dma is not None else 1)
                op.count = cnt[key]
        self.sem_keys = sorted(cnt.keys(), key=str)

    def emit(self, nc, final_waits=()):
        ops = self.ops
        with contextlib.ExitStack() as es:
            sems = {}
            for i, k in enumerate(self.sem_keys):
                sems[k] = es.enter_context(nc.semaphore("s%d" % i))
            block = es.enter_context(nc.Block())

            def run(engname):
                def body(e):
                    waited = {}
                    for op in ops:
                        if op.eng != engname:
                            continue
                        for d in sorted(op.deps):
                            p = ops[d]
                            key = self._key(p)
                            if waited.get(key, 0) >= p.count:
                                continue
                            waited[key] = p.count
                            e.wait_ge(sems[key], p.count)
                        ins = op.fn(e)
                        if op.signal:
                            ins.then_inc(sems[self._key(op)], 16 if op.dma is not None else 1)
                    for (en, p) in final_waits:
                        if en == engname:
                            e.wait_ge(sems[self._key(p)], p.count)
                return body

            block.tensor(run("pe"))
            block.scalar(run("act"))
            block.vector(run("dve"))
            block.gpsimd(run("pool"))
            block.sync(run("sp"))


def _t5_bucket_np(dist):
    n = np.maximum(dist, 0)
    nf = np.maximum(n, 1).astype(np.float32)
    large = 16 + (np.log(nf / np.float32(16)) / np.float32(np.log(128 / 16)) * np.float32(16)).astype(np.int32)
    large = np.minimum(large, 31)
    return np.where(n < 16, n, large)


def _slabs_w1(w1):
    g = w1[:, :DFF].reshape(KC, 128, FC, 128)
    u = w1[:, DFF:].reshape(KC, 128, FC, 128)
    s = np.concatenate([g, u], axis=0)
    return np.ascontiguousarray(s.transpose(2, 1, 0, 3)).reshape(FC, 128, 2048)


def _slabs_w2(w2):
    s = w2.reshape(2, 11, 128, KC, 128)
    return np.ascontiguousarray(s.transpose(3, 0, 2, 1, 4)).reshape(16, 128, 1408)


def _colslab(w):
    return np.ascontiguousarray(w.reshape(KC, 128, 128).transpose(1, 0, 2)).reshape(128, 1024)


def prep_common(inp):
    f = lambda a: np.ascontiguousarray(np.asarray(a, dtype=np.float32))
    norm_g = f(inp["norm_g"])
    w1 = f(inp["ffn_w1"])
    w2 = f(inp["ffn_w2"])
    w_in = f(inp["mix_w_in"])[0]
    w_out = f(inp["mix_w_out"])[0]
    rel_bias = f(inp["rel_bias"])
    sinks = f(inp["attn_sinks"])[0]
    conv_w = f(inp["conv_w"])[0]
    pool_w = f(inp["pool_w"])[0]
    pool_scale = f(inp["pool_scale"])[0]
    meta = f(inp["meta_tokens"])
    out = {}
    out["metaT"] = np.ascontiguousarray(meta.T)
    gT = norm_g.reshape(12, KC, 128).transpose(2, 0, 1).reshape(128, 96)
    psc = pool_scale.reshape(KC, 128).T
    cw = conv_w.reshape(3, 4, 128).transpose(2, 1, 0).reshape(128, 12)
    sinkP = np.repeat(sinks.reshape(4, 2), 64, axis=1).T
    b31 = np.broadcast_to(rel_bias[31][None, :], (128, 8))
    out["vecs"] = np.ascontiguousarray(np.concatenate([gT, psc, cw, sinkP, b31], axis=1))
    out["w1s"] = np.stack([_slabs_w1(w1[l, s]) for l in range(2) for s in range(2)])
    out["w2s"] = np.stack([_slabs_w2(w2[l, s]) for l in range(2) for s in range(2)])
    q0, k0, v0, b0, c0, x0 = 0, 512, 640, 768, 1280, 1792
    z64 = np.zeros((D, 64), np.float32)
    slabs = []
    for ci in range(4):
        slabs.append(_colslab(w_in[:, c0 + ci * 128:c0 + (ci + 1) * 128]))
        slabs.append(_colslab(w_in[:, x0 + ci * 128:x0 + (ci + 1) * 128]))
        slabs.append(_colslab(w_in[:, b0 + ci * 128:b0 + (ci + 1) * 128]))
    for c in range(4):
        slabs.append(_colslab(w_in[:, q0 + c * 128:q0 + (c + 1) * 128]))
    for kv in range(2):
        kk_ = w_in[:, k0 + kv * 64:k0 + (kv + 1) * 64]
        slabs.append(_colslab(np.concatenate([kk_, z64], axis=1)))
        slabs.append(_colslab(np.concatenate([z64, kk_], axis=1)))
    slabs.append(_colslab(w_in[:, v0:v0 + 128]))
    slabs.append(np.zeros((128, 1024), np.float32))
    sl = np.stack(slabs)
    out["wins"] = np.ascontiguousarray(sl.reshape(11, 2, 128, 1024).transpose(0, 2, 1, 3)).reshape(11, 128, 2048)
    wo = np.stack([_colslab(w_out[:, m * 128:(m + 1) * 128]) for m in range(8)])
    out["wouts"] = np.ascontiguousarray(wo.reshape(4, 2, 128, 1024).transpose(0, 2, 1, 3)).reshape(4, 128, 2048)
    pw = pool_w.reshape(4, 2, 128, 2, 128)
    out["pws"] = np.ascontiguousarray(pw.transpose(2, 0, 1, 3, 4)).reshape(128, 2048)
    kk_ = np.arange(128)[:, None]
    nn_ = np.arange(128)[None, :]
    bands = np.zeros((128, 4, 3, 128), np.float32)
    for gi, wsz in enumerate(POOL_SIZES):
        dcur = nn_ - kk_
        bands[:, gi, 0] = np.where((dcur >= 0) & (dcur < wsz), 1.0 / wsz, 0.0) - (dcur == 0)
        bands[:, gi, 1] = np.where(128 + dcur < wsz, 1.0 / wsz, 0.0)
        bands[:16, gi, 2] = np.where(16 + nn_ - kk_[:16] < wsz, 1.0 / wsz, 0.0)
    out["bands"] = np.ascontiguousarray(bands.reshape(128, 1536))
    s_ = np.arange(128)[:, None]
    r_ = np.arange(128)[None, :]
    bk_cur = _t5_bucket_np(r_ - s_)
    bk_prev = _t5_bucket_np(128 + r_ - s_)
    bt = np.stack([rel_bias[bk_cur], rel_bias[bk_prev]], axis=0)
    out["btab"] = np.ascontiguousarray(bt.transpose(1, 3, 0, 2)).reshape(128, 8 * 256)
    mcur = (r_ - s_ >= 0).astype(np.float32)
    mprev = (s_ > r_).astype(np.float32)
    out["mtab"] = np.ascontiguousarray(np.stack([mcur, mprev], axis=1)).reshape(128, 256)
    m_ = np.arange(16)[:, None]
    bk_m1 = _t5_bucket_np(16 + r_ - m_)
    rq = np.arange(16)[None, :]
    bk_m0 = _t5_bucket_np(rq - m_)
    bm = np.concatenate([rel_bias[bk_m1].transpose(0, 2, 1), rel_bias[bk_m0].transpose(0, 2, 1)], axis=2)
    mm0 = np.broadcast_to((rq - m_ >= 0).astype(np.float32)[:, None, :], (16, 8, 16))
    out["bmeta"] = np.ascontiguousarray(np.concatenate([bm, mm0], axis=2)).reshape(16, 8 * 160)
    return out


class Builder:
    def __init__(self, nc, es, stop_after=None, debug=False, same_engine_sync=True):
        self.nc = nc
        self.S = Sched(same_engine_sync)
        self.stop_after = stop_after
        self.debug = debug
        S = self.S
        dt = nc.dram_tensor
        self.xT = dt("xT", [D, SEQ], F32, kind="ExternalInput").ap()
        self.metaT = dt("metaT", [D, NMETA], F32, kind="ExternalInput").ap()
        self.vecs_d = dt("vecs", [128, 128], F32, kind="ExternalInput").ap()
        self.w1s = dt("w1s", [4, FC, 128, 2048], F32, kind="ExternalInput").ap()
        self.w2s = dt("w2s", [4, 16, 128, 1408], F32, kind="ExternalInput").ap()
        self.wins = dt("wins", [11, 128, 2048], F32, kind="ExternalInput").ap()
        self.wouts = dt("wouts", [4, 128, 2048], F32, kind="ExternalInput").ap()
        self.pws = dt("pws", [128, 2048], F32, kind="ExternalInput").ap()
        self.bands_d = dt("bands", [128, 1536], F32, kind="ExternalInput").ap()
        self.btab_d = dt("btab", [128, 2048], F32, kind="ExternalInput").ap()
        self.mtab_d = dt("mtab", [128, 256], F32, kind="ExternalInput").ap()
        self.bmeta_d = dt("bmeta", [16, 1280], F32, kind="ExternalInput").ap()
        self.outT = dt("outT", [D, SEQ], F32, kind="ExternalOutput").ap()

        sb = lambda name, n, dtp: es.enter_context(nc.sbuf_tensor(name, [128, n], dtp))
        self.hT = sb("hT", KC * T, F32)
        self.vecs = sb("vecs_sb", 128, F32)
        self.gc = sb("gc", 96, F32)
        self.ones = sb("ones", 128, BF16)
        self.esink = sb("esink", 4, F32)
        self.eb31 = sb("eb31", 8, F32)
        self.stage = [sb("stage%d" % i, SLAB, F32) for i in range(NS)]
        self.wb = [sb("wb%d" % i, SLAB, BF16) for i in range(NB)]
        self.rstd = [sb("rstd%d" % i, 512, F32) for i in range(2)]
        self.sq = [sb("sq%d" % i, 512, BF16) for i in range(2)]
        self.tmp = [sb("tmp%d" % i, 512, F32) for i in range(2)]
        ARENA_F32 = 25408
        self.arena = sb("arena", ARENA_F32, F32)
        self.junk = sb("junk", 8, F32)
        self.ones1 = sb("ones1", 64, BF16)
        self.ps = [es.enter_context(nc.psum_tensor("ps%d" % i, [128, 512], F32)) for i in range(8)]
        self.bank = 0
        self.cnt = {"rstd": 0, "sq": 0, "tmp": 0, "Pt": 0, "Pm": 0, "d2": 0}
        a = self.arena
        self.xn = a[:, 0:4160].bitcast(BF16)
        self.ybuf = a[:, 4160:4160 + 8320]
        self.act = a[:, 12480:12480 + 11440].bitcast(BF16)
        o = 0
        o_u = o
        self.uT = a[:, o:o + 8256].bitcast(BF16); o += 8256
        o_q = o
        self.qT = a[:, o:o + 4128].bitcast(BF16); o += 4128
        self.kz = a[:, o:o + 4128].bitcast(BF16); o += 4128
        self.Vt = a[:, o:o + 1088].bitcast(BF16); o += 1088
        o_c = o
        self.ucv = a[:, o:o + 2080]; o += 2080
        self.ymC = a[:, o:o + 4128].bitcast(BF16); o += 4128
        self.EBT = a[:, o:o + 1024].bitcast(BF16); o += 1024
        self.EBm1 = a[:, o:o + 512].bitcast(BF16); o += 512
        self.EBm0 = a[:, o:o + 64].bitcast(BF16); o += 64
        assert o <= ARENA_F32, o
        self.ymA = a[:, o_u:o_u + 4128].bitcast(BF16)
        self.woutb = a[:, o_u + 4128:o_u + 4128 + 4096].bitcast(BF16)
        self.ybufM = a[:, o_q + 4128:o_q + 4128 + 4096]
        self.ybufM2 = a[:, o_q:o_q + 4096]
        self.btab = a[:, o_q:o_q + 2048]
        self.mtab = a[:, o_q + 2048:o_q + 2304]
        self.bmeta = a[:, o_q + 2304:o_q + 2304 + 1280]
        oc = o_c
        self.Pt = [a[:, oc + i * 256:oc + (i + 1) * 256].bitcast(BF16) for i in range(2)]; oc += 512
        self.Pm = [a[:, oc + i * 128:oc + (i + 1) * 128].bitcast(BF16) for i in range(2)]; oc += 256
        self.d2 = [a[:, oc + i * 128:oc + (i + 1) * 128] for i in range(2)]; oc += 256
        assert oc <= o_c + 2080
        o = 8256
        self.zb = [a[:, o + i * 512:o + (i + 1) * 512].bitcast(BF16) for i in range(3)]; o += 1536
        self.ybufP = [a[:, o + i * 4096:o + (i + 1) * 4096] for i in range(2)]; o += 8192
        self.bandf = a[:, o:o + 1536]; o += 1536
        self.bandb = a[:, o:o + 768].bitcast(BF16); o += 768
        assert o <= ARENA_F32, o
        self.out_dmas = []
        self.dbg_out = []

    def nb(self):
        ov = getattr(self, "_nb_override", None)
        if ov is not None:
            b = ov[0][ov[1] % len(ov[0])]
            ov[1] += 1
            return b
        b = self.bank
        self.bank = (self.bank + 1) % 8
        return b

    def rot(self, name):
        i = self.cnt[name]
        self.cnt[name] = (i + 1) % 2
        return i

    def hsl(self, c, c0, n):
        return self.hT[:, c * T + c0:c * T + c0 + n]

    def fence(self, reads, writes):
        self.S.add("pool", lambda e: e.memset(self.junk[0:1, 0:1], 0.0), reads=list(reads), writes=list(reads) + list(writes) + ["junk"])

    def wplan(self, items):
        self.witems = items
        self.w_dma = 0
        self.w_cast = 0
        self.w_use = 0

    def _pump(self, i):
        S = self.S
        n_items = len(self.witems)
        while True:
            prog = False
            if self.w_dma < n_items and self.w_dma - NS < self.w_cast:
                k = self.w_dma
                ap, n, dest, dkeys = self.witems[k]
                sl = k % NS
                S.add("sp", lambda e, sl=sl, n=n, ap=ap: e.dma_start(out=self.stage[sl][:, 0:n], in_=ap),
                      writes=[("stage", sl)], dma=("stage", sl))
                self.w_dma += 1
                prog = True
            if (self.w_cast < n_items and self.w_cast < self.w_dma and self.w_cast <= i + NB - 1
                    and (self.witems[self.w_cast][2] is None or self.w_cast <= i)):
                k = self.w_cast
                ap, n, dest, dkeys = self.witems[k]
                sl = k % NS
                if dest is None:
                    dst = self.wb[k % NB][:, 0:n]
                    wk = [("wb", k % NB)]
                else:
                    dst = dest
                    wk = dkeys
                for p0 in range(0, n, 512):
                    p1 = min(n, p0 + 512)
                    if p0 == 0:
                        S.add("pool", lambda e, dst=dst, sl=sl, p0=p0, p1=p1: e.tensor_copy(out=dst[:, p0:p1], in_=self.stage[sl][:, p0:p1]),
                              reads=[("stage", sl)], writes=wk)
                    else:
                        S.add("act", lambda e, dst=dst, sl=sl, p0=p0, p1=p1: e.activation(out=dst[:, p0:p1], in_=self.stage[sl][:, p0:p1], func=AF.Copy),
                              reads=[("stage", sl)], writes=wk)
                self.w_cast += 1
                prog = True
            if not prog:
                break

    def wnext(self):
        i = self.w_use
        self.w_use += 1
        self._pump(i)
        ap, n, dest, dkeys = self.witems[i]
        if dest is None:
            return self.wb[i % NB][:, 0:n], [("wb", i % NB)]
        return dest, dkeys

    def setup(self):
        S = self.S
        S.add("sp", lambda e: e.dma_start(out=self.vecs[:, :], in_=self.vecs_d), writes=["vecs"], dma="vecs")
        S.add("sp", lambda e: e.dma_start(out=self.hT[:, :].rearrange("p (c t) -> p c t", c=KC)[:, :, 0:NMETA],
                                           in_=self.metaT.rearrange("(c p) t -> p c t", p=128)),
              writes=[("h", c, 0) for c in range(KC)], dma="meta")
        for t in range(1, 5):
            c0, n = TT[t]
            S.add("sp", lambda e, c0=c0, n=n: e.dma_start(out=self.hT[:, :].rearrange("p (c t) -> p c t", c=KC)[:, :, c0:c0 + n],
                                                           in_=self.xT.rearrange("(c p) t -> p c t", p=128)[:, :, c0 - NMETA:c0 - NMETA + n]),
                  writes=[("h", c, t) for c in range(KC)], dma=("x", t))
        S.add("pool", lambda e: e.memset(self.ones[:, :], 1.0 / 1024), writes=["ones"])
        S.add("pool", lambda e: e.memset(self.ones1[:, :], 1.0), writes=["ones1"])
        for l in range(2):
            for i, coef in ((1, 0.5), (3, 1.0), (5, 0.5)):
                o = (l * 6 + i) * 8
                S.add("dve", lambda e, o=o, coef=coef: e.tensor_scalar(out=self.gc[:, o:o + 8], in0=self.vecs[:, o:o + 8],
                                                                       scalar1=coef, scalar2=None, op0=ALU.mult),
                      reads=["vecs"], writes=[("gc", l, i)])

    def rms_stats(self, src_fn, src_keys_fn, t, nchunks=8):
        S = self.S
        c0, n = TT[t]
        b = self.nb()
        for c in range(nchunks):
            si = self.rot("sq")
            S.add("act", lambda e, c=c, si=si: e.activation(out=self.sq[si][:, 0:n], in_=src_fn(c), func=AF.Square),
                  reads=src_keys_fn(c), writes=[("sq", si)])
            S.add("pe", lambda e, c=c, si=si, b=b: e.matmul(self.ps[b][:, 0:n], lhsT=self.ones[:, :], rhs=self.sq[si][:, 0:n],
                                                            start=(c == 0), stop=(c == nchunks - 1)),
                  reads=[("sq", si), "ones"], writes=[("ps", b)])
        ri = self.rot("rstd")
        r = self.rstd[ri]
        S.add("dve", lambda e, b=b: e.tensor_scalar(out=r[:, 0:n], in0=self.ps[b][:, 0:n], scalar1=EPS, scalar2=None, op0=ALU.add),
              reads=[("ps", b)], writes=[("rstd", ri)])
        S.add("dve", lambda e: e.reciprocal(out=r[:, 0:n], in_=r[:, 0:n]), reads=[("rstd", ri)], writes=[("rstd", ri)])
        S.add("act", lambda e: e.activation(out=r[:, 0:n], in_=r[:, 0:n], func=AF.Sqrt), reads=[("rstd", ri)], writes=[("rstd", ri)])
        return ri

    def prenorm_tile(self, t, gidx, dst_fn, dst_key_fn):
        S = self.S
        c0, n = TT[t]
        ri = self.rms_stats(lambda c: self.hsl(c, c0, n), lambda c: [("h", c, t)], t)
        for c in range(KC):
            S.add("dve", lambda e, c=c: e.scalar_tensor_tensor(out=dst_fn(c), in0=self.hsl(c, c0, n),
                                                               scalar=self.vecs[:, gidx * 8 + c:gidx * 8 + c + 1],
                                                               in1=self.rstd[ri][:, 0:n], op0=ALU.mult, op1=ALU.mult),
                  reads=[("h", c, t), ("rstd", ri), "vecs"], writes=[dst_key_fn(c)])

    def prenorm_parts(self, t, gidx, dst_fn, dst_key_fn, nsplit=4):
        S = self.S
        c0, n = TT[t]
        st = {}

        def stats():
            st["ri"] = self.rms_stats(lambda c: self.hsl(c, c0, n), lambda c: [("h", c, t)], t)

        def apply(cs):
            ri = st["ri"]
            for c in cs:
                S.add("dve", lambda e, c=c: e.scalar_tensor_tensor(out=dst_fn(c), in0=self.hsl(c, c0, n),
                                                                   scalar=self.vecs[:, gidx * 8 + c:gidx * 8 + c + 1],
                                                                   in1=self.rstd[ri][:, 0:n], op0=ALU.mult, op1=ALU.mult),
                      reads=[("h", c, t), ("rstd", ri), "vecs"], writes=[dst_key_fn(c)])

        per = KC // nsplit
        return [stats] + [(lambda cs=list(range(k * per, (k + 1) * per)): apply(cs)) for k in range(nsplit)]

    def post_parts(self, t, ysrc_fn, ykey_fn, l, i, final=False, nsplit=2, add_eng="dve", pool_ms=()):
        S = self.S
        c0, n = TT[t]
        go = (l * 6 + i) * 8
        st = {}

        def stats():
            st["ri"] = self.rms_stats(ysrc_fn, lambda m: [ykey_fn(m)], t)

        def apply(ms, last):
            ri = st["ri"]
            for m in ms:
                if m in pool_ms:
                    S.add("dve", lambda e, m=m: e.scalar_tensor_tensor(out=ysrc_fn(m), in0=ysrc_fn(m),
                                                                        scalar=self.gc[:, go + m:go + m + 1],
                                                                        in1=self.rstd[ri][:, 0:n], op0=ALU.mult, op1=ALU.mult),
                          reads=[ykey_fn(m), ("rstd", ri), ("gc", l, i)], writes=[ykey_fn(m)])
                    S.add("pool", lambda e, m=m: e.tensor_tensor(out=self.hsl(m, c0, n), in0=self.hsl(m, c0, n),
                                                                 in1=ysrc_fn(m), op=ALU.add),
                          reads=[ykey_fn(m), ("h", m, t)], writes=[("h", m, t)])
                    continue
                ti = self.rot("tmp")
                S.add("dve", lambda e, m=m, ti=ti: e.scalar_tensor_tensor(out=self.tmp[ti][:, 0:n], in0=ysrc_fn(m),
                                                                          scalar=self.gc[:, go + m:go + m + 1],
                                                                          in1=self.rstd[ri][:, 0:n], op0=ALU.mult, op1=ALU.mult),
                      reads=[ykey_fn(m), ("rstd", ri), ("gc", l, i)], writes=[("tmp", ti)])
                S.add(add_eng, lambda e, m=m, ti=ti: e.tensor_tensor(out=self.hsl(m, c0, n), in0=self.hsl(m, c0, n),
                                                                     in1=self.tmp[ti][:, 0:n], op=ALU.add),
                      reads=[("tmp", ti), ("h", m, t)], writes=[("h", m, t)])
            if last and final and t > 0:
                oc0 = c0 - NMETA
                op = S.add("sp", lambda e: e.dma_start(out=self.outT.rearrange("(c p) t -> p c t", p=128)[:, :, oc0:oc0 + n],
                                                       in_=self.hT[:, :].rearrange("p (c t) -> p c t", c=KC)[:, :, c0:c0 + n]),
                           reads=[("h", m, t) for m in range(KC)], dma=("out", t))
                self.out_dmas.append(op)

        per = KC // nsplit
        parts = [stats]
        for k in range(nsplit):
            ms = list(range(k * per, (k + 1) * per))
            parts.append(lambda ms=ms, last=(k == nsplit - 1): apply(ms, last))
        return parts

    def post_tile(self, t, ysrc_fn, ykey_fn, l, i, final=False, add_eng="dve", pool_ms=()):
        for f in self.post_parts(t, ysrc_fn, ykey_fn, l, i, final=final, nsplit=1, add_eng=add_eng, pool_ms=pool_ms):
            f()

    def ffn(self, l, s, final=False, next_ffn=None):
        S = self.S
        f = l * 2 + s
        gpre = l * 6 + (0 if s == 0 else 4)
        ipost = 1 if s == 0 else 5

        def xn_ap(p, c, t):
            c0, n = TT[t]
            off = c * PW + (c0 - PLO[p])
            return self.xn[:, off:off + n]

        def act_ap(p, j, t):
            c0, n = TT[t]
            off = j * PW + (c0 - PLO[p])
            return self.act[:, off:off + n]

        def y_ap(p, m, t):
            c0, n = TT[t]
            off = m * PW + (c0 - PLO[p])
            return self.ybuf[:, off:off + n]

        PASSES_ = [[t for t in ps_ if not (final and t == 0)] for ps_ in PASSES]

        def prenorm(p):
            for t in PASSES_[p]:
                self.prenorm_tile(t, gpre, lambda c, t=t: xn_ap(p, c, t), lambda c, t=t: ("xn", c, t))

        def w1phase(p, hooks=None):
            for j in range(FC):
                w, wk = self.wnext()
                for t in PASSES_[p]:
                    c0, n = TT[t]
                    ba = self.nb()
                    bb = self.nb()
                    for half, bk in ((0, ba), (1, bb)):
                        for kk in range(KC):
                            S.add("pe", lambda e, kk=kk, bk=bk, half=half, t=t, n=n, w=w: e.matmul(
                                self.ps[bk][:, 0:n], lhsT=w[:, (half * 8 + kk) * 128:(half * 8 + kk + 1) * 128],
                                rhs=xn_ap(p, kk, t), start=(kk == 0), stop=(kk == KC - 1)),
                                reads=wk + [("xn", kk, t)], writes=[("ps", bk)])
                    ti = self.rot("tmp")
                    S.add("act", lambda e, ba=ba, ti=ti, n=n: e.activation(out=self.tmp[ti][:, 0:n], in_=self.ps[ba][:, 0:n], func=AF.Silu),
                          reads=[("ps", ba)], writes=[("tmp", ti)])
                    S.add("dve", lambda e, bb=bb, ti=ti, n=n, j=j, t=t: e.tensor_tensor(out=act_ap(p, j, t), in0=self.tmp[ti][:, 0:n],
                                                                                         in1=self.ps[bb][:, 0:n], op=ALU.mult),
                          reads=[("tmp", ti), ("ps", bb)], writes=[("act", j, t)])
                if hooks and j in hooks:
                    hooks[j]()

        def w2phase(p):
            for m in range(KC):
                banks = {t: self.nb() for t in PASSES_[p]}
                for half in range(2):
                    w, wk = self.wnext()
                    for t in PASSES_[p]:
                        c0, n = TT[t]
                        for k2 in range(11):
                            kk = half * 11 + k2
                            S.add("pe", lambda e, kk=kk, k2=k2, t=t, n=n, w=w, bk=banks[t]: e.matmul(
                                self.ps[bk][:, 0:n], lhsT=w[:, k2 * 128:(k2 + 1) * 128], rhs=act_ap(p, kk, t),
                                start=(kk == 0), stop=(kk == FC - 1)),
                                reads=wk + [("act", kk, t)], writes=[("ps", banks[t])])
                for t in PASSES_[p]:
                    c0, n = TT[t]
                    S.add("act", lambda e, m=m, t=t, n=n, bk=banks[t]: e.activation(out=y_ap(p, m, t), in_=self.ps[bk][:, 0:n], func=AF.Copy),
                          reads=[("ps", banks[t])], writes=[("y", m, t)])

        def post_t(p, t):
            self.post_tile(t, lambda m, t=t: y_ap(p, m, t), lambda m, t=t: ("y", m, t), l, ipost, final=final)

        def post_hooks(p, tiles, j0):
            hk = {}
            j = j0
            for t in tiles:
                if TT[t][1] <= 16:
                    hk[j] = (lambda t=t: post_t(p, t))
                    j += 1
                    continue
                parts = self.post_parts(t, lambda m, t=t: y_ap(p, m, t), lambda m, t=t: ("y", m, t), l, ipost, final=final, nsplit=4)
                for f in parts:
                    hk[j] = f
                    j += 1
            return hk

        pend = getattr(self, "_pending_post", None)
        self._pending_post = None
        if not getattr(self, "_prenorm0_done", False):
            prenorm(0)
        self._prenorm0_done = False
        pre1 = [self.prenorm_parts(t, gpre, lambda c, t=t: xn_ap(1, c, t), lambda c, t=t: ("xn", c, t), nsplit=1)
                for t in PASSES_[1]]
        hooksA = dict(pend or {})
        assert (FC - 4) not in hooksA and (FC - 2) not in hooksA
        hooksA[FC - 4] = pre1[0][0]
        hooksA[FC - 2] = pre1[1][0]
        w1phase(0, hooks=hooksA)
        for p_ in pre1:
            p_[1]()
        w2phase(0)
        hooksB = post_hooks(0, PASSES_[0], 1)
        npre = None
        if next_ffn is not None:
            nl, ns_ = next_ffn
            ngpre = nl * 6 + (0 if ns_ == 0 else 4)
            npre = [self.prenorm_parts(t, ngpre, lambda c, t=t: xn_ap(0, c, t), lambda c, t=t: ("xn", c, t), nsplit=1)
                    for t in (1, 2)]
            assert (FC - 5) not in hooksB and (FC - 3) not in hooksB and max(hooksB) < FC - 5
            hooksB[FC - 5] = npre[0][0]
            hooksB[FC - 3] = npre[1][0]
        w1phase(1, hooks=hooksB)
        if next_ffn is not None:
            for p_ in npre:
                p_[1]()
            self.prenorm_tile(0, ngpre, lambda c: xn_ap(0, c, 0), lambda c: ("xn", c, 0))
            self._prenorm0_done = True
        w2phase(1)
        if next_ffn is not None:
            self._pending_post = post_hooks(1, [3, 4], 1)
        else:
            post_t(1, 3)
            post_t(1, 4)

    def _ffn_keys(self):
        return ([("xn", c, t) for c in range(KC) for t in range(5)] + [("y", m, t) for m in range(KC) for t in range(5)]
                + [("act", j, t) for j in range(FC) for t in range(5)])

    def _m0_keys(self):
        return (["btab", "mtab", "bmeta", "EBT", "EBm1", "EBm0", "V", "ucvpad", "woutb"]
                + [("uT", c, t) for c in range(KC) for t in range(5)]
                + [("q", c, t) for c in range(4) for t in range(5)] + [("kz", c, t) for c in range(4) for t in range(5)]
                + [("ucv", t) for t in range(5)] + [("ymC", c, t) for c in range(4) for t in range(5)]
                + [("ymA", c, t) for c in range(4) for t in range(5)] + [("ybM", i, m) for i in range(2) for m in range(KC)]
                + [("Pt", i) for i in range(2)] + [("Pm", i) for i in range(2)] + [("d2", i) for i in range(2)])

    def _m1_keys(self):
        return (["bandf", "bandb"] + [("uT", c, t) for c in range(KC) for t in range(5)] + [("z", i) for i in range(3)]
                + [("ybP", i, m) for i in range(2) for m in range(KC)])

    def half_slab(self):
        if getattr(self, "_hs", None) is None:
            w, wk = self.wnext()
            self._hs = (w, wk)
            return w[:, 0:1024], wk
        w, wk = self._hs
        self._hs = None
        return w[:, 1024:2048], wk

    def proj_tile(self, w, wk, src_fn, src_key_fn, t):
        S = self.S
        c0, n = TT[t]
        b = self.nb()
        for kk in range(KC):
            S.add("pe", lambda e, kk=kk, b=b, n=n, t=t: e.matmul(self.ps[b][:, 0:n], lhsT=w[:, kk * 128:(kk + 1) * 128],
                                                                  rhs=src_fn(kk, t), start=(kk == 0), stop=(kk == KC - 1)),
                  reads=wk + [src_key_fn(kk, t)], writes=[("ps", b)])
        return b

    def mixer0(self):
        S = self.S
        V = self.vecs
        self.fence(self._ffn_keys(), self._m0_keys())
        S.add("pool", lambda e: e.memset(self.ucv[:, 0:16], 0.0), writes=["ucvpad"])
        gidx = 2

        def uT_ap(c, t):
            c0, n = TT[t]
            return self.uT[:, c * T + c0:c * T + c0 + n]

        for t in range(5):
            self.prenorm_tile(t, gidx, lambda c, t=t: uT_ap(c, t), lambda c, t=t: ("uT", c, t))
        ukey = lambda kk, t: ("uT", kk, t)
        S.add("sp", lambda e: e.dma_start(out=self.btab, in_=self.btab_d), writes=["btab"], dma="btab")
        S.add("sp", lambda e: e.dma_start(out=self.mtab, in_=self.mtab_d), writes=["mtab"], dma="mtab")
        S.add("sp", lambda e: e.dma_start(out=self.bmeta[0:16, :], in_=self.bmeta_d), writes=["bmeta"], dma="bmeta")
        S.add("act", lambda e: e.activation(out=self.esink[:, :], in_=V[:, 116:120], func=AF.Exp), reads=["vecs"], writes=["esink"])
        S.add("act", lambda e: e.activation(out=self.eb31[:, :], in_=V[:, 120:128], func=AF.Exp), reads=["vecs"], writes=["eb31"])
        for h_ in range(8):
            bt_h = self.btab[:, h_ * 256:(h_ + 1) * 256]
            S.add("act", lambda e, bt_h=bt_h: e.activation(out=bt_h, in_=bt_h, func=AF.Exp), reads=["btab"], writes=["btab"])
            S.add("dve", lambda e, bt_h=bt_h, h_=h_: e.tensor_tensor(out=self.EBT[:, h_ * 256:(h_ + 1) * 256], in0=bt_h, in1=self.mtab, op=ALU.mult),
                  reads=["btab", "mtab"], writes=["EBT"])
            bm_h = self.bmeta[0:16, h_ * 160:(h_ + 1) * 160]
            S.add("act", lambda e, bm_h=bm_h: e.activation(out=bm_h[:, 0:144], in_=bm_h[:, 0:144], func=AF.Exp), reads=["bmeta"], writes=["bmeta"])
            S.add("dve", lambda e, bm_h=bm_h, h_=h_: e.tensor_copy(out=self.EBm1[0:16, h_ * 128:(h_ + 1) * 128], in_=bm_h[:, 0:128]),
                  reads=["bmeta"], writes=["EBm1"])
            S.add("dve", lambda e, bm_h=bm_h, h_=h_: e.tensor_tensor(out=self.EBm0[0:16, h_ * 16:(h_ + 1) * 16], in0=bm_h[:, 128:144],
                                                                       in1=bm_h[:, 144:160], op=ALU.mult),
                  reads=["bmeta"], writes=["EBm0"])
        for ci in range(4):
            w, wk = self.half_slab()
            for t in range(5):
                c0, n = TT[t]
                b = self.proj_tile(w, wk, uT_ap, ukey, t)
                S.add("act", lambda e, b=b, c0=c0, n=n: e.activation(out=self.ucv[:, 16 + c0:16 + c0 + n], in_=self.ps[b][:, 0:n], func=AF.Copy),
                      reads=[("ps", b)], writes=[("ucv", t)])
            w, wk = self.half_slab()
            for t in range(5):
                c0, n = TT[t]
                b = self.proj_tile(w, wk, uT_ap, ukey, t)
                S.add("dve", lambda e, b=b, c0=c0, n=n: e.tensor_tensor(out=self.ucv[:, 16 + c0:16 + c0 + n], in0=self.ucv[:, 16 + c0:16 + c0 + n],
                                                                         in1=self.ps[b][:, 0:n], op=ALU.mult),
                      reads=[("ps", b), ("ucv", t)], writes=[("ucv", t)])
            w, wk = self.half_slab()
            for t in range(5):
                c0, n = TT[t]
                b = self.proj_tile(w, wk, uT_ap, ukey, t)
                ti = self.rot("tmp")
                tm = self.tmp[ti][:, 0:n]
                prevk = [("ucv", t - 1)] if t > 0 else ["ucvpad"]
                cwo = 104 + ci * 3
                S.add("act", lambda e, tm=tm, c0=c0, n=n, cwo=cwo: e.activation(out=tm, in_=self.ucv[:, 16 + c0:16 + c0 + n], func=AF.Copy,
                                                                                scale=V[:, cwo:cwo + 1]),
                      reads=[("ucv", t), "vecs"], writes=[("tmp", ti)])
                for j in (1, 2):
                    S.add("dve", lambda e, tm=tm, c0=c0, n=n, cwo=cwo, j=j: e.scalar_tensor_tensor(
                        out=tm, in0=self.ucv[:, 16 + c0 - j:16 + c0 - j + n], scalar=V[:, cwo + j:cwo + j + 1], in1=tm,
                        op0=ALU.mult, op1=ALU.add),
                        reads=[("ucv", t), ("tmp", ti), "vecs"] + prevk, writes=[("tmp", ti)])
                S.add("dve", lambda e, tm=tm, b=b, c0=c0, n=n, ci=ci: e.tensor_tensor(out=self.ymC[:, ci * T + c0:ci * T + c0 + n], in0=tm,
                                                                                       in1=self.ps[b][:, 0:n], op=ALU.mult),
                      reads=[("tmp", ti), ("ps", b)], writes=[("ymC", ci, t)])
        self.fence(["btab", "mtab", "bmeta"], [("q", c, t) for c in range(4) for t in range(5)])
        self.fence([("ucv", t) for t in range(5)] + ["ucvpad"],
                   [("Pt", i) for i in range(2)] + [("Pm", i) for i in range(2)] + [("d2", i) for i in range(2)])
        ev = 0
        for dst, name in ((self.qT, "q"), (self.kz, "kz")):
            for c in range(4):
                w, wk = self.half_slab()
                for t in range(5):
                    c0, n = TT[t]
                    b = self.proj_tile(w, wk, uT_ap, ukey, t)
                    o = c * T + c0
                    if ev % 2 == 0:
                        S.add("act", lambda e, b=b, o=o, n=n, dst=dst: e.activation(out=dst[:, o:o + n], in_=self.ps[b][:, 0:n], func=AF.Copy),
                              reads=[("ps", b)], writes=[(name, c, t)])
                    else:
                        S.add("dve", lambda e, b=b, o=o, n=n, dst=dst: e.tensor_copy(out=dst[:, o:o + n], in_=self.ps[b][:, 0:n]),
                              reads=[("ps", b)], writes=[(name, c, t)])
                    ev += 1
        w, wk = self.half_slab()
        _pad, _ = self.half_slab()

        def blk(tb):
            return (0, 16, 0) if tb == 0 else (16 + 128 * (tb - 1), 128, 1 + (tb - 1) // 4)

        for tb in range(17):
            c0, n, t = blk(tb)
            b = self.nb()
            for kk in range(KC):
                S.add("pe", lambda e, kk=kk, b=b, c0=c0, n=n, w=w: e.matmul(self.ps[b][0:n, 0:128], lhsT=self.uT[:, kk * T + c0:kk * T + c0 + n],
                                                                        rhs=w[:, kk * 128:(kk + 1) * 128], start=(kk == 0), stop=(kk == KC - 1)),
                      reads=wk + [("uT", kk, t)], writes=[("ps", b)])
            S.add("act", lambda e, b=b, n=n, tb=tb: e.activation(out=self.Vt[0:n, tb * 128:(tb + 1) * 128], in_=self.ps[b][0:n, 0:128], func=AF.Copy),
                  reads=[("ps", b)], writes=["V"])
        self.fence([("uT", c, t) for c in range(KC) for t in range(5)],
                   [("ymA", c, t) for c in range(4) for t in range(5)] + ["woutb"])
        def attn_scores(bq, c):
            c0, n, tq = blk(bq)
            kv = c // 2
            qrhs = self.qT[:, c * T + c0:c * T + c0 + n]
            qkey = ("q", c, tq)
            nk_main = 0 if bq == 0 else (1 if bq == 1 else 2)
            pi = self.rot("Pt")
            pmi = self.rot("Pm")
            Pt = self.Pt[pi]
            Pm = self.Pm[pmi]
            if nk_main:
                bs = self.nb()
                for hh in range(2):
                    kzc = kv * 2 + hh
                    for k in range(nk_main):
                        kb = bq - k
                        kc0, kn, kt = blk(kb)
                        S.add("pe", lambda e, bs=bs, hh=hh, k=k, kzc=kzc, kc0=kc0, qrhs=qrhs: e.matmul(
                            self.ps[bs][:, (hh * 2 + k) * 128:(hh * 2 + k + 1) * 128],
                            lhsT=self.kz[:, kzc * T + kc0:kzc * T + kc0 + 128], rhs=qrhs, start=True, stop=True),
                            reads=[("kz", kzc, kt), qkey], writes=[("ps", bs)])
                ti = self.rot("tmp")
                if nk_main == 2:
                    src = self.ps[bs][:, 0:512]
                    tm = self.tmp[ti][:, 0:512]
                    pt = Pt[:, 0:512]
                    eb = self.EBT[:, 2 * c * 256:(2 * c + 2) * 256]
                else:
                    v3 = lambda ap: ap.rearrange("p (h x) -> p h x", h=2)[:, :, 0:128]
                    src = v3(self.ps[bs][:, 0:512])
                    tm = v3(self.tmp[ti][:, 0:512])
                    pt = v3(Pt[:, 0:512])
                    eb = v3(self.EBT[:, 2 * c * 256:(2 * c + 2) * 256])
                S.add("act", lambda e, src=src, tm=tm: e.activation(out=tm, in_=src, func=AF.Exp, scale=0.125),
                      reads=[("ps", bs)], writes=[("tmp", ti)])
                S.add("dve", lambda e, tm=tm, pt=pt, eb=eb: e.tensor_tensor(out=pt, in0=tm, in1=eb, op=ALU.mult),
                      reads=[("tmp", ti), "EBT"], writes=[("Pt", pi)])
            bm = self.nb()
            for hh in range(2):
                kzc = kv * 2 + hh
                S.add("pe", lambda e, bm=bm, hh=hh, kzc=kzc, qrhs=qrhs, n=n: e.matmul(
                    self.ps[bm][0:16, hh * n:(hh + 1) * n], lhsT=self.kz[:, kzc * T:kzc * T + 16], rhs=qrhs, start=True, stop=True),
                    reads=[("kz", kzc, 0), qkey], writes=[("ps", bm)])
            ti2 = self.rot("tmp")
            tm2 = self.tmp[ti2][0:16, 0:2 * n]
            S.add("act", lambda e, bm=bm, tm2=tm2, n=n: e.activation(out=tm2, in_=self.ps[bm][0:16, 0:2 * n], func=AF.Exp, scale=0.125),
                  reads=[("ps", bm)], writes=[("tmp", ti2)])
            if bq == 0:
                S.add("dve", lambda e, tm2=tm2, Pm=Pm, c=c: e.tensor_tensor(out=Pm[0:16, 0:32], in0=tm2, in1=self.EBm0[0:16, 2 * c * 16:(2 * c + 2) * 16], op=ALU.mult),
                      reads=[("tmp", ti2), "EBm0"], writes=[("Pm", pmi)])
            elif bq == 1:
                S.add("dve", lambda e, tm2=tm2, Pm=Pm, c=c: e.tensor_tensor(out=Pm[0:16, 0:256], in0=tm2, in1=self.EBm1[0:16, 2 * c * 128:(2 * c + 2) * 128], op=ALU.mult),
                      reads=[("tmp", ti2), "EBm1"], writes=[("Pm", pmi)])
            else:
                e16 = self.eb31[0:16, 2 * c:2 * c + 2]
                eb_bc = bass.AP(e16.tensor, e16.offset, [list(e16.ap[0]), list(e16.ap[1]), [0, 128]])
                S.add("dve", lambda e, Pm=Pm, ti2=ti2, eb_bc=eb_bc: e.tensor_tensor(
                    out=Pm[0:16, 0:256].rearrange("p (h x) -> p h x", h=2),
                    in0=self.tmp[ti2][0:16, 0:256].rearrange("p (h x) -> p h x", h=2), in1=eb_bc, op=ALU.mult),
                    reads=[("tmp", ti2), "eb31"], writes=[("Pm", pmi)])
            return (bq, c, nk_main, pi, pmi)

        def attn_pv(state):
            bq, c, nk_main, pi, pmi = state
            c0, n, tq = blk(bq)
            kv = c // 2
            Pt = self.Pt[pi]
            Pm = self.Pm[pmi]
            bo = self.nb()
            for hh in range(2):
                rows = slice(hh * 64, hh * 64 + 64)
                chunks = []
                for k in range(nk_main):
                    kb = bq - k
                    chunks.append((self.Vt[:, kb * 128 + kv * 64:kb * 128 + kv * 64 + 64], self.ones1[:, 0:64],
                                   Pt[:, (hh * 2 + k) * 128:(hh * 2 + k) * 128 + n], ("Pt", pi)))
                chunks.append((self.Vt[0:16, kv * 64:kv * 64 + 64], self.ones1[0:16, 0:64], Pm[0:16, hh * n:(hh + 1) * n], ("Pm", pmi)))
                for which in range(2):
                    for ci_, (vl, ol, pr, pk) in enumerate(chunks):
                        lhs = vl if which == 0 else ol
                        S.add("pe", lambda e, bo=bo, rows=rows, which=which, lhs=lhs, pr=pr, ci_=ci_, nch=len(chunks), n=n: e.matmul(
                            self.ps[bo][rows, which * 128:which * 128 + n], lhsT=lhs, rhs=pr, start=(ci_ == 0), stop=(ci_ == nch - 1)),
                            reads=["V", "ones1", pk], writes=[("ps", bo)])
            di = self.rot("d2")
            d2 = self.d2[di][:, 0:n]
            S.add("dve", lambda e, bo=bo, d2=d2, c=c, n=n: e.tensor_scalar(out=d2, in0=self.ps[bo][:, 128:128 + n], scalar1=self.esink[:, c:c + 1],
                                                                           scalar2=None, op0=ALU.add),
                  reads=[("ps", bo), "esink"], writes=[("d2", di)])
            S.add("dve", lambda e, d2=d2: e.reciprocal(out=d2, in_=d2), reads=[("d2", di)], writes=[("d2", di)])
            S.add("dve", lambda e, bo=bo, d2=d2, c=c, c0=c0, n=n: e.tensor_tensor(out=self.ymA[:, c * T + c0:c * T + c0 + n], in0=self.ps[bo][:, 0:n],
                                                                                  in1=d2, op=ALU.mult),
                  reads=[("ps", bo), ("d2", di)], writes=[("ymA", c, tq)])

        iters = [(bq, c) for bq in range(17) for c in range(4)]
        pend = None
        for it, (bq, c) in enumerate(iters):
            if it in (6, 18, 30, 42):
                self.wnext()
            st = attn_scores(bq, c)
            if pend is not None:
                attn_pv(pend)
            pend = st
        attn_pv(pend)
        if self.debug:
            dbg = self.nc.dram_tensor("dbg", [128, 8 * T], BF16, kind="ExternalOutput").ap()
            op = S.add("sp", lambda e: e.dma_start(out=dbg[:, 0:4 * T], in_=self.ymA[:, 0:4 * T]),
                       reads=[("ymA", c, t) for c in range(4) for t in range(5)], dma="dbgA")
            self.out_dmas.append(op)
            op = S.add("sp", lambda e: e.dma_start(out=dbg[:, 4 * T:8 * T], in_=self.ymC[:, 0:4 * T]),
                       reads=[("ymC", c, t) for c in range(4) for t in range(5)], dma="dbgC")
            self.out_dmas.append(op)
            dbg2 = self.nc.dram_tensor("dbg2", [128, 2176 + 2048 + 1024 + 512], BF16, kind="ExternalOutput").ap()
            for (o_, n_, src, key) in ((0, 2176, self.Vt, ["V"]), (2176, 2048, self.EBT, ["EBT"]),
                                        (4224, 512, self.Pt[0], [("Pt", 0)]), (4736, 512, self.Pt[1], [("Pt", 1)]),
                                        (5248, 256, self.Pm[0], [("Pm", 0)]), (5504, 256, self.Pm[1], [("Pm", 1)])):
                op = S.add("sp", lambda e, o_=o_, n_=n_, src=src: e.dma_start(out=dbg2[:, o_:o_ + n_], in_=src[:, 0:n_]), reads=key, dma=("dbg2", o_))
                self.out_dmas.append(op)
            dbg3 = self.nc.dram_tensor("dbg3", [128, 256 + 12], F32, kind="ExternalOutput").ap()
            for (o_, n_, src, key) in ((0, 128, self.d2[0], [("d2", 0)]), (128, 128, self.d2[1], [("d2", 1)]),
                                        (256, 4, self.esink, ["esink"]), (260, 8, self.eb31, ["eb31"])):
                op = S.add("sp", lambda e, o_=o_, n_=n_, src=src: e.dma_start(out=dbg3[:, o_:o_ + n_], in_=src[:, 0:n_]), reads=key, dma=("dbg3", o_))
                self.out_dmas.append(op)
        def ymix_ap(kk, t):
            c0, n = TT[t]
            if kk < 4:
                return self.ymA[:, kk * T + c0:kk * T + c0 + n]
            return self.ymC[:, (kk - 4) * T + c0:(kk - 4) * T + c0 + n]

        ymkey = lambda kk, t: ("ymA", kk, t) if kk < 4 else ("ymC", kk - 4, t)
        self.fence([("q", c, t) for c in range(4) for t in range(5)] + [("kz", c, t) for c in range(4) for t in range(5)],
                   [("ybM", i, m) for i in range(2) for m in range(KC)])
        ybM = [self.ybufM, self.ybufM2]

        def wout_post(t):
            n = TT[t][1]
            yb = ybM[t % 2]
            self.post_tile(t, lambda m, n=n, yb=yb: yb[:, m * 512:m * 512 + n], lambda m, t=t: ("ybM", t % 2, m), 0, 3, pool_ms=(1, 3, 5, 7))

        for t in range(5):
            c0, n = TT[t]
            yb = ybM[t % 2]
            for m in range(KC):
                b = self.proj_tile(self.woutb[:, m * 1024:(m + 1) * 1024], ["woutb"], ymix_ap, ymkey, t)
                S.add("act", lambda e, b=b, m=m, n=n, yb=yb: e.activation(out=yb[:, m * 512:m * 512 + n], in_=self.ps[b][:, 0:n], func=AF.Copy),
                      reads=[("ps", b)], writes=[("ybM", t % 2, m)])
                if m == 3 and t > 0:
                    wout_post(t - 1)
        wout_post(4)
        self.fence(self._m0_keys(), self._ffn_keys())

    def mixer1(self):
        S = self.S
        V = self.vecs
        self.fence(self._ffn_keys(), self._m1_keys())
        S.add("sp", lambda e: e.dma_start(out=self.bandf, in_=self.bands_d), writes=["bandf"], dma="bands")
        for i in range(3):
            S.add("dve", lambda e, i=i: e.tensor_copy(out=self.bandb[:, i * 512:(i + 1) * 512], in_=self.bandf[:, i * 512:(i + 1) * 512]),
                  reads=["bandf"], writes=["bandb"])
        w, wk = self.wnext()
        gidx = 6 + 2

        def uT_ap(c, t):
            c0, n = TT[t]
            return self.uT[:, c * T + c0:c * T + c0 + n]

        def blk(tb):
            return (0, 16, 0) if tb == 0 else (16 + 128 * (tb - 1), 128, 1 + (tb - 1) // 4)

        self._ev = 0

        def zproj(tb):
            c0, n, t = blk(tb)
            zi = tb % 3
            bz = [0, 1]
            for g in range(4):
                for kk in range(2):
                    ch = g * 2 + kk
                    S.add("pe", lambda e, g=g, kk=kk, ch=ch, c0=c0, n=n, b=bz[g // 2]: e.matmul(
                        self.ps[b][0:n, (g % 2) * 256:(g % 2) * 256 + 256], lhsT=self.uT[:, ch * T + c0:ch * T + c0 + n],
                        rhs=w[:, ch * 256:ch * 256 + 256], start=(kk == 0), stop=(kk == 1)),
                        reads=wk + [("uT", ch, t)], writes=[("ps", bz[g // 2])])
            for hb in range(2):
                if self._ev % 2 == 0:
                    S.add("act", lambda e, hb=hb, n=n, zi=zi, b=bz[hb]: e.activation(out=self.zb[zi][0:n, hb * 512:(hb + 1) * 512], in_=self.ps[b][0:n, 0:512], func=AF.Copy),
                          reads=[("ps", bz[hb])], writes=[("z", zi)])
                else:
                    S.add("dve", lambda e, hb=hb, n=n, zi=zi, b=bz[hb]: e.tensor_copy(out=self.zb[zi][0:n, hb * 512:(hb + 1) * 512], in_=self.ps[b][0:n, 0:512]),
                          reads=[("ps", bz[hb])], writes=[("z", zi)])
                self._ev += 1

        def tokmix(tb):
            c0, n, t = blk(tb)
            zi = tb % 3
            zprev = (tb - 1) % 3
            yi = t % 2
            by = [2, 3] if tb % 2 == 0 else [4, 5]
            for m in range(KC):
                g = m // 2
                bo = g * 384
                b = by[m // 4]
                oc0 = (m % 4) * 128
                S.add("pe", lambda e, m=m, n=n, zi=zi, bo=bo, b=b, oc0=oc0, last=(tb == 0): e.matmul(
                    self.ps[b][:, oc0:oc0 + n], lhsT=self.zb[zi][0:n, m * 128:(m + 1) * 128], rhs=self.bandb[0:n, bo:bo + n],
                    start=True, stop=last),
                    reads=[("z", zi), "bandb"], writes=[("ps", b)])
                if tb == 1:
                    S.add("pe", lambda e, m=m, zprev=zprev, bo=bo, b=b, oc0=oc0: e.matmul(
                        self.ps[b][:, oc0:oc0 + 128], lhsT=self.zb[zprev][0:16, m * 128:(m + 1) * 128], rhs=self.bandb[0:16, bo + 256:bo + 384],
                        start=False, stop=True),
                        reads=[("z", zprev), "bandb"], writes=[("ps", b)])
                elif tb >= 2:
                    S.add("pe", lambda e, m=m, zprev=zprev, bo=bo, b=b, oc0=oc0: e.matmul(
                        self.ps[b][:, oc0:oc0 + 128], lhsT=self.zb[zprev][:, m * 128:(m + 1) * 128], rhs=self.bandb[:, bo + 128:bo + 256],
                        start=False, stop=True),
                        reads=[("z", zprev), "bandb"], writes=[("ps", b)])
            lc0 = 0 if tb == 0 else ((tb - 1) % 4) * 128
            for m in range(KC):
                b = by[m // 4]
                oc0 = (m % 4) * 128
                S.add("act", lambda e, m=m, n=n, b=b, oc0=oc0, lc0=lc0, yi=yi: e.activation(
                    out=self.ybufP[yi][:, m * 512 + lc0:m * 512 + lc0 + n], in_=self.ps[b][:, oc0:oc0 + n], func=AF.Copy,
                    scale=V[:, 96 + m:96 + m + 1]),
                    reads=[("ps", b), "vecs"], writes=[("ybP", yi, m)])

        def post(t):
            yi = t % 2
            tn = TT[t][1]
            self.post_tile(t, lambda m, yi=yi, tn=tn: self.ybufP[yi][:, m * 512:m * 512 + tn], lambda m, yi=yi: ("ybP", yi, m), 1, 3)

        def post_parts1(t, nsplit):
            yi = t % 2
            tn = TT[t][1]
            return self.post_parts(t, lambda m, yi=yi, tn=tn: self.ybufP[yi][:, m * 512:m * 512 + tn],
                                   lambda m, yi=yi: ("ybP", yi, m), 1, 3, nsplit=nsplit, pool_ms=(1, 3, 5, 7))

        self.prenorm_tile(0, gidx, lambda c: uT_ap(c, 0), lambda c: ("uT", c, 0))
        self.prenorm_tile(1, gidx, lambda c: uT_ap(c, 1), lambda c: ("uT", c, 1))
        self._nb_override = [[6, 7], 0]
        sched = {}
        for t in range(1, 5):
            tb0 = 4 * (t - 1) + 1
            pre = None
            if t + 1 < 5:
                pre = self.prenorm_parts(t + 1, gidx, lambda c, t=t: uT_ap(c, t + 1), lambda c, t=t: ("uT", c, t + 1), nsplit=4)
            if t == 1:
                post = [lambda: [f() for f in post_parts1(0, 1)]]
            else:
                post = post_parts1(t - 1, 4)
            lst = [[] for _ in range(4)]
            if pre is not None:
                lst[0].append(pre[0])
            if len(post) > 1:
                lst[0].append(post[0])
            if pre is not None:
                lst[0].append(pre[1])
            if len(post) > 1:
                lst[0].append(post[1])
            else:
                lst[0].append(post[0])
            if pre is not None:
                lst[1] += [pre[2], pre[3]]
                lst[2].append(pre[4])
            if len(post) > 1:
                lst[1].append(post[2])
                lst[2].append(post[3])
                lst[3].append(post[4])
            for k in range(4):
                sched[tb0 + k] = lst[k]
        zproj(0)
        for tb in range(17):
            if tb + 1 < 17:
                zproj(tb + 1)
            tokmix(tb)
            for f in sched.get(tb, []):
                f()
        for f in post_parts1(4, 1):
            f()
        self._nb_override = None
        self.fence(self._m1_keys(), self._ffn_keys())

    def build(self, stages):
        items = []

        def ffn_items(f):
            for p in range(2):
                for j in range(FC):
                    items.append((self.w1s[f, j], 2048, None, None))
                for mh in range(16):
                    items.append((self.w2s[f, mh], 1408, None, None))

        for st in stages:
            if st.startswith("ffn"):
                ffn_items(int(st[3]) * 2 + int(st[4]))
            elif st == "mixer0":
                for i in range(11):
                    items.append((self.wins[i], 2048, None, None))
                for i in range(4):
                    items.append((self.wouts[i], 2048, self.woutb[:, i * 2048:(i + 1) * 2048], ["woutb"]))
            elif st == "mixer1":
                items.append((self.pws, 2048, None, None))
        self.wplan(items)
        self.setup()
        last = stages[-1]
        for si, st in enumerate(stages):
            if st.startswith("ffn"):
                nxt = stages[si + 1] if si + 1 < len(stages) else None
                next_ffn = (int(nxt[3]), int(nxt[4])) if (nxt is not None and nxt.startswith("ffn")) else None
                self.ffn(int(st[3]), int(st[4]), final=(st == last), next_ffn=next_ffn)
            elif st == "mixer0":
                self.mixer0()
            elif st == "mixer1":
                self.mixer1()
        S = self.S
        if not last.startswith("ffn"):
            for t in range(1, 5):
                c0, n = TT[t]
                oc0 = c0 - NMETA
                op = S.add("sp", lambda e, c0=c0, n=n, oc0=oc0: e.dma_start(
                    out=self.outT.rearrange("(c p) t -> p c t", p=128)[:, :, oc0:oc0 + n],
                    in_=self.hT[:, :].rearrange("p (c t) -> p c t", c=KC)[:, :, c0:c0 + n]),
                    reads=[("h", m, t) for m in range(KC)], dma=("out", t))
                self.out_dmas.append(op)
        S.finalize(force_signal=self.out_dmas)
        S.emit(self.nc, final_waits=[("sp", p) for p in self.out_dmas])


ALL_STAGES = ["ffn00", "mixer0", "ffn01", "ffn10", "mixer1", "ffn11"]


def build_nc(stages=None, same_engine_sync=True, debug=False):
    stages = stages or ALL_STAGES
    nc = bass.Bass("TRN2", target_bir_lowering=False)
    with contextlib.ExitStack() as es:
        b = Builder(nc, es, same_engine_sync=same_engine_sync, debug=debug)
        b.build(stages)
    return nc


def run(inputs, stages=None, trace=False, n_cores=8, debug=False):
    common = prep_common(inputs)
    x = np.asarray(inputs["x"], dtype=np.float32)
    in_maps = []
    for i in range(n_cores):
        m = dict(common)
        m["xT"] = np.ascontiguousarray(x[i].T)
        in_maps.append(m)
    nc = build_nc(stages, debug=debug)
    res = run_bass_kernel_spmd(nc, in_maps, core_ids=list(range(n_cores)), trace=trace)
    out = np.stack([np.ascontiguousarray(r["outT"].T) for r in res.results], axis=0)
    return out.astype(np.float32), res


def kernel(**inputs):
    out, _ = run(inputs)
    return out
```

```python
import contextlib
import numpy as np
import concourse.bass as bass
import concourse.mybir as mybir
from concourse.bass_utils import run_bass_kernel_spmd

F32 = mybir.dt.float32
BF16 = mybir.dt.bfloat16
AF = mybir.ActivationFunctionType
ALU = mybir.AluOpType

D = 1024
KC = 8
NMETA = 16
SEQ = 2048
T = NMETA + SEQ
DFF = 2816
FC = 22
EPS = 1e-6
TT = [(0, 16), (16, 512), (528, 512), (1040, 512), (1552, 512)]
PASSES = [[0, 1, 2], [3, 4]]
PLO = [0, 1040]
PW = 1040
POOL_SIZES = (2, 4, 8, 16)
SLAB = 2048
NS = 2
NB = 3


class _Op:
    __slots__ = ("eng", "fn", "reads", "writes", "dma", "deps", "raw", "signal", "count", "idx")


class Sched:
    def __init__(self, same_engine_sync=True):
        self.ops = []
        self.last_w = {}
        self.readers = {}
        self.same_engine_sync = same_engine_sync

    def add(self, eng, fn, reads=(), writes=(), dma=None):
        op = _Op()
        op.eng = eng
        op.fn = fn
        op.dma = dma
        op.reads = tuple(reads)
        op.writes = tuple(writes)
        op.signal = False
        op.count = None
        op.idx = len(self.ops)
        deps = set()
        for r in op.reads:
            w = self.last_w.get(r)
            if w is not None:
                deps.add(w)
        op.raw = set(deps)
        for w_ in op.writes:
            w = self.last_w.get(w_)
            if w is not None:
                deps.add(w)
            for rd in self.readers.get(w_, ()):
                deps.add(rd)
        deps.discard(op.idx)
        for r in op.reads:
            self.readers.setdefault(r, []).append(op.idx)
        for w_ in op.writes:
            self.last_w[w_] = op.idx
            self.readers[w_] = []
        op.deps = deps
        self.ops.append(op)
        return op

    @staticmethod
    def _key(p):
        return ("dma", p.dma) if p.dma is not None else ("eng", p.eng)

    def finalize(self, force_signal=()):
        ops = self.ops
        for op in ops:
            best = {}
            for d in op.deps:
                p = ops[d]
                key = self._key(p)
                if p.dma is None and p.eng == op.eng and op.dma is None:
                    if p.eng == "pe" or not self.same_engine_sync:
                        continue
                if key not in best or best[key] < d:
                    best[key] = d
            op.deps = set(best.values())
            for d in op.deps:
                ops[d].signal = True
        for p in force_signal:
            p.signal = True
        cnt = {}
        for op in ops:
            if op.signal:
                key = self._key(op)
                cnt[key] = cnt.get(key, 0) + (16 if op.dma is not None else 1)
                op.count = cnt[key]
        self.sem_keys = sorted(cnt.keys(), key=str)

    def emit(self, nc, final_waits=()):
        ops = self.ops
        with contextlib.ExitStack() as es:
            sems = {}
            for i, k in enumerate(self.sem_keys):
                sems[k] = es.enter_context(nc.semaphore("s%d" % i))
            block = es.enter_context(nc.Block())

            def run(engname):
                def body(e):
                    waited = {}
                    for op in ops:
                        if op.eng != engname:
                            continue
                        for d in sorted(op.deps):
                            p = ops[d]
                            key = self._key(p)
                            if waited.get(key, 0) >= p.count:
                                continue
                            waited[key] = p.count
                            e.wait_ge(sems[key], p.count)
                        ins = op.fn(e)
                        if op.signal:
                            ins.then_inc(sems[self._key(op)], 16 if op.dma is not None else 1)
                    for (en, p) in final_waits:
                        if en == engname:
                            e.wait_ge(sems[self._key(p)], p.count)
                return body

            block.tensor(run("pe"))
            block.scalar(run("act"))
            block.vector(run("dve"))
            block.gpsimd(run("pool"))
            block.sync(run("sp"))


def _t5_bucket_np(dist):
    n = np.maximum(dist, 0)
    nf = np.maximum(n, 1).astype(np.float32)
    large = 16 + (np.log(nf / np.float32(16)) / np.float32(np.log(128 / 16)) * np.float32(16)).astype(np.int32)
    large = np.minimum(large, 31)
    return np.where(n < 16, n, large)


def _slabs_w1(w1):
    g = w1[:, :DFF].reshape(KC, 128, FC, 128)
    u = w1[:, DFF:].reshape(KC, 128, FC, 128)
    s = np.concatenate([g, u], axis=0)
    return np.ascontiguousarray(s.transpose(2, 1, 0, 3)).reshape(FC, 128, 2048)


def _slabs_w2(w2):
    s = w2.reshape(2, 11, 128, KC, 128)
    return np.ascontiguousarray(s.transpose(3, 0, 2, 1, 4)).reshape(16, 128, 1408)


def _colslab(w):
    return np.ascontiguousarray(w.reshape(KC, 128, 128).transpose(1, 0, 2)).reshape(128, 1024)


def prep_common(inp):
    f = lambda a: np.ascontiguousarray(np.asarray(a, dtype=np.float32))
    norm_g = f(inp["norm_g"])
    w1 = f(inp["ffn_w1"])
    w2 = f(inp["ffn_w2"])
    w_in = f(inp["mix_w_in"])[0]
    w_out = f(inp["mix_w_out"])[0]
    rel_bias = f(inp["rel_bias"])
    sinks = f(inp["attn_sinks"])[0]
    conv_w = f(inp["conv_w"])[0]
    pool_w = f(inp["pool_w"])[0]
    pool_scale = f(inp["pool_scale"])[0]
    meta = f(inp["meta_tokens"])
    out = {}
    out["metaT"] = np.ascontiguousarray(meta.T)
    gT = norm_g.reshape(12, KC, 128).transpose(2, 0, 1).reshape(128, 96)
    psc = pool_scale.reshape(KC, 128).T
    cw = conv_w.reshape(3, 4, 128).transpose(2, 1, 0).reshape(128, 12)
    sinkP = np.repeat(sinks.reshape(4, 2), 64, axis=1).T
    b31 = np.broadcast_to(rel_bias[31][None, :], (128, 8))
    out["vecs"] = np.ascontiguousarray(np.concatenate([gT, psc, cw, sinkP, b31], axis=1))
    out["w1s"] = np.stack([_slabs_w1(w1[l, s]) for l in range(2) for s in range(2)])
    out["w2s"] = np.stack([_slabs_w2(w2[l, s]) for l in range(2) for s in range(2)])
    q0, k0, v0, b0, c0, x0 = 0, 512, 640, 768, 1280, 1792
    z64 = np.zeros((D, 64), np.float32)
    slabs = []
    for ci in range(4):
        slabs.append(_colslab(w_in[:, c0 + ci * 128:c0 + (ci + 1) * 128]))
        slabs.append(_colslab(w_in[:, x0 + ci * 128:x0 + (ci + 1) * 128]))
        slabs.append(_colslab(w_in[:, b0 + ci * 128:b0 + (ci + 1) * 128]))
    for c in range(4):
        slabs.append(_colslab(w_in[:, q0 + c * 128:q0 + (c + 1) * 128]))
    for kv in range(2):
        kk_ = w_in[:, k0 + kv * 64:k0 + (kv + 1) * 64]
        slabs.append(_colslab(np.concatenate([kk_, kk_], axis=1)))
    slabs.append(_colslab(w_in[:, v0:v0 + 128]))
    slabs.append(np.zeros((128, 1024), np.float32))
    sl = np.stack(slabs)
    out["wins"] = np.ascontiguousarray(sl.reshape(10, 2, 128, 1024).transpose(0, 2, 1, 3)).reshape(10, 128, 2048)
    wo = np.stack([_colslab(w_out[:, m * 128:(m + 1) * 128]) for m in range(8)])
    out["wouts"] = np.ascontiguousarray(wo.reshape(4, 2, 128, 1024).transpose(0, 2, 1, 3)).reshape(4, 128, 2048)
    pw = pool_w.reshape(4, 2, 128, 2, 128)
    out["pws"] = np.ascontiguousarray(pw.transpose(2, 0, 1, 3, 4)).reshape(128, 2048)
    kk_ = np.arange(128)[:, None]
    nn_ = np.arange(128)[None, :]
    bands = np.zeros((128, 4, 3, 128), np.float32)
    for gi, wsz in enumerate(POOL_SIZES):
        dcur = nn_ - kk_
        bands[:, gi, 0] = np.where((dcur >= 0) & (dcur < wsz), 1.0 / wsz, 0.0) - (dcur == 0)
        bands[:, gi, 1] = np.where(128 + dcur < wsz, 1.0 / wsz, 0.0)
        bands[:16, gi, 2] = np.where(16 + nn_ - kk_[:16] < wsz, 1.0 / wsz, 0.0)
    out["bands"] = np.ascontiguousarray(bands.reshape(128, 1536))
    s_ = np.arange(128)[:, None]
    r_ = np.arange(128)[None, :]
    bk_cur = _t5_bucket_np(r_ - s_)
    bk_prev = _t5_bucket_np(128 + r_ - s_)
    bt = np.stack([rel_bias[bk_cur], rel_bias[bk_prev]], axis=0)
    out["btab"] = np.ascontiguousarray(bt.transpose(1, 3, 0, 2)).reshape(128, 8 * 256)
    mcur = (r_ - s_ >= 0).astype(np.float32)
    mprev = (s_ > r_).astype(np.float32)
    out["mtab"] = np.ascontiguousarray(np.stack([mcur, mprev], axis=1)).reshape(128, 256)
    m_ = np.arange(16)[:, None]
    bk_m1 = _t5_bucket_np(16 + r_ - m_)
    rq = np.arange(16)[None, :]
    bk_m0 = _t5_bucket_np(rq - m_)
    bm = np.concatenate([rel_bias[bk_m1].transpose(0, 2, 1), rel_bias[bk_m0].transpose(0, 2, 1)], axis=2)
    mm0 = np.broadcast_to((rq - m_ >= 0).astype(np.float32)[:, None, :], (16, 8, 16))
    out["bmeta"] = np.ascontiguousarray(np.concatenate([bm, mm0], axis=2)).reshape(16, 8 * 160)
    return out


class Builder:
    def __init__(self, nc, es, stop_after=None, debug=False, same_engine_sync=True):
        self.nc = nc
        self.S = Sched(same_engine_sync)
        self.stop_after = stop_after
        self.debug = debug
        S = self.S
        dt = nc.dram_tensor
        self.xT = dt("xT", [D, SEQ], F32, kind="ExternalInput").ap()
        self.metaT = dt("metaT", [D, NMETA], F32, kind="ExternalInput").ap()
        self.vecs_d = dt("vecs", [128, 128], F32, kind="ExternalInput").ap()
        self.w1s = dt("w1s", [4, FC, 128, 2048], F32, kind="ExternalInput").ap()
        self.w2s = dt("w2s", [4, 16, 128, 1408], F32, kind="ExternalInput").ap()
        self.wins = dt("wins", [10, 128, 2048], F32, kind="ExternalInput").ap()
        self.wouts = dt("wouts", [4, 128, 2048], F32, kind="ExternalInput").ap()
        self.pws = dt("pws", [128, 2048], F32, kind="ExternalInput").ap()
        self.bands_d = dt("bands", [128, 1536], F32, kind="ExternalInput").ap()
        self.btab_d = dt("btab", [128, 2048], F32, kind="ExternalInput").ap()
        self.mtab_d = dt("mtab", [128, 256], F32, kind="ExternalInput").ap()
        self.bmeta_d = dt("bmeta", [16, 1280], F32, kind="ExternalInput").ap()
        self.outT = dt("outT", [D, SEQ], F32, kind="ExternalOutput").ap()

        sb = lambda name, n, dtp: es.enter_context(nc.sbuf_tensor(name, [128, n], dtp))
        self.hT = sb("hT", KC * T, F32)
        self.vecs = sb("vecs_sb", 128, F32)
        self.gc = sb("gc", 96, F32)
        self.ones = sb("ones", 128, BF16)
        self.esink = sb("esink", 4, F32)
        self.eb31 = sb("eb31", 8, F32)
        self.stage = [sb("stage%d" % i, SLAB, F32) for i in range(NS)]
        self.wb = [sb("wb%d" % i, SLAB, BF16) for i in range(NB)]
        self.rstd = [sb("rstd%d" % i, 512, F32) for i in range(2)]
        self.sq = [sb("sq%d" % i, 512, BF16) for i in range(2)]
        self.tmp = [sb("tmp%d" % i, 512, F32) for i in range(2)]
        ARENA_F32 = 25408
        self.arena = sb("arena", ARENA_F32, F32)
        self.junk = sb("junk", 8, F32)
        self.ones1 = sb("ones1", 64, BF16)
        self.ps = [es.enter_context(nc.psum_tensor("ps%d" % i, [128, 512], F32)) for i in range(8)]
        self.bank = 0
        self.cnt = {"rstd": 0, "sq": 0, "tmp": 0, "Pt": 0, "Pm": 0, "d2": 0}
        a = self.arena
        self.xn = a[:, 0:4160].bitcast(BF16)
        self.ybuf = a[:, 4160:4160 + 8320]
        self.act = a[:, 12480:12480 + 11440].bitcast(BF16)
        o = 0
        o_u = o
        self.uT = a[:, o:o + 8256].bitcast(BF16); o += 8256
        o_q = o
        self.qT = a[:, o:o + 4128].bitcast(BF16); o += 4128
        self.kz = a[:, o:o + 4128].bitcast(BF16); o += 4128
        self.Vt = a[:, o:o + 1088].bitcast(BF16); o += 1088
        o_c = o
        self.ucv = a[:, o:o + 2080]; o += 2080
        self.ymC = a[:, o:o + 4128].bitcast(BF16); o += 4128
        self.EBT = a[:, o:o + 1024].bitcast(BF16); o += 1024
        self.EBm1 = a[:, o:o + 512].bitcast(BF16); o += 512
        self.EBm0 = a[:, o:o + 64].bitcast(BF16); o += 64
        assert o <= ARENA_F32, o
        self.ymA = a[:, o_u:o_u + 4128].bitcast(BF16)
        self.woutb = a[:, o_u + 4128:o_u + 4128 + 4096].bitcast(BF16)
        self.ybufM = a[:, o_q + 4128:o_q + 4128 + 4096]
        self.ybufM2 = a[:, o_q:o_q + 4096]
        self.btab = a[:, o_q:o_q + 2048]
        self.mtab = a[:, o_q + 2048:o_q + 2304]
        self.bmeta = a[:, o_q + 2304:o_q + 2304 + 1280]
        oc = o_c
        self.Pt = [a[:, oc + i * 256:oc + (i + 1) * 256].bitcast(BF16) for i in range(2)]; oc += 512
        self.Pm = [a[:, oc + i * 128:oc + (i + 1) * 128].bitcast(BF16) for i in range(2)]; oc += 256
        self.d2 = [a[:, oc + i * 128:oc + (i + 1) * 128] for i in range(2)]; oc += 256
        assert oc <= o_c + 2080
        o = 8256
        self.zb = [a[:, o + i * 512:o + (i + 1) * 512].bitcast(BF16) for i in range(3)]; o += 1536
        self.ybufP = [a[:, o + i * 4096:o + (i + 1) * 4096] for i in range(2)]; o += 8192
        self.bandf = a[:, o:o + 1536]; o += 1536
        self.bandb = a[:, o:o + 768].bitcast(BF16); o += 768
        assert o <= ARENA_F32, o
        self.out_dmas = []
        self.dbg_out = []

    def nb(self):
        ov = getattr(self, "_nb_override", None)
        if ov is not None:
            b = ov[0][ov[1] % len(ov[0])]
            ov[1] += 1
            return b
        b = self.bank
        self.bank = (self.bank + 1) % 8
        return b

    def rot(self, name):
        i = self.cnt[name]
        self.cnt[name] = (i + 1) % 2
        return i

    def hsl(self, c, c0, n):
        return self.hT[:, c * T + c0:c * T + c0 + n]

    def fence(self, reads, writes):
        self.S.add("pool", lambda e: e.memset(self.junk[0:1, 0:1], 0.0), reads=list(reads), writes=list(reads) + list(writes) + ["junk"])

    def wplan(self, items):
        self.witems = items
        self.w_dma = 0
        self.w_cast = 0
        self.w_use = 0

    def _pump(self, i):
        S = self.S
        n_items = len(self.witems)
        while True:
            prog = False
            if self.w_dma < n_items and self.w_dma - NS < self.w_cast:
                k = self.w_dma
                ap, n, dest, dkeys = self.witems[k]
                sl = k % NS
                S.add("sp", lambda e, sl=sl, n=n, ap=ap: e.dma_start(out=self.stage[sl][:, 0:n], in_=ap),
                      writes=[("stage", sl)], dma=("stage", sl))
                self.w_dma += 1
                prog = True
            if (self.w_cast < n_items and self.w_cast < self.w_dma and self.w_cast <= i + NB - 1
                    and (self.witems[self.w_cast][2] is None or self.w_cast <= i)):
                k = self.w_cast
                ap, n, dest, dkeys = self.witems[k]
                sl = k % NS
                if dest is None:
                    dst = self.wb[k % NB][:, 0:n]
                    wk = [("wb", k % NB)]
                else:
                    dst = dest
                    wk = dkeys
                for p0 in range(0, n, 512):
                    p1 = min(n, p0 + 512)
                    if p0 == 0:
                        S.add("pool", lambda e, dst=dst, sl=sl, p0=p0, p1=p1: e.tensor_copy(out=dst[:, p0:p1], in_=self.stage[sl][:, p0:p1]),
                              reads=[("stage", sl)], writes=wk)
                    else:
                        S.add("act", lambda e, dst=dst, sl=sl, p0=p0, p1=p1: e.activation(out=dst[:, p0:p1], in_=self.stage[sl][:, p0:p1], func=AF.Copy),
                              reads=[("stage", sl)], writes=wk)
                self.w_cast += 1
                prog = True
            if not prog:
                break

    def wnext(self):
        i = self.w_use
        self.w_use += 1
        self._pump(i)
        ap, n, dest, dkeys = self.witems[i]
        if dest is None:
            return self.wb[i % NB][:, 0:n], [("wb", i % NB)]
        return dest, dkeys

    def setup(self):
        S = self.S
        S.add("sp", lambda e: e.dma_start(out=self.vecs[:, :], in_=self.vecs_d), writes=["vecs"], dma="vecs")
        S.add("sp", lambda e: e.dma_start(out=self.hT[:, :].rearrange("p (c t) -> p c t", c=KC)[:, :, 0:NMETA],
                                           in_=self.metaT.rearrange("(c p) t -> p c t", p=128)),
              writes=[("h", c, 0) for c in range(KC)], dma="meta")
        for t in range(1, 5):
            c0, n = TT[t]
            S.add("sp", lambda e, c0=c0, n=n: e.dma_start(out=self.hT[:, :].rearrange("p (c t) -> p c t", c=KC)[:, :, c0:c0 + n],
                                                           in_=self.xT.rearrange("(c p) t -> p c t", p=128)[:, :, c0 - NMETA:c0 - NMETA + n]),
                  writes=[("h", c, t) for c in range(KC)], dma=("x", t))
        S.add("pool", lambda e: e.memset(self.ones[:, :], 1.0 / 1024), writes=["ones"])
        S.add("pool", lambda e: e.memset(self.ones1[:, :], 1.0), writes=["ones1"])
        for l in range(2):
            for i, coef in ((1, 0.5), (3, 1.0), (5, 0.5)):
                o = (l * 6 + i) * 8
                S.add("dve", lambda e, o=o, coef=coef: e.tensor_scalar(out=self.gc[:, o:o + 8], in0=self.vecs[:, o:o + 8],
                                                                       scalar1=coef, scalar2=None, op0=ALU.mult),
                      reads=["vecs"], writes=[("gc", l, i)])

    def rms_stats(self, src_fn, src_keys_fn, t, nchunks=8):
        S = self.S
        c0, n = TT[t]
        b = self.nb()
        for c in range(nchunks):
            si = self.rot("sq")
            S.add("act", lambda e, c=c, si=si: e.activation(out=self.sq[si][:, 0:n], in_=src_fn(c), func=AF.Square),
                  reads=src_keys_fn(c), writes=[("sq", si)])
            S.add("pe", lambda e, c=c, si=si, b=b: e.matmul(self.ps[b][:, 0:n], lhsT=self.ones[:, :], rhs=self.sq[si][:, 0:n],
                                                            start=(c == 0), stop=(c == nchunks - 1)),
                  reads=[("sq", si), "ones"], writes=[("ps", b)])
        ri = self.rot("rstd")
        r = self.rstd[ri]
        S.add("dve", lambda e, b=b: e.tensor_scalar(out=r[:, 0:n], in0=self.ps[b][:, 0:n], scalar1=EPS, scalar2=None, op0=ALU.add),
              reads=[("ps", b)], writes=[("rstd", ri)])
        S.add("dve", lambda e: e.reciprocal(out=r[:, 0:n], in_=r[:, 0:n]), reads=[("rstd", ri)], writes=[("rstd", ri)])
        S.add("act", lambda e: e.activation(out=r[:, 0:n], in_=r[:, 0:n], func=AF.Sqrt), reads=[("rstd", ri)], writes=[("rstd", ri)])
        return ri

    def prenorm_tile(self, t, gidx, dst_fn, dst_key_fn):
        S = self.S
        c0, n = TT[t]
        ri = self.rms_stats(lambda c: self.hsl(c, c0, n), lambda c: [("h", c, t)], t)
        for c in range(KC):
            S.add("dve", lambda e, c=c: e.scalar_tensor_tensor(out=dst_fn(c), in0=self.hsl(c, c0, n),
                                                               scalar=self.vecs[:, gidx * 8 + c:gidx * 8 + c + 1],
                                                               in1=self.rstd[ri][:, 0:n], op0=ALU.mult, op1=ALU.mult),
                  reads=[("h", c, t), ("rstd", ri), "vecs"], writes=[dst_key_fn(c)])

    def prenorm_parts(self, t, gidx, dst_fn, dst_key_fn, nsplit=4):
        S = self.S
        c0, n = TT[t]
        st = {}

        def stats():
            st["ri"] = self.rms_stats(lambda c: self.hsl(c, c0, n), lambda c: [("h", c, t)], t)

        def apply(cs):
            ri = st["ri"]
            for c in cs:
                S.add("dve", lambda e, c=c: e.scalar_tensor_tensor(out=dst_fn(c), in0=self.hsl(c, c0, n),
                                                                   scalar=self.vecs[:, gidx * 8 + c:gidx * 8 + c + 1],
                                                                   in1=self.rstd[ri][:, 0:n], op0=ALU.mult, op1=ALU.mult),
                      reads=[("h", c, t), ("rstd", ri), "vecs"], writes=[dst_key_fn(c)])

        per = KC // nsplit
        return [stats] + [(lambda cs=list(range(k * per, (k + 1) * per)): apply(cs)) for k in range(nsplit)]

    def post_parts(self, t, ysrc_fn, ykey_fn, l, i, final=False, nsplit=2, add_eng="dve", pool_ms=()):
        S = self.S
        c0, n = TT[t]
        go = (l * 6 + i) * 8
        st = {}

        def stats():
            st["ri"] = self.rms_stats(ysrc_fn, lambda m: [ykey_fn(m)], t)

        def apply(ms, last):
            ri = st["ri"]
            for m in ms:
                if m in pool_ms:
                    S.add("dve", lambda e, m=m: e.scalar_tensor_tensor(out=ysrc_fn(m), in0=ysrc_fn(m),
                                                                        scalar=self.gc[:, go + m:go + m + 1],
                                                                        in1=self.rstd[ri][:, 0:n], op0=ALU.mult, op1=ALU.mult),
                          reads=[ykey_fn(m), ("rstd", ri), ("gc", l, i)], writes=[ykey_fn(m)])
                    S.add("pool", lambda e, m=m: e.tensor_tensor(out=self.hsl(m, c0, n), in0=self.hsl(m, c0, n),
                                                                 in1=ysrc_fn(m), op=ALU.add),
                          reads=[ykey_fn(m), ("h", m, t)], writes=[("h", m, t)])
                    continue
                ti = self.rot("tmp")
                S.add("dve", lambda e, m=m, ti=ti: e.scalar_tensor_tensor(out=self.tmp[ti][:, 0:n], in0=ysrc_fn(m),
                                                                          scalar=self.gc[:, go + m:go + m + 1],
                                                                          in1=self.rstd[ri][:, 0:n], op0=ALU.mult, op1=ALU.mult),
                      reads=[ykey_fn(m), ("rstd", ri), ("gc", l, i)], writes=[("tmp", ti)])
                S.add(add_eng, lambda e, m=m, ti=ti: e.tensor_tensor(out=self.hsl(m, c0, n), in0=self.hsl(m, c0, n),
                                                                     in1=self.tmp[ti][:, 0:n], op=ALU.add),
                      reads=[("tmp", ti), ("h", m, t)], writes=[("h", m, t)])
            if last and final and t > 0:
                oc0 = c0 - NMETA
                op = S.add("sp", lambda e: e.dma_start(out=self.outT.rearrange("(c p) t -> p c t", p=128)[:, :, oc0:oc0 + n],
                                                       in_=self.hT[:, :].rearrange("p (c t) -> p c t", c=KC)[:, :, c0:c0 + n]),
                           reads=[("h", m, t) for m in range(KC)], dma=("out", t))
                self.out_dmas.append(op)

        per = KC // nsplit
        parts = [stats]
        for k in range(nsplit):
            ms = list(range(k * per, (k + 1) * per))
            parts.append(lambda ms=ms, last=(k == nsplit - 1): apply(ms, last))
        return parts

    def post_tile(self, t, ysrc_fn, ykey_fn, l, i, final=False, add_eng="dve", pool_ms=()):
        for f in self.post_parts(t, ysrc_fn, ykey_fn, l, i, final=final, nsplit=1, add_eng=add_eng, pool_ms=pool_ms):
            f()

    def ffn(self, l, s, final=False, next_ffn=None):
        S = self.S
        f = l * 2 + s
        gpre = l * 6 + (0 if s == 0 else 4)
        ipost = 1 if s == 0 else 5

        def xn_ap(p, c, t):
            c0, n = TT[t]
            off = c * PW + (c0 - PLO[p])
            return self.xn[:, off:off + n]

        def act_ap(p, j, t):
            c0, n = TT[t]
            off = j * PW + (c0 - PLO[p])
            return self.act[:, off:off + n]

        def y_ap(p, m, t):
            c0, n = TT[t]
            off = m * PW + (c0 - PLO[p])
            return self.ybuf[:, off:off + n]

        PASSES_ = [[t for t in ps_ if not (final and t == 0)] for ps_ in PASSES]

        def prenorm(p):
            for t in PASSES_[p]:
                self.prenorm_tile(t, gpre, lambda c, t=t: xn_ap(p, c, t), lambda c, t=t: ("xn", c, t))

        def w1phase(p, hooks=None):
            for j in range(FC):
                w, wk = self.wnext()
                for t in PASSES_[p]:
                    c0, n = TT[t]
                    ba = self.nb()
                    bb = self.nb()
                    for half, bk in ((0, ba), (1, bb)):
                        for kk in range(KC):
                            S.add("pe", lambda e, kk=kk, bk=bk, half=half, t=t, n=n, w=w: e.matmul(
                                self.ps[bk][:, 0:n], lhsT=w[:, (half * 8 + kk) * 128:(half * 8 + kk + 1) * 128],
                                rhs=xn_ap(p, kk, t), start=(kk == 0), stop=(kk == KC - 1)),
                                reads=wk + [("xn", kk, t)], writes=[("ps", bk)])
                    ti = self.rot("tmp")
                    S.add("act", lambda e, ba=ba, ti=ti, n=n: e.activation(out=self.tmp[ti][:, 0:n], in_=self.ps[ba][:, 0:n], func=AF.Silu),
                          reads=[("ps", ba)], writes=[("tmp", ti)])
                    S.add("dve", lambda e, bb=bb, ti=ti, n=n, j=j, t=t: e.tensor_tensor(out=act_ap(p, j, t), in0=self.tmp[ti][:, 0:n],
                                                                                         in1=self.ps[bb][:, 0:n], op=ALU.mult),
                          reads=[("tmp", ti), ("ps", bb)], writes=[("act", j, t)])
                if hooks and j in hooks:
                    hooks[j]()

        def w2phase(p):
            for m in range(KC):
                banks = {t: self.nb() for t in PASSES_[p]}
                for half in range(2):
                    w, wk = self.wnext()
                    for t in PASSES_[p]:
                        c0, n = TT[t]
                        for k2 in range(11):
                            kk = half * 11 + k2
                            S.add("pe", lambda e, kk=kk, k2=k2, t=t, n=n, w=w, bk=banks[t]: e.matmul(
                                self.ps[bk][:, 0:n], lhsT=w[:, k2 * 128:(k2 + 1) * 128], rhs=act_ap(p, kk, t),
                                start=(kk == 0), stop=(kk == FC - 1)),
                                reads=wk + [("act", kk, t)], writes=[("ps", banks[t])])
                for t in PASSES_[p]:
                    c0, n = TT[t]
                    S.add("act", lambda e, m=m, t=t, n=n, bk=banks[t]: e.activation(out=y_ap(p, m, t), in_=self.ps[bk][:, 0:n], func=AF.Copy),
                          reads=[("ps", banks[t])], writes=[("y", m, t)])

        def post_t(p, t):
            self.post_tile(t, lambda m, t=t: y_ap(p, m, t), lambda m, t=t: ("y", m, t), l, ipost, final=final)

        def post_hooks(p, tiles, j0):
            hk = {}
            j = j0
            for t in tiles:
                if TT[t][1] <= 16:
                    hk[j] = (lambda t=t: post_t(p, t))
                    j += 1
                    continue
                parts = self.post_parts(t, lambda m, t=t: y_ap(p, m, t), lambda m, t=t: ("y", m, t), l, ipost, final=final, nsplit=4)
                for f in parts:
                    hk[j] = f
                    j += 1
            return hk

        pend = getattr(self, "_pending_post", None)
        self._pending_post = None
        if not getattr(self, "_prenorm0_done", False):
            prenorm(0)
        self._prenorm0_done = False
        pre1 = [self.prenorm_parts(t, gpre, lambda c, t=t: xn_ap(1, c, t), lambda c, t=t: ("xn", c, t), nsplit=1)
                for t in PASSES_[1]]
        hooksA = dict(pend or {})
        assert (FC - 4) not in hooksA and (FC - 2) not in hooksA
        hooksA[FC - 4] = pre1[0][0]
        hooksA[FC - 2] = pre1[1][0]
        w1phase(0, hooks=hooksA)
        for p_ in pre1:
            p_[1]()
        w2phase(0)
        hooksB = post_hooks(0, PASSES_[0], 1)
        npre = None
        if next_ffn is not None:
            nl, ns_ = next_ffn
            ngpre = nl * 6 + (0 if ns_ == 0 else 4)
            npre = [self.prenorm_parts(t, ngpre, lambda c, t=t: xn_ap(0, c, t), lambda c, t=t: ("xn", c, t), nsplit=1)
                    for t in (1, 2)]
            assert (FC - 5) not in hooksB and (FC - 3) not in hooksB and max(hooksB) < FC - 5
            hooksB[FC - 5] = npre[0][0]
            hooksB[FC - 3] = npre[1][0]
        w1phase(1, hooks=hooksB)
        if next_ffn is not None:
            for p_ in npre:
                p_[1]()
            self.prenorm_tile(0, ngpre, lambda c: xn_ap(0, c, 0), lambda c: ("xn", c, 0))
            self._prenorm0_done = True
        w2phase(1)
        if next_ffn is not None:
            self._pending_post = post_hooks(1, [3, 4], 1)
        else:
            post_t(1, 3)
            post_t(1, 4)

    def _ffn_keys(self):
        return ([("xn", c, t) for c in range(KC) for t in range(5)] + [("y", m, t) for m in range(KC) for t in range(5)]
                + [("act", j, t) for j in range(FC) for t in range(5)])

    def _m0_keys(self):
        return (["btab", "mtab", "bmeta", "EBT", "EBm1", "EBm0", "V", "ucvpad", "woutb"]
                + [("uT", c, t) for c in range(KC) for t in range(5)]
                + [("q", c, t) for c in range(4) for t in range(5)] + [("kz", c, t) for c in range(4) for t in range(5)]
                + [("ucv", t) for t in range(5)] + [("ymC", c, t) for c in range(4) for t in range(5)]
                + [("ymA", c, t) for c in range(4) for t in range(5)] + [("ybM", i, m) for i in range(2) for m in range(KC)]
                + [("Pt", i) for i in range(2)] + [("Pm", i) for i in range(2)] + [("d2", i) for i in range(2)])

    def _m1_keys(self):
        return (["bandf", "bandb"] + [("uT", c, t) for c in range(KC) for t in range(5)] + [("z", i) for i in range(3)]
                + [("ybP", i, m) for i in range(2) for m in range(KC)])

    def half_slab(self):
        if getattr(self, "_hs", None) is None:
            w, wk = self.wnext()
            self._hs = (w, wk)
            return w[:, 0:1024], wk
        w, wk = self._hs
        self._hs = None
        return w[:, 1024:2048], wk

    def proj_tile(self, w, wk, src_fn, src_key_fn, t):
        S = self.S
        c0, n = TT[t]
        b = self.nb()
        for kk in range(KC):
            S.add("pe", lambda e, kk=kk, b=b, n=n, t=t: e.matmul(self.ps[b][:, 0:n], lhsT=w[:, kk * 128:(kk + 1) * 128],
                                                                  rhs=src_fn(kk, t), start=(kk == 0), stop=(kk == KC - 1)),
                  reads=wk + [src_key_fn(kk, t)], writes=[("ps", b)])
        return b

    def mixer0(self):
        S = self.S
        V = self.vecs
        self.fence(self._ffn_keys(), self._m0_keys())
        S.add("pool", lambda e: e.memset(self.ucv[:, 0:16], 0.0), writes=["ucvpad"])
        gidx = 2

        def uT_ap(c, t):
            c0, n = TT[t]
            return self.uT[:, c * T + c0:c * T + c0 + n]

        for t in range(5):
            self.prenorm_tile(t, gidx, lambda c, t=t: uT_ap(c, t), lambda c, t=t: ("uT", c, t))
        ukey = lambda kk, t: ("uT", kk, t)
        S.add("sp", lambda e: e.dma_start(out=self.btab, in_=self.btab_d), writes=["btab"], dma="btab")
        S.add("sp", lambda e: e.dma_start(out=self.mtab, in_=self.mtab_d), writes=["mtab"], dma="mtab")
        S.add("sp", lambda e: e.dma_start(out=self.bmeta[0:16, :], in_=self.bmeta_d), writes=["bmeta"], dma="bmeta")
        S.add("act", lambda e: e.activation(out=self.esink[:, :], in_=V[:, 116:120], func=AF.Exp), reads=["vecs"], writes=["esink"])
        S.add("act", lambda e: e.activation(out=self.eb31[:, :], in_=V[:, 120:128], func=AF.Exp), reads=["vecs"], writes=["eb31"])
        for h_ in range(8):
            bt_h = self.btab[:, h_ * 256:(h_ + 1) * 256]
            S.add("act", lambda e, bt_h=bt_h: e.activation(out=bt_h, in_=bt_h, func=AF.Exp), reads=["btab"], writes=["btab"])
            S.add("dve", lambda e, bt_h=bt_h, h_=h_: e.tensor_tensor(out=self.EBT[:, h_ * 256:(h_ + 1) * 256], in0=bt_h, in1=self.mtab, op=ALU.mult),
                  reads=["btab", "mtab"], writes=["EBT"])
            bm_h = self.bmeta[0:16, h_ * 160:(h_ + 1) * 160]
            S.add("act", lambda e, bm_h=bm_h: e.activation(out=bm_h[:, 0:144], in_=bm_h[:, 0:144], func=AF.Exp), reads=["bmeta"], writes=["bmeta"])
            S.add("dve", lambda e, bm_h=bm_h, h_=h_: e.tensor_copy(out=self.EBm1[0:16, h_ * 128:(h_ + 1) * 128], in_=bm_h[:, 0:128]),
                  reads=["bmeta"], writes=["EBm1"])
            S.add("dve", lambda e, bm_h=bm_h, h_=h_: e.tensor_tensor(out=self.EBm0[0:16, h_ * 16:(h_ + 1) * 16], in0=bm_h[:, 128:144],
                                                                       in1=bm_h[:, 144:160], op=ALU.mult),
                  reads=["bmeta"], writes=["EBm0"])
        for ci in range(4):
            w, wk = self.half_slab()
            for t in range(5):
                c0, n = TT[t]
                b = self.proj_tile(w, wk, uT_ap, ukey, t)
                S.add("act", lambda e, b=b, c0=c0, n=n: e.activation(out=self.ucv[:, 16 + c0:16 + c0 + n], in_=self.ps[b][:, 0:n], func=AF.Copy),
                      reads=[("ps", b)], writes=[("ucv", t)])
            w, wk = self.half_slab()
            for t in range(5):
                c0, n = TT[t]
                b = self.proj_tile(w, wk, uT_ap, ukey, t)
                S.add("dve", lambda e, b=b, c0=c0, n=n: e.tensor_tensor(out=self.ucv[:, 16 + c0:16 + c0 + n], in0=self.ucv[:, 16 + c0:16 + c0 + n],
                                                                         in1=self.ps[b][:, 0:n], op=ALU.mult),
                      reads=[("ps", b), ("ucv", t)], writes=[("ucv", t)])
            w, wk = self.half_slab()
            for t in range(5):
                c0, n = TT[t]
                b = self.proj_tile(w, wk, uT_ap, ukey, t)
                ti = self.rot("tmp")
                tm = self.tmp[ti][:, 0:n]
                prevk = [("ucv", t - 1)] if t > 0 else ["ucvpad"]
                cwo = 104 + ci * 3
                S.add("act", lambda e, tm=tm, c0=c0, n=n, cwo=cwo: e.activation(out=tm, in_=self.ucv[:, 16 + c0:16 + c0 + n], func=AF.Copy,
                                                                                scale=V[:, cwo:cwo + 1]),
                      reads=[("ucv", t), "vecs"], writes=[("tmp", ti)])
                for j in (1, 2):
                    S.add("dve", lambda e, tm=tm, c0=c0, n=n, cwo=cwo, j=j: e.scalar_tensor_tensor(
                        out=tm, in0=self.ucv[:, 16 + c0 - j:16 + c0 - j + n], scalar=V[:, cwo + j:cwo + j + 1], in1=tm,
                        op0=ALU.mult, op1=ALU.add),
                        reads=[("ucv", t), ("tmp", ti), "vecs"] + prevk, writes=[("tmp", ti)])
                S.add("dve", lambda e, tm=tm, b=b, c0=c0, n=n, ci=ci: e.tensor_tensor(out=self.ymC[:, ci * T + c0:ci * T + c0 + n], in0=tm,
                                                                                       in1=self.ps[b][:, 0:n], op=ALU.mult),
                      reads=[("tmp", ti), ("ps", b)], writes=[("ymC", ci, t)])
        self.fence(["btab", "mtab", "bmeta"], [("q", c, t) for c in range(4) for t in range(5)])
        self.fence([("ucv", t) for t in range(5)] + ["ucvpad"],
                   [("Pt", i) for i in range(2)] + [("Pm", i) for i in range(2)] + [("d2", i) for i in range(2)])
        for kzc in range(4):
            r0 = 64 if kzc % 2 == 0 else 0
            for t in range(5):
                c0, n = TT[t]
                S.add("pool", lambda e, kzc=kzc, r0=r0, c0=c0, n=n: e.memset(self.kz[r0:r0 + 64, kzc * T + c0:kzc * T + c0 + n], 0.0),
                      writes=[("kz", kzc, t)])
        ev = 0
        for dst, name in ((self.qT, "q"),):
            for c in range(4):
                w, wk = self.half_slab()
                for t in range(5):
                    c0, n = TT[t]
                    b = self.proj_tile(w, wk, uT_ap, ukey, t)
                    o = c * T + c0
                    if ev % 2 == 0:
                        S.add("act", lambda e, b=b, o=o, n=n, dst=dst: e.activation(out=dst[:, o:o + n], in_=self.ps[b][:, 0:n], func=AF.Copy),
                              reads=[("ps", b)], writes=[(name, c, t)])
                    else:
                        S.add("dve", lambda e, b=b, o=o, n=n, dst=dst: e.tensor_copy(out=dst[:, o:o + n], in_=self.ps[b][:, 0:n]),
                              reads=[("ps", b)], writes=[(name, c, t)])
                    ev += 1
        for kv in range(2):
            w, wk = self.half_slab()
            for t in range(5):
                c0, n = TT[t]
                b = self.proj_tile(w, wk, uT_ap, ukey, t)
                oa = (kv * 2) * T + c0
                ob = (kv * 2 + 1) * T + c0
                S.add("act", lambda e, b=b, oa=oa, n=n: e.activation(out=self.kz[0:64, oa:oa + n], in_=self.ps[b][0:64, 0:n], func=AF.Copy),
                      reads=[("ps", b)], writes=[("kz", kv * 2, t)])
                S.add("dve", lambda e, b=b, ob=ob, n=n: e.tensor_copy(out=self.kz[64:128, ob:ob + n], in_=self.ps[b][64:128, 0:n]),
                      reads=[("ps", b)], writes=[("kz", kv * 2 + 1, t)])
        w, wk = self.half_slab()
        _pad, _ = self.half_slab()

        def blk(tb):
            return (0, 16, 0) if tb == 0 else (16 + 128 * (tb - 1), 128, 1 + (tb - 1) // 4)

        for tb in range(17):
            c0, n, t = blk(tb)
            b = self.nb()
            for kk in range(KC):
                S.add("pe", lambda e, kk=kk, b=b, c0=c0, n=n, w=w: e.matmul(self.ps[b][0:n, 0:128], lhsT=self.uT[:, kk * T + c0:kk * T + c0 + n],
                                                                        rhs=w[:, kk * 128:(kk + 1) * 128], start=(kk == 0), stop=(kk == KC - 1)),
                      reads=wk + [("uT", kk, t)], writes=[("ps", b)])
            S.add("act", lambda e, b=b, n=n, tb=tb: e.activation(out=self.Vt[0:n, tb * 128:(tb + 1) * 128], in_=self.ps[b][0:n, 0:128], func=AF.Copy),
                  reads=[("ps", b)], writes=["V"])
        self.fence([("uT", c, t) for c in range(KC) for t in range(5)],
                   [("ymA", c, t) for c in range(4) for t in range(5)] + ["woutb"])
        def attn_scores(bq, c):
            c0, n, tq = blk(bq)
            kv = c // 2
            qrhs = self.qT[:, c * T + c0:c * T + c0 + n]
            qkey = ("q", c, tq)
            nk_main = 0 if bq == 0 else (1 if bq == 1 else 2)
            pi = self.rot("Pt")
            pmi = self.rot("Pm")
            Pt = self.Pt[pi]
            Pm = self.Pm[pmi]
            if nk_main:
                bs = self.nb()
                for hh in range(2):
                    kzc = kv * 2 + hh
                    for k in range(nk_main):
                        kb = bq - k
                        kc0, kn, kt = blk(kb)
                        S.add("pe", lambda e, bs=bs, hh=hh, k=k, kzc=kzc, kc0=kc0, qrhs=qrhs: e.matmul(
                            self.ps[bs][:, (hh * 2 + k) * 128:(hh * 2 + k + 1) * 128],
                            lhsT=self.kz[:, kzc * T + kc0:kzc * T + kc0 + 128], rhs=qrhs, start=True, stop=True),
                            reads=[("kz", kzc, kt), qkey], writes=[("ps", bs)])
                ti = self.rot("tmp")
                if nk_main == 2:
                    src = self.ps[bs][:, 0:512]
                    tm = self.tmp[ti][:, 0:512]
                    pt = Pt[:, 0:512]
                    eb = self.EBT[:, 2 * c * 256:(2 * c + 2) * 256]
                else:
                    v3 = lambda ap: ap.rearrange("p (h x) -> p h x", h=2)[:, :, 0:128]
                    src = v3(self.ps[bs][:, 0:512])
                    tm = v3(self.tmp[ti][:, 0:512])
                    pt = v3(Pt[:, 0:512])
                    eb = v3(self.EBT[:, 2 * c * 256:(2 * c + 2) * 256])
                S.add("act", lambda e, src=src, tm=tm: e.activation(out=tm, in_=src, func=AF.Exp, scale=0.125),
                      reads=[("ps", bs)], writes=[("tmp", ti)])
                S.add("dve", lambda e, tm=tm, pt=pt, eb=eb: e.tensor_tensor(out=pt, in0=tm, in1=eb, op=ALU.mult),
                      reads=[("tmp", ti), "EBT"], writes=[("Pt", pi)])
            bm = self.nb()
            for hh in range(2):
                kzc = kv * 2 + hh
                S.add("pe", lambda e, bm=bm, hh=hh, kzc=kzc, qrhs=qrhs, n=n: e.matmul(
                    self.ps[bm][0:16, hh * n:(hh + 1) * n], lhsT=self.kz[:, kzc * T:kzc * T + 16], rhs=qrhs, start=True, stop=True),
                    reads=[("kz", kzc, 0), qkey], writes=[("ps", bm)])
            ti2 = self.rot("tmp")
            tm2 = self.tmp[ti2][0:16, 0:2 * n]
            S.add("act", lambda e, bm=bm, tm2=tm2, n=n: e.activation(out=tm2, in_=self.ps[bm][0:16, 0:2 * n], func=AF.Exp, scale=0.125),
                  reads=[("ps", bm)], writes=[("tmp", ti2)])
            if bq == 0:
                S.add("dve", lambda e, tm2=tm2, Pm=Pm, c=c: e.tensor_tensor(out=Pm[0:16, 0:32], in0=tm2, in1=self.EBm0[0:16, 2 * c * 16:(2 * c + 2) * 16], op=ALU.mult),
                      reads=[("tmp", ti2), "EBm0"], writes=[("Pm", pmi)])
            elif bq == 1:
                S.add("dve", lambda e, tm2=tm2, Pm=Pm, c=c: e.tensor_tensor(out=Pm[0:16, 0:256], in0=tm2, in1=self.EBm1[0:16, 2 * c * 128:(2 * c + 2) * 128], op=ALU.mult),
                      reads=[("tmp", ti2), "EBm1"], writes=[("Pm", pmi)])
            else:
                e16 = self.eb31[0:16, 2 * c:2 * c + 2]
                eb_bc = bass.AP(e16.tensor, e16.offset, [list(e16.ap[0]), list(e16.ap[1]), [0, 128]])
                S.add("dve", lambda e, Pm=Pm, ti2=ti2, eb_bc=eb_bc: e.tensor_tensor(
                    out=Pm[0:16, 0:256].rearrange("p (h x) -> p h x", h=2),
                    in0=self.tmp[ti2][0:16, 0:256].rearrange("p (h x) -> p h x", h=2), in1=eb_bc, op=ALU.mult),
                    reads=[("tmp", ti2), "eb31"], writes=[("Pm", pmi)])
            return (bq, c, nk_main, pi, pmi)

        def attn_pv(state):
            bq, c, nk_main, pi, pmi = state
            c0, n, tq = blk(bq)
            kv = c // 2
            Pt = self.Pt[pi]
            Pm = self.Pm[pmi]
            bo = self.nb()
            for hh in range(2):
                rows = slice(hh * 64, hh * 64 + 64)
                chunks = []
                for k in range(nk_main):
                    kb = bq - k
                    chunks.append((self.Vt[:, kb * 128 + kv * 64:kb * 128 + kv * 64 + 64], self.ones1[:, 0:64],
                                   Pt[:, (hh * 2 + k) * 128:(hh * 2 + k) * 128 + n], ("Pt", pi)))
                chunks.append((self.Vt[0:16, kv * 64:kv * 64 + 64], self.ones1[0:16, 0:64], Pm[0:16, hh * n:(hh + 1) * n], ("Pm", pmi)))
                for which in range(2):
                    for ci_, (vl, ol, pr, pk) in enumerate(chunks):
                        lhs = vl if which == 0 else ol
                        S.add("pe", lambda e, bo=bo, rows=rows, which=which, lhs=lhs, pr=pr, ci_=ci_, nch=len(chunks), n=n: e.matmul(
                            self.ps[bo][rows, which * 128:which * 128 + n], lhsT=lhs, rhs=pr, start=(ci_ == 0), stop=(ci_ == nch - 1)),
                            reads=["V", "ones1", pk], writes=[("ps", bo)])
            di = self.rot("d2")
            d2 = self.d2[di][:, 0:n]
            S.add("dve", lambda e, bo=bo, d2=d2, c=c, n=n: e.tensor_scalar(out=d2, in0=self.ps[bo][:, 128:128 + n], scalar1=self.esink[:, c:c + 1],
                                                                           scalar2=None, op0=ALU.add),
                  reads=[("ps", bo), "esink"], writes=[("d2", di)])
            S.add("dve", lambda e, d2=d2: e.reciprocal(out=d2, in_=d2), reads=[("d2", di)], writes=[("d2", di)])
            S.add("dve", lambda e, bo=bo, d2=d2, c=c, c0=c0, n=n: e.tensor_tensor(out=self.ymA[:, c * T + c0:c * T + c0 + n], in0=self.ps[bo][:, 0:n],
                                                                                  in1=d2, op=ALU.mult),
                  reads=[("ps", bo), ("d2", di)], writes=[("ymA", c, tq)])

        iters = [(bq, c) for bq in range(17) for c in range(4)]
        pend = None
        for it, (bq, c) in enumerate(iters):
            if it in (6, 18, 30, 42):
                self.wnext()
            st = attn_scores(bq, c)
            if pend is not None:
                attn_pv(pend)
            pend = st
        attn_pv(pend)
        if self.debug:
            dbg = self.nc.dram_tensor("dbg", [128, 8 * T], BF16, kind="ExternalOutput").ap()
            op = S.add("sp", lambda e: e.dma_start(out=dbg[:, 0:4 * T], in_=self.ymA[:, 0:4 * T]),
                       reads=[("ymA", c, t) for c in range(4) for t in range(5)], dma="dbgA")
            self.out_dmas.append(op)
            op = S.add("sp", lambda e: e.dma_start(out=dbg[:, 4 * T:8 * T], in_=self.ymC[:, 0:4 * T]),
                       reads=[("ymC", c, t) for c in range(4) for t in range(5)], dma="dbgC")
            self.out_dmas.append(op)
            dbg2 = self.nc.dram_tensor("dbg2", [128, 2176 + 2048 + 1024 + 512], BF16, kind="ExternalOutput").ap()
            for (o_, n_, src, key) in ((0, 2176, self.Vt, ["V"]), (2176, 2048, self.EBT, ["EBT"]),
                                        (4224, 512, self.Pt[0], [("Pt", 0)]), (4736, 512, self.Pt[1], [("Pt", 1)]),
                                        (5248, 256, self.Pm[0], [("Pm", 0)]), (5504, 256, self.Pm[1], [("Pm", 1)])):
                op = S.add("sp", lambda e, o_=o_, n_=n_, src=src: e.dma_start(out=dbg2[:, o_:o_ + n_], in_=src[:, 0:n_]), reads=key, dma=("dbg2", o_))
                self.out_dmas.append(op)
            dbg3 = self.nc.dram_tensor("dbg3", [128, 256 + 12], F32, kind="ExternalOutput").ap()
            for (o_, n_, src, key) in ((0, 128, self.d2[0], [("d2", 0)]), (128, 128, self.d2[1], [("d2", 1)]),
                                        (256, 4, self.esink, ["esink"]), (260, 8, self.eb31, ["eb31"])):
                op = S.add("sp", lambda e, o_=o_, n_=n_, src=src: e.dma_start(out=dbg3[:, o_:o_ + n_], in_=src[:, 0:n_]), reads=key, dma=("dbg3", o_))
                self.out_dmas.append(op)
        def ymix_ap(kk, t):
            c0, n = TT[t]
            if kk < 4:
                return self.ymA[:, kk * T + c0:kk * T + c0 + n]
            return self.ymC[:, (kk - 4) * T + c0:(kk - 4) * T + c0 + n]

        ymkey = lambda kk, t: ("ymA", kk, t) if kk < 4 else ("ymC", kk - 4, t)
        self.fence([("q", c, t) for c in range(4) for t in range(5)] + [("kz", c, t) for c in range(4) for t in range(5)],
                   [("ybM", i, m) for i in range(2) for m in range(KC)])
        ybM = [self.ybufM, self.ybufM2]

        def wout_post(t):
            n = TT[t][1]
            yb = ybM[t % 2]
            self.post_tile(t, lambda m, n=n, yb=yb: yb[:, m * 512:m * 512 + n], lambda m, t=t: ("ybM", t % 2, m), 0, 3, pool_ms=(1, 3, 5, 7))

        for t in range(5):
            c0, n = TT[t]
            yb = ybM[t % 2]
            for m in range(KC):
                b = self.proj_tile(self.woutb[:, m * 1024:(m + 1) * 1024], ["woutb"], ymix_ap, ymkey, t)
                S.add("act", lambda e, b=b, m=m, n=n, yb=yb: e.activation(out=yb[:, m * 512:m * 512 + n], in_=self.ps[b][:, 0:n], func=AF.Copy),
                      reads=[("ps", b)], writes=[("ybM", t % 2, m)])
                if m == 3 and t > 0:
                    wout_post(t - 1)
        wout_post(4)
        self.fence(self._m0_keys(), self._ffn_keys())

    def mixer1(self):
        S = self.S
        V = self.vecs
        self.fence(self._ffn_keys(), self._m1_keys())
        S.add("sp", lambda e: e.dma_start(out=self.bandf, in_=self.bands_d), writes=["bandf"], dma="bands")
        for i in range(3):
            S.add("dve", lambda e, i=i: e.tensor_copy(out=self.bandb[:, i * 512:(i + 1) * 512], in_=self.bandf[:, i * 512:(i + 1) * 512]),
                  reads=["bandf"], writes=["bandb"])
        w, wk = self.wnext()
        gidx = 6 + 2

        def uT_ap(c, t):
            c0, n = TT[t]
            return self.uT[:, c * T + c0:c * T + c0 + n]

        def blk(tb):
            return (0, 16, 0) if tb == 0 else (16 + 128 * (tb - 1), 128, 1 + (tb - 1) // 4)

        self._ev = 0

        def zproj(tb):
            c0, n, t = blk(tb)
            zi = tb % 3
            bz = [0, 1]
            for g in range(4):
                for kk in range(2):
                    ch = g * 2 + kk
                    S.add("pe", lambda e, g=g, kk=kk, ch=ch, c0=c0, n=n, b=bz[g // 2]: e.matmul(
                        self.ps[b][0:n, (g % 2) * 256:(g % 2) * 256 + 256], lhsT=self.uT[:, ch * T + c0:ch * T + c0 + n],
                        rhs=w[:, ch * 256:ch * 256 + 256], start=(kk == 0), stop=(kk == 1)),
                        reads=wk + [("uT", ch, t)], writes=[("ps", bz[g // 2])])
            for hb in range(2):
                if self._ev % 2 == 0:
                    S.add("act", lambda e, hb=hb, n=n, zi=zi, b=bz[hb]: e.activation(out=self.zb[zi][0:n, hb * 512:(hb + 1) * 512], in_=self.ps[b][0:n, 0:512], func=AF.Copy),
                          reads=[("ps", bz[hb])], writes=[("z", zi)])
                else:
                    S.add("dve", lambda e, hb=hb, n=n, zi=zi, b=bz[hb]: e.tensor_copy(out=self.zb[zi][0:n, hb * 512:(hb + 1) * 512], in_=self.ps[b][0:n, 0:512]),
                          reads=[("ps", bz[hb])], writes=[("z", zi)])
                self._ev += 1

        def tokmix(tb):
            c0, n, t = blk(tb)
            zi = tb % 3
            zprev = (tb - 1) % 3
            yi = t % 2
            by = [2, 3] if tb % 2 == 0 else [4, 5]
            for m in range(KC):
                g = m // 2
                bo = g * 384
                b = by[m // 4]
                oc0 = (m % 4) * 128
                S.add("pe", lambda e, m=m, n=n, zi=zi, bo=bo, b=b, oc0=oc0, last=(tb == 0): e.matmul(
                    self.ps[b][:, oc0:oc0 + n], lhsT=self.zb[zi][0:n, m * 128:(m + 1) * 128], rhs=self.bandb[0:n, bo:bo + n],
                    start=True, stop=last),
                    reads=[("z", zi), "bandb"], writes=[("ps", b)])
                if tb == 1:
                    S.add("pe", lambda e, m=m, zprev=zprev, bo=bo, b=b, oc0=oc0: e.matmul(
                        self.ps[b][:, oc0:oc0 + 128], lhsT=self.zb[zprev][0:16, m * 128:(m + 1) * 128], rhs=self.bandb[0:16, bo + 256:bo + 384],
                        start=False, stop=True),
                        reads=[("z", zprev), "bandb"], writes=[("ps", b)])
                elif tb >= 2:
                    S.add("pe", lambda e, m=m, zprev=zprev, bo=bo, b=b, oc0=oc0: e.matmul(
                        self.ps[b][:, oc0:oc0 + 128], lhsT=self.zb[zprev][:, m * 128:(m + 1) * 128], rhs=self.bandb[:, bo + 128:bo + 256],
                        start=False, stop=True),
                        reads=[("z", zprev), "bandb"], writes=[("ps", b)])
            lc0 = 0 if tb == 0 else ((tb - 1) % 4) * 128
            for m in range(KC):
                b = by[m // 4]
                oc0 = (m % 4) * 128
                S.add("act", lambda e, m=m, n=n, b=b, oc0=oc0, lc0=lc0, yi=yi: e.activation(
                    out=self.ybufP[yi][:, m * 512 + lc0:m * 512 + lc0 + n], in_=self.ps[b][:, oc0:oc0 + n], func=AF.Copy,
                    scale=V[:, 96 + m:96 + m + 1]),
                    reads=[("ps", b), "vecs"], writes=[("ybP", yi, m)])

        def post(t):
            yi = t % 2
            tn = TT[t][1]
            self.post_tile(t, lambda m, yi=yi, tn=tn: self.ybufP[yi][:, m * 512:m * 512 + tn], lambda m, yi=yi: ("ybP", yi, m), 1, 3)

        def post_parts1(t, nsplit):
            yi = t % 2
            tn = TT[t][1]
            return self.post_parts(t, lambda m, yi=yi, tn=tn: self.ybufP[yi][:, m * 512:m * 512 + tn],
                                   lambda m, yi=yi: ("ybP", yi, m), 1, 3, nsplit=nsplit, pool_ms=(1, 3, 5, 7))

        self.prenorm_tile(0, gidx, lambda c: uT_ap(c, 0), lambda c: ("uT", c, 0))
        self.prenorm_tile(1, gidx, lambda c: uT_ap(c, 1), lambda c: ("uT", c, 1))
        self._nb_override = [[6, 7], 0]
        sched = {}
        for t in range(1, 5):
            tb0 = 4 * (t - 1) + 1
            pre = None
            if t + 1 < 5:
                pre = self.prenorm_parts(t + 1, gidx, lambda c, t=t: uT_ap(c, t + 1), lambda c, t=t: ("uT", c, t + 1), nsplit=4)
            if t == 1:
                post = [lambda: [f() for f in post_parts1(0, 1)]]
            else:
                post = post_parts1(t - 1, 4)
            lst = [[] for _ in range(4)]
            if pre is not None:
                lst[0].append(pre[0])
            if len(post) > 1:
                lst[0].append(post[0])
            if pre is not None:
                lst[0].append(pre[1])
            if len(post) > 1:
                lst[0].append(post[1])
            else:
                lst[0].append(post[0])
            if pre is not None:
                lst[1] += [pre[2], pre[3]]
                lst[2].append(pre[4])
            if len(post) > 1:
                lst[1].append(post[2])
                lst[2].append(post[3])
                lst[3].append(post[4])
            for k in range(4):
                sched[tb0 + k] = lst[k]
        zproj(0)
        for tb in range(17):
            if tb + 1 < 17:
                zproj(tb + 1)
            tokmix(tb)
            for f in sched.get(tb, []):
                f()
        for f in post_parts1(4, 1):
            f()
        self._nb_override = None
        self.fence(self._m1_keys(), self._ffn_keys())

    def build(self, stages):
        items = []

        def ffn_items(f):
            for p in range(2):
                for j in range(FC):
                    items.append((self.w1s[f, j], 2048, None, None))
                for mh in range(16):
                    items.append((self.w2s[f, mh], 1408, None, None))

        for st in stages:
            if st.startswith("ffn"):
                ffn_items(int(st[3]) * 2 + int(st[4]))
            elif st == "mixer0":
                for i in range(10):
                    items.append((self.wins[i], 2048, None, None))
                for i in range(4):
                    items.append((self.wouts[i], 2048, self.woutb[:, i * 2048:(i + 1) * 2048], ["woutb"]))
            elif st == "mixer1":
                items.append((self.pws, 2048, None, None))
        self.wplan(items)
        self.setup()
        last = stages[-1]
        for si, st in enumerate(stages):
            if st.startswith("ffn"):
                nxt = stages[si + 1] if si + 1 < len(stages) else None
                next_ffn = (int(nxt[3]), int(nxt[4])) if (nxt is not None and nxt.startswith("ffn")) else None
                self.ffn(int(st[3]), int(st[4]), final=(st == last), next_ffn=next_ffn)
            elif st == "mixer0":
                self.mixer0()
            elif st == "mixer1":
                self.mixer1()
        S = self.S
        if not last.startswith("ffn"):
            for t in range(1, 5):
                c0, n = TT[t]
                oc0 = c0 - NMETA
                op = S.add("sp", lambda e, c0=c0, n=n, oc0=oc0: e.dma_start(
                    out=self.outT.rearrange("(c p) t -> p c t", p=128)[:, :, oc0:oc0 + n],
                    in_=self.hT[:, :].rearrange("p (c t) -> p c t", c=KC)[:, :, c0:c0 + n]),
                    reads=[("h", m, t) for m in range(KC)], dma=("out", t))
                self.out_dmas.append(op)
        S.finalize(force_signal=self.out_dmas)
        S.emit(self.nc, final_waits=[("sp", p) for p in self.out_dmas])


ALL_STAGES = ["ffn00", "mixer0", "ffn01", "ffn10", "mixer1", "ffn11"]


def build_nc(stages=None, same_engine_sync=True, debug=False):
    stages = stages or ALL_STAGES
    nc = bass.Bass("TRN2", target_bir_lowering=False)
    with contextlib.ExitStack() as es:
        b = Builder(nc, es, same_engine_sync=same_engine_sync, debug=debug)
        b.build(stages)
    return nc


def run(inputs, stages=None, trace=False, n_cores=8, debug=False):
    common = prep_common(inputs)
    x = np.asarray(inputs["x"], dtype=np.float32)
    in_maps = []
    for i in range(n_cores):
        m = dict(common)
        m["xT"] = np.ascontiguousarray(x[i].T)
        in_maps.append(m)
    nc = build_nc(stages, debug=debug)
    res = run_bass_kernel_spmd(nc, in_maps, core_ids=list(range(n_cores)), trace=trace)
    out = np.stack([np.ascontiguousarray(r["outT"].T) for r in res.results], axis=0)
    return out.astype(np.float32), res


def kernel(**inputs):
    out, _ = run(inputs)
    return out
```

```python
import contextlib
import numpy as np
import concourse.bass as bass
import concourse.mybir as mybir
from concourse.bass_utils import run_bass_kernel_spmd

F32 = mybir.dt.float32
BF16 = mybir.dt.bfloat16
AF = mybir.ActivationFunctionType
ALU = mybir.AluOpType

D = 1024
KC = 8
NMETA = 16
SEQ = 2048
T = NMETA + SEQ
DFF = 2816
FC = 22
EPS = 1e-6
TT = [(0, 16), (16, 512), (528, 512), (1040, 512), (1552, 512)]
PASSES = [[0, 1, 2], [3, 4]]
PLO = [0, 1040]
PW = 1040
POOL_SIZES = (2, 4, 8, 16)
SLAB = 2048
NS = 2
NB = 3


class _Op:
    __slots__ = ("eng", "fn", "reads", "writes", "dma", "deps", "raw", "signal", "count", "idx")


class Sched:
    def __init__(self, same_engine_sync=True):
        self.ops = []
        self.last_w = {}
        self.readers = {}
        self.same_engine_sync = same_engine_sync

    def add(self, eng, fn, reads=(), writes=(), dma=None):
        op = _Op()
        op.eng = eng
        op.fn = fn
        op.dma = dma
        op.reads = tuple(reads)
        op.writes = tuple(writes)
        op.signal = False
        op.count = None
        op.idx = len(self.ops)
        deps = set()
        for r in op.reads:
            w = self.last_w.get(r)
            if w is not None:
                deps.add(w)
        op.raw = set(deps)
        for w_ in op.writes:
            w = self.last_w.get(w_)
            if w is not None:
                deps.add(w)
            for rd in self.readers.get(w_, ()):
                deps.add(rd)
        deps.discard(op.idx)
        for r in op.reads:
            self.readers.setdefault(r, []).append(op.idx)
        for w_ in op.writes:
            self.last_w[w_] = op.idx
            self.readers[w_] = []
        op.deps = deps
        self.ops.append(op)
        return op

    @staticmethod
    def _key(p):
        return ("dma", p.dma) if p.dma is not None else ("eng", p.eng)

    def finalize(self, force_signal=()):
        ops = self.ops
        for op in ops:
            best = {}
            for d in op.deps:
                p = ops[d]
                key = self._key(p)
                if p.dma is None and p.eng == op.eng and op.dma is None:
                    if p.eng == "pe" or not self.same_engine_sync:
                        continue
                if key not in best or best[key] < d:
                    best[key] = d
            op.deps = set(best.values())
            for d in op.deps:
                ops[d].signal = True
        for p in force_signal:
            p.signal = True
        cnt = {}
        for op in ops:
            if op.signal:
                key = self._key(op)
                cnt[key] = cnt.get(key, 0) + (16 if op.dma is not None else 1)
                op.count = cnt[key]
        self.sem_keys = sorted(cnt.keys(), key=str)

    def emit(self, nc, final_waits=()):
        ops = self.ops
        with contextlib.ExitStack() as es:
            sems = {}
            for i, k in enumerate(self.sem_keys):
                sems[k] = es.enter_context(nc.semaphore("s%d" % i))
            block = es.enter_context(nc.Block())

            def run(engname):
                def body(e):
                    waited = {}
                    for op in ops:
                        if op.eng != engname:
                            continue
                        for d in sorted(op.deps):
                            p = ops[d]
                            key = self._key(p)
                            if waited.get(key, 0) >= p.count:
                                continue
                            waited[key] = p.count
                            e.wait_ge(sems[key], p.count)
                        ins = op.fn(e)
                        if op.signal:
                            ins.then_inc(sems[self._key(op)], 16 if op.dma is not None else 1)
                    for (en, p) in final_waits:
                        if en == engname:
                            e.wait_ge(sems[self._key(p)], p.count)
                return body

            block.tensor(run("pe"))
            block.scalar(run("act"))
            block.vector(run("dve"))
            block.gpsimd(run("pool"))
            block.sync(run("sp"))


def _t5_bucket_np(dist):
    n = np.maximum(dist, 0)
    nf = np.maximum(n, 1).astype(np.float32)
    large = 16 + (np.log(nf / np.float32(16)) / np.float32(np.log(128 / 16)) * np.float32(16)).astype(np.int32)
    large = np.minimum(large, 31)
    return np.where(n < 16, n, large)


def _slabs_w1(w1):
    g = w1[:, :DFF].reshape(KC, 128, FC, 128)
    u = w1[:, DFF:].reshape(KC, 128, FC, 128)
    s = np.concatenate([g, u], axis=0)
    return np.ascontiguousarray(s.transpose(2, 1, 0, 3)).reshape(FC, 128, 2048)


def _slabs_w2(w2):
    s = w2.reshape(2, 11, 128, KC, 128)
    return np.ascontiguousarray(s.transpose(3, 0, 2, 1, 4)).reshape(16, 128, 1408)


def _colslab(w):
    return np.ascontiguousarray(w.reshape(KC, 128, 128).transpose(1, 0, 2)).reshape(128, 1024)


def prep_common(inp):
    f = lambda a: np.ascontiguousarray(np.asarray(a, dtype=np.float32))
    norm_g = f(inp["norm_g"])
    w1 = f(inp["ffn_w1"])
    w2 = f(inp["ffn_w2"])
    w_in = f(inp["mix_w_in"])[0]
    w_out = f(inp["mix_w_out"])[0]
    rel_bias = f(inp["rel_bias"])
    sinks = f(inp["attn_sinks"])[0]
    conv_w = f(inp["conv_w"])[0]
    pool_w = f(inp["pool_w"])[0]
    pool_scale = f(inp["pool_scale"])[0]
    meta = f(inp["meta_tokens"])
    out = {}
    out["metaT"] = np.ascontiguousarray(meta.T)
    gT = norm_g.reshape(12, KC, 128).transpose(2, 0, 1).reshape(128, 96)
    psc = pool_scale.reshape(KC, 128).T
    cw = conv_w.reshape(3, 4, 128).transpose(2, 1, 0).reshape(128, 12)
    sinkP = np.repeat(sinks.reshape(4, 2), 64, axis=1).T
    b31 = np.broadcast_to(rel_bias[31][None, :], (128, 8))
    out["vecs"] = np.ascontiguousarray(np.concatenate([gT, psc, cw, sinkP, b31], axis=1))
    out["w1s"] = np.stack([_slabs_w1(w1[l, s]) for l in range(2) for s in range(2)])
    out["w2s"] = np.stack([_slabs_w2(w2[l, s]) for l in range(2) for s in range(2)])
    q0, k0, v0, b0, c0, x0 = 0, 512, 640, 768, 1280, 1792
    z64 = np.zeros((D, 64), np.float32)
    slabs = []
    for ci in range(4):
        slabs.append(_colslab(w_in[:, c0 + ci * 128:c0 + (ci + 1) * 128]))
        slabs.append(_colslab(w_in[:, x0 + ci * 128:x0 + (ci + 1) * 128]))
        slabs.append(_colslab(w_in[:, b0 + ci * 128:b0 + (ci + 1) * 128]))
    for c in range(4):
        slabs.append(_colslab(w_in[:, q0 + c * 128:q0 + (c + 1) * 128]))
    for kv in range(2):
        kk_ = w_in[:, k0 + kv * 64:k0 + (kv + 1) * 64]
        slabs.append(_colslab(np.concatenate([kk_, kk_], axis=1)))
    slabs.append(_colslab(w_in[:, v0:v0 + 128]))
    slabs.append(np.zeros((128, 1024), np.float32))
    sl = np.stack(slabs)
    out["wins"] = np.ascontiguousarray(sl.reshape(10, 2, 128, 1024).transpose(0, 2, 1, 3)).reshape(10, 128, 2048)
    wo = np.stack([_colslab(w_out[:, m * 128:(m + 1) * 128]) for m in range(8)])
    out["wouts"] = np.ascontiguousarray(wo.reshape(4, 2, 128, 1024).transpose(0, 2, 1, 3)).reshape(4, 128, 2048)
    pw = pool_w.reshape(4, 2, 128, 2, 128)
    out["pws"] = np.ascontiguousarray(pw.transpose(2, 0, 1, 3, 4)).reshape(128, 2048)
    kk_ = np.arange(128)[:, None]
    nn_ = np.arange(128)[None, :]
    bands = np.zeros((128, 4, 3, 128), np.float32)
    for gi, wsz in enumerate(POOL_SIZES):
        dcur = nn_ - kk_
        bands[:, gi, 0] = np.where((dcur >= 0) & (dcur < wsz), 1.0 / wsz, 0.0) - (dcur == 0)
        bands[:, gi, 1] = np.where(128 + dcur < wsz, 1.0 / wsz, 0.0)
        bands[:16, gi, 2] = np.where(16 + nn_ - kk_[:16] < wsz, 1.0 / wsz, 0.0)
    out["bands"] = np.ascontiguousarray(bands.reshape(128, 1536))
    s_ = np.arange(128)[:, None]
    r_ = np.arange(128)[None, :]
    bk_cur = _t5_bucket_np(r_ - s_)
    bk_prev = _t5_bucket_np(128 + r_ - s_)
    bt = np.stack([rel_bias[bk_cur], rel_bias[bk_prev]], axis=0)
    out["btab"] = np.ascontiguousarray(bt.transpose(1, 3, 0, 2)).reshape(128, 8 * 256)
    mcur = (r_ - s_ >= 0).astype(np.float32)
    mprev = (s_ > r_).astype(np.float32)
    out["mtab"] = np.ascontiguousarray(np.stack([mcur, mprev], axis=1)).reshape(128, 256)
    m_ = np.arange(16)[:, None]
    bk_m1 = _t5_bucket_np(16 + r_ - m_)
    rq = np.arange(16)[None, :]
    bk_m0 = _t5_bucket_np(rq - m_)
    bm = np.concatenate([rel_bias[bk_m1].transpose(0, 2, 1), rel_bias[bk_m0].transpose(0, 2, 1)], axis=2)
    mm0 = np.broadcast_to((rq - m_ >= 0).astype(np.float32)[:, None, :], (16, 8, 16))
    out["bmeta"] = np.ascontiguousarray(np.concatenate([bm, mm0], axis=2)).reshape(16, 8 * 160)
    return out


class Builder:
    def __init__(self, nc, es, stop_after=None, debug=False, same_engine_sync=True):
        self.nc = nc
        self.S = Sched(same_engine_sync)
        self.stop_after = stop_after
        self.debug = debug
        S = self.S
        dt = nc.dram_tensor
        self.xT = dt("xT", [D, SEQ], F32, kind="ExternalInput").ap()
        self.metaT = dt("metaT", [D, NMETA], F32, kind="ExternalInput").ap()
        self.vecs_d = dt("vecs", [128, 128], F32, kind="ExternalInput").ap()
        self.w1s = dt("w1s", [4, FC, 128, 2048], F32, kind="ExternalInput").ap()
        self.w2s = dt("w2s", [4, 16, 128, 1408], F32, kind="ExternalInput").ap()
        self.wins = dt("wins", [10, 128, 2048], F32, kind="ExternalInput").ap()
        self.wouts = dt("wouts", [4, 128, 2048], F32, kind="ExternalInput").ap()
        self.pws = dt("pws", [128, 2048], F32, kind="ExternalInput").ap()
        self.bands_d = dt("bands", [128, 1536], F32, kind="ExternalInput").ap()
        self.btab_d = dt("btab", [128, 2048], F32, kind="ExternalInput").ap()
        self.mtab_d = dt("mtab", [128, 256], F32, kind="ExternalInput").ap()
        self.bmeta_d = dt("bmeta", [16, 1280], F32, kind="ExternalInput").ap()
        self.outT = dt("outT", [D, SEQ], F32, kind="ExternalOutput").ap()

        sb = lambda name, n, dtp: es.enter_context(nc.sbuf_tensor(name, [128, n], dtp))
        self.hT = sb("hT", KC * T, F32)
        self.vecs = sb("vecs_sb", 128, F32)
        self.gc = sb("gc", 96, F32)
        self.ones = sb("ones", 128, BF16)
        self.esink = sb("esink", 4, F32)
        self.eb31 = sb("eb31", 8, F32)
        self.stage = [sb("stage%d" % i, SLAB, F32) for i in range(NS)]
        self.wb = [sb("wb%d" % i, SLAB, BF16) for i in range(NB)]
        self.rstd = [sb("rstd%d" % i, 512, F32) for i in range(2)]
        self.sq = [sb("sq%d" % i, 512, BF16) for i in range(2)]
        self.tmp = [sb("tmp%d" % i, 512, F32) for i in range(2)]
        ARENA_F32 = 25408
        self.arena = sb("arena", ARENA_F32, F32)
        self.junk = sb("junk", 8, F32)
        self.ones1 = sb("ones1", 64, BF16)
        self.ps = [es.enter_context(nc.psum_tensor("ps%d" % i, [128, 512], F32)) for i in range(8)]
        self.bank = 0
        self.cnt = {"rstd": 0, "sq": 0, "tmp": 0, "Pt": 0, "Pm": 0, "d2": 0}
        a = self.arena
        self.xn = a[:, 0:4160].bitcast(BF16)
        self.ybuf = a[:, 4160:4160 + 8320]
        self.act = a[:, 12480:12480 + 11440].bitcast(BF16)
        o = 0
        o_u = o
        self.uT = a[:, o:o + 8256].bitcast(BF16); o += 8256
        o_q = o
        self.qT = a[:, o:o + 4128].bitcast(BF16); o += 4128
        self.kz = a[:, o:o + 4128].bitcast(BF16); o += 4128
        self.Vt = a[:, o:o + 1088].bitcast(BF16); o += 1088
        o_c = o
        self.ucv = a[:, o:o + 2080]; o += 2080
        self.ymC = a[:, o:o + 4128].bitcast(BF16); o += 4128
        self.EBT = a[:, o:o + 1024].bitcast(BF16); o += 1024
        self.EBm1 = a[:, o:o + 512].bitcast(BF16); o += 512
        self.EBm0 = a[:, o:o + 64].bitcast(BF16); o += 64
        assert o <= ARENA_F32, o
        self.ymA = a[:, o_u:o_u + 4128].bitcast(BF16)
        self.woutb = a[:, o_u + 4128:o_u + 4128 + 4096].bitcast(BF16)
        self.ybufM = a[:, o_q + 4128:o_q + 4128 + 4096]
        self.ybufM2 = a[:, o_q:o_q + 4096]
        self.btab = a[:, o_q:o_q + 2048]
        self.mtab = a[:, o_q + 2048:o_q + 2304]
        self.bmeta = a[:, o_q + 2304:o_q + 2304 + 1280]
        oc = o_c
        self.Pt = [a[:, oc + i * 256:oc + (i + 1) * 256].bitcast(BF16) for i in range(2)]; oc += 512
        self.Pm = [a[:, oc + i * 128:oc + (i + 1) * 128].bitcast(BF16) for i in range(2)]; oc += 256
        self.d2 = [a[:, oc + i * 128:oc + (i + 1) * 128] for i in range(2)]; oc += 256
        assert oc <= o_c + 2080
        o = 8256
        self.zb = [a[:, o + i * 512:o + (i + 1) * 512].bitcast(BF16) for i in range(3)]; o += 1536
        self.ybufP = [a[:, o + i * 4096:o + (i + 1) * 4096] for i in range(2)]; o += 8192
        self.bandf = a[:, o:o + 1536]; o += 1536
        self.bandb = a[:, o:o + 768].bitcast(BF16); o += 768
        assert o <= ARENA_F32, o
        self.out_dmas = []
        self.dbg_out = []

    def nb(self):
        ov = getattr(self, "_nb_override", None)
        if ov is not None:
            b = ov[0][ov[1] % len(ov[0])]
            ov[1] += 1
            return b
        b = self.bank
        self.bank = (self.bank + 1) % 8
        return b

    def rot(self, name):
        i = self.cnt[name]
        self.cnt[name] = (i + 1) % 2
        return i

    def hsl(self, c, c0, n):
        return self.hT[:, c * T + c0:c * T + c0 + n]

    def fence(self, reads, writes):
        self.S.add("pool", lambda e: e.memset(self.junk[0:1, 0:1], 0.0), reads=list(reads), writes=list(reads) + list(writes) + ["junk"])

    def wplan(self, items):
        self.witems = items
        self.w_dma = 0
        self.w_cast = 0
        self.w_use = 0

    def _pump(self, i):
        S = self.S
        n_items = len(self.witems)
        while True:
            prog = False
            if self.w_dma < n_items and self.w_dma - NS < self.w_cast:
                k = self.w_dma
                ap, n, dest, dkeys = self.witems[k]
                sl = k % NS
                S.add("sp", lambda e, sl=sl, n=n, ap=ap: e.dma_start(out=self.stage[sl][:, 0:n], in_=ap),
                      writes=[("stage", sl)], dma=("stage", sl))
                self.w_dma += 1
                prog = True
            if (self.w_cast < n_items and self.w_cast < self.w_dma and self.w_cast <= i + NB - 1
                    and (self.witems[self.w_cast][2] is None or self.w_cast <= i)):
                k = self.w_cast
                ap, n, dest, dkeys = self.witems[k]
                sl = k % NS
                if dest is None:
                    dst = self.wb[k % NB][:, 0:n]
                    wk = [("wb", k % NB)]
                else:
                    dst = dest
                    wk = dkeys
                for p0 in range(0, n, 512):
                    p1 = min(n, p0 + 512)
                    if p0 == 0:
                        S.add("pool", lambda e, dst=dst, sl=sl, p0=p0, p1=p1: e.tensor_copy(out=dst[:, p0:p1], in_=self.stage[sl][:, p0:p1]),
                              reads=[("stage", sl)], writes=wk)
                    else:
                        S.add("act", lambda e, dst=dst, sl=sl, p0=p0, p1=p1: e.activation(out=dst[:, p0:p1], in_=self.stage[sl][:, p0:p1], func=AF.Copy),
                              reads=[("stage", sl)], writes=wk)
                self.w_cast += 1
                prog = True
            if not prog:
                break

    def wnext(self):
        i = self.w_use
        self.w_use += 1
        self._pump(i)
        ap, n, dest, dkeys = self.witems[i]
        if dest is None:
            return self.wb[i % NB][:, 0:n], [("wb", i % NB)]
        return dest, dkeys

    def setup(self):
        S = self.S
        S.add("sp", lambda e: e.dma_start(out=self.vecs[:, :], in_=self.vecs_d), writes=["vecs"], dma="vecs")
        S.add("sp", lambda e: e.dma_start(out=self.hT[:, :].rearrange("p (c t) -> p c t", c=KC)[:, :, 0:NMETA],
                                           in_=self.metaT.rearrange("(c p) t -> p c t", p=128)),
              writes=[("h", c, 0) for c in range(KC)], dma="meta")
        for t in range(1, 5):
            c0, n = TT[t]
            S.add("sp", lambda e, c0=c0, n=n: e.dma_start(out=self.hT[:, :].rearrange("p (c t) -> p c t", c=KC)[:, :, c0:c0 + n],
                                                           in_=self.xT.rearrange("(c p) t -> p c t", p=128)[:, :, c0 - NMETA:c0 - NMETA + n]),
                  writes=[("h", c, t) for c in range(KC)], dma=("x", t))
        S.add("pool", lambda e: e.memset(self.ones[:, :], 1.0 / 1024), writes=["ones"])
        S.add("pool", lambda e: e.memset(self.ones1[:, :], 1.0), writes=["ones1"])
        for l in range(2):
            for i, coef in ((1, 0.5), (3, 1.0), (5, 0.5)):
                o = (l * 6 + i) * 8
                S.add("dve", lambda e, o=o, coef=coef: e.tensor_scalar(out=self.gc[:, o:o + 8], in0=self.vecs[:, o:o + 8],
                                                                       scalar1=coef, scalar2=None, op0=ALU.mult),
                      reads=["vecs"], writes=[("gc", l, i)])

    def rms_stats(self, src_fn, src_keys_fn, t, nchunks=8):
        S = self.S
        c0, n = TT[t]
        b = self.nb()
        for c in range(nchunks):
            si = self.rot("sq")
            S.add("act", lambda e, c=c, si=si: e.activation(out=self.sq[si][:, 0:n], in_=src_fn(c), func=AF.Square),
                  reads=src_keys_fn(c), writes=[("sq", si)])
            S.add("pe", lambda e, c=c, si=si, b=b: e.matmul(self.ps[b][:, 0:n], lhsT=self.ones[:, :], rhs=self.sq[si][:, 0:n],
                                                            start=(c == 0), stop=(c == nchunks - 1)),
                  reads=[("sq", si), "ones"], writes=[("ps", b)])
        ri = self.rot("rstd")
        r = self.rstd[ri]
        S.add("dve", lambda e, b=b: e.tensor_scalar(out=r[:, 0:n], in0=self.ps[b][:, 0:n], scalar1=EPS, scalar2=None, op0=ALU.add),
              reads=[("ps", b)], writes=[("rstd", ri)])
        S.add("dve", lambda e: e.reciprocal(out=r[:, 0:n], in_=r[:, 0:n]), reads=[("rstd", ri)], writes=[("rstd", ri)])
        S.add("act", lambda e: e.activation(out=r[:, 0:n], in_=r[:, 0:n], func=AF.Sqrt), reads=[("rstd", ri)], writes=[("rstd", ri)])
        return ri

    def prenorm_tile(self, t, gidx, dst_fn, dst_key_fn):
        S = self.S
        c0, n = TT[t]
        ri = self.rms_stats(lambda c: self.hsl(c, c0, n), lambda c: [("h", c, t)], t)
        for c in range(KC):
            S.add("dve", lambda e, c=c: e.scalar_tensor_tensor(out=dst_fn(c), in0=self.hsl(c, c0, n),
                                                               scalar=self.vecs[:, gidx * 8 + c:gidx * 8 + c + 1],
                                                               in1=self.rstd[ri][:, 0:n], op0=ALU.mult, op1=ALU.mult),
                  reads=[("h", c, t), ("rstd", ri), "vecs"], writes=[dst_key_fn(c)])

    def prenorm_parts(self, t, gidx, dst_fn, dst_key_fn, nsplit=4):
        S = self.S
        c0, n = TT[t]
        st = {}

        def stats():
            st["ri"] = self.rms_stats(lambda c: self.hsl(c, c0, n), lambda c: [("h", c, t)], t)

        def apply(cs):
            ri = st["ri"]
            for c in cs:
                S.add("dve", lambda e, c=c: e.scalar_tensor_tensor(out=dst_fn(c), in0=self.hsl(c, c0, n),
                                                                   scalar=self.vecs[:, gidx * 8 + c:gidx * 8 + c + 1],
                                                                   in1=self.rstd[ri][:, 0:n], op0=ALU.mult, op1=ALU.mult),
                      reads=[("h", c, t), ("rstd", ri), "vecs"], writes=[dst_key_fn(c)])

        per = KC // nsplit
        return [stats] + [(lambda cs=list(range(k * per, (k + 1) * per)): apply(cs)) for k in range(nsplit)]

    def post_parts(self, t, ysrc_fn, ykey_fn, l, i, final=False, nsplit=2, add_eng="dve", pool_ms=()):
        S = self.S
        c0, n = TT[t]
        go = (l * 6 + i) * 8
        st = {}

        def stats():
            st["ri"] = self.rms_stats(ysrc_fn, lambda m: [ykey_fn(m)], t)

        def apply(ms, last):
            ri = st["ri"]
            for m in ms:
                if m in pool_ms:
                    S.add("dve", lambda e, m=m: e.scalar_tensor_tensor(out=ysrc_fn(m), in0=ysrc_fn(m),
                                                                        scalar=self.gc[:, go + m:go + m + 1],
                                                                        in1=self.rstd[ri][:, 0:n], op0=ALU.mult, op1=ALU.mult),
                          reads=[ykey_fn(m), ("rstd", ri), ("gc", l, i)], writes=[ykey_fn(m)])
                    S.add("pool", lambda e, m=m: e.tensor_tensor(out=self.hsl(m, c0, n), in0=self.hsl(m, c0, n),
                                                                 in1=ysrc_fn(m), op=ALU.add),
                          reads=[ykey_fn(m), ("h", m, t)], writes=[("h", m, t)])
                    continue
                ti = self.rot("tmp")
                S.add("dve", lambda e, m=m, ti=ti: e.scalar_tensor_tensor(out=self.tmp[ti][:, 0:n], in0=ysrc_fn(m),
                                                                          scalar=self.gc[:, go + m:go + m + 1],
                                                                          in1=self.rstd[ri][:, 0:n], op0=ALU.mult, op1=ALU.mult),
                      reads=[ykey_fn(m), ("rstd", ri), ("gc", l, i)], writes=[("tmp", ti)])
                S.add(add_eng, lambda e, m=m, ti=ti: e.tensor_tensor(out=self.hsl(m, c0, n), in0=self.hsl(m, c0, n),
                                                                     in1=self.tmp[ti][:, 0:n], op=ALU.add),
                      reads=[("tmp", ti), ("h", m, t)], writes=[("h", m, t)])
            if last and final and t > 0:
                oc0 = c0 - NMETA
                op = S.add("sp", lambda e: e.dma_start(out=self.outT.rearrange("(c p) t -> p c t", p=128)[:, :, oc0:oc0 + n],
                                                       in_=self.hT[:, :].rearrange("p (c t) -> p c t", c=KC)[:, :, c0:c0 + n]),
                           reads=[("h", m, t) for m in range(KC)], dma=("out", t))
                self.out_dmas.append(op)

        per = KC // nsplit
        parts = [stats]
        for k in range(nsplit):
            ms = list(range(k * per, (k + 1) * per))
            parts.append(lambda ms=ms, last=(k == nsplit - 1): apply(ms, last))
        return parts

    def post_tile(self, t, ysrc_fn, ykey_fn, l, i, final=False, add_eng="dve", pool_ms=()):
        for f in self.post_parts(t, ysrc_fn, ykey_fn, l, i, final=final, nsplit=1, add_eng=add_eng, pool_ms=pool_ms):
            f()

    def ffn(self, l, s, final=False, next_ffn=None):
        S = self.S
        f = l * 2 + s
        gpre = l * 6 + (0 if s == 0 else 4)
        ipost = 1 if s == 0 else 5

        def xn_ap(p, c, t):
            c0, n = TT[t]
            off = c * PW + (c0 - PLO[p])
            return self.xn[:, off:off + n]

        def act_ap(p, j, t):
            c0, n = TT[t]
            off = j * PW + (c0 - PLO[p])
            return self.act[:, off:off + n]

        def y_ap(p, m, t):
            c0, n = TT[t]
            off = m * PW + (c0 - PLO[p])
            return self.ybuf[:, off:off + n]

        PASSES_ = [[t for t in ps_ if not (final and t == 0)] for ps_ in PASSES]

        def prenorm(p):
            for t in PASSES_[p]:
                self.prenorm_tile(t, gpre, lambda c, t=t: xn_ap(p, c, t), lambda c, t=t: ("xn", c, t))

        def w1phase(p, hooks=None):
            for j in range(FC):
                w, wk = self.wnext()
                for t in PASSES_[p]:
                    c0, n = TT[t]
                    ba = self.nb()
                    bb = self.nb()
                    for half, bk in ((0, ba), (1, bb)):
                        for kk in range(KC):
                            S.add("pe", lambda e, kk=kk, bk=bk, half=half, t=t, n=n, w=w: e.matmul(
                                self.ps[bk][:, 0:n], lhsT=w[:, (half * 8 + kk) * 128:(half * 8 + kk + 1) * 128],
                                rhs=xn_ap(p, kk, t), start=(kk == 0), stop=(kk == KC - 1)),
                                reads=wk + [("xn", kk, t)], writes=[("ps", bk)])
                    ti = self.rot("tmp")
                    S.add("act", lambda e, ba=ba, ti=ti, n=n: e.activation(out=self.tmp[ti][:, 0:n], in_=self.ps[ba][:, 0:n], func=AF.Silu),
                          reads=[("ps", ba)], writes=[("tmp", ti)])
                    S.add("dve", lambda e, bb=bb, ti=ti, n=n, j=j, t=t: e.tensor_tensor(out=act_ap(p, j, t), in0=self.tmp[ti][:, 0:n],
                                                                                         in1=self.ps[bb][:, 0:n], op=ALU.mult),
                          reads=[("tmp", ti), ("ps", bb)], writes=[("act", j, t)])
                if hooks and j in hooks:
                    hooks[j]()

        def w2phase(p):
            for m in range(KC):
                banks = {t: self.nb() for t in PASSES_[p]}
                for half in range(2):
                    w, wk = self.wnext()
                    for t in PASSES_[p]:
                        c0, n = TT[t]
                        for k2 in range(11):
                            kk = half * 11 + k2
                            S.add("pe", lambda e, kk=kk, k2=k2, t=t, n=n, w=w, bk=banks[t]: e.matmul(
                                self.ps[bk][:, 0:n], lhsT=w[:, k2 * 128:(k2 + 1) * 128], rhs=act_ap(p, kk, t),
                                start=(kk == 0), stop=(kk == FC - 1)),
                                reads=wk + [("act", kk, t)], writes=[("ps", banks[t])])
                for t in PASSES_[p]:
                    c0, n = TT[t]
                    S.add("act", lambda e, m=m, t=t, n=n, bk=banks[t]: e.activation(out=y_ap(p, m, t), in_=self.ps[bk][:, 0:n], func=AF.Copy),
                          reads=[("ps", banks[t])], writes=[("y", m, t)])

        def post_t(p, t):
            self.post_tile(t, lambda m, t=t: y_ap(p, m, t), lambda m, t=t: ("y", m, t), l, ipost, final=final)

        def post_hooks(p, tiles, j0):
            hk = {}
            j = j0
            for t in tiles:
                if TT[t][1] <= 16:
                    hk[j] = (lambda t=t: post_t(p, t))
                    j += 1
                    continue
                parts = self.post_parts(t, lambda m, t=t: y_ap(p, m, t), lambda m, t=t: ("y", m, t), l, ipost, final=final, nsplit=4)
                for f in parts:
                    hk[j] = f
                    j += 1
            return hk

        pend = getattr(self, "_pending_post", None)
        self._pending_post = None
        if not getattr(self, "_prenorm0_done", False):
            prenorm(0)
        self._prenorm0_done = False
        pre1 = [self.prenorm_parts(t, gpre, lambda c, t=t: xn_ap(1, c, t), lambda c, t=t: ("xn", c, t), nsplit=1)
                for t in PASSES_[1]]
        hooksA = dict(pend or {})
        assert (FC - 4) not in hooksA and (FC - 2) not in hooksA
        hooksA[FC - 4] = pre1[0][0]
        hooksA[FC - 2] = pre1[1][0]
        w1phase(0, hooks=hooksA)
        for p_ in pre1:
            p_[1]()
        w2phase(0)
        hooksB = post_hooks(0, PASSES_[0], 1)
        npre = None
        if next_ffn is not None:
            nl, ns_ = next_ffn
            ngpre = nl * 6 + (0 if ns_ == 0 else 4)
            npre = [self.prenorm_parts(t, ngpre, lambda c, t=t: xn_ap(0, c, t), lambda c, t=t: ("xn", c, t), nsplit=1)
                    for t in (1, 2)]
            assert (FC - 5) not in hooksB and (FC - 3) not in hooksB and max(hooksB) < FC - 5
            hooksB[FC - 5] = npre[0][0]
            hooksB[FC - 3] = npre[1][0]
        w1phase(1, hooks=hooksB)
        if next_ffn is not None:
            for p_ in npre:
                p_[1]()
            self.prenorm_tile(0, ngpre, lambda c: xn_ap(0, c, 0), lambda c: ("xn", c, 0))
            self._prenorm0_done = True
        w2phase(1)
        if next_ffn is not None:
            self._pending_post = post_hooks(1, [3, 4], 1)
        else:
            post_t(1, 3)
            post_t(1, 4)

    def _ffn_keys(self):
        return ([("xn", c, t) for c in range(KC) for t in range(5)] + [("y", m, t) for m in range(KC) for t in range(5)]
                + [("act", j, t) for j in range(FC) for t in range(5)])

    def _m0_keys(self):
        return (["btab", "mtab", "bmeta", "EBT", "EBm1", "EBm0", "V", "ucvpad", "woutb"]
                + [("uT", c, t) for c in range(KC) for t in range(5)]
                + [("q", c, t) for c in range(4) for t in range(5)] + [("kz", c, t) for c in range(4) for t in range(5)]
                + [("ucv", t) for t in range(5)] + [("ymC", c, t) for c in range(4) for t in range(5)]
                + [("ymA", c, t) for c in range(4) for t in range(5)] + [("ybM", i, m) for i in range(2) for m in range(KC)]
                + [("Pt", i) for i in range(2)] + [("Pm", i) for i in range(2)] + [("d2", i) for i in range(2)])

    def _m1_keys(self):
        return (["bandf", "bandb"] + [("uT", c, t) for c in range(KC) for t in range(5)] + [("z", i) for i in range(3)]
                + [("ybP", i, m) for i in range(2) for m in range(KC)])

    def half_slab(self):
        if getattr(self, "_hs", None) is None:
            w, wk = self.wnext()
            self._hs = (w, wk)
            return w[:, 0:1024], wk
        w, wk = self._hs
        self._hs = None
        return w[:, 1024:2048], wk

    def proj_tile(self, w, wk, src_fn, src_key_fn, t):
        S = self.S
        c0, n = TT[t]
        b = self.nb()
        for kk in range(KC):
            S.add("pe", lambda e, kk=kk, b=b, n=n, t=t: e.matmul(self.ps[b][:, 0:n], lhsT=w[:, kk * 128:(kk + 1) * 128],
                                                                  rhs=src_fn(kk, t), start=(kk == 0), stop=(kk == KC - 1)),
                  reads=wk + [src_key_fn(kk, t)], writes=[("ps", b)])
        return b

    def mixer0(self):
        S = self.S
        V = self.vecs
        self.fence(self._ffn_keys(), self._m0_keys())
        S.add("pool", lambda e: e.memset(self.ucv[:, 0:16], 0.0), writes=["ucvpad"])
        gidx = 2

        def uT_ap(c, t):
            c0, n = TT[t]
            return self.uT[:, c * T + c0:c * T + c0 + n]

        for t in range(5):
            self.prenorm_tile(t, gidx, lambda c, t=t: uT_ap(c, t), lambda c, t=t: ("uT", c, t))
        ukey = lambda kk, t: ("uT", kk, t)
        S.add("sp", lambda e: e.dma_start(out=self.btab, in_=self.btab_d), writes=["btab"], dma="btab")
        S.add("sp", lambda e: e.dma_start(out=self.mtab, in_=self.mtab_d), writes=["mtab"], dma="mtab")
        S.add("sp", lambda e: e.dma_start(out=self.bmeta[0:16, :], in_=self.bmeta_d), writes=["bmeta"], dma="bmeta")
        S.add("act", lambda e: e.activation(out=self.esink[:, :], in_=V[:, 116:120], func=AF.Exp), reads=["vecs"], writes=["esink"])
        S.add("act", lambda e: e.activation(out=self.eb31[:, :], in_=V[:, 120:128], func=AF.Exp), reads=["vecs"], writes=["eb31"])
        for h_ in range(8):
            bt_h = self.btab[:, h_ * 256:(h_ + 1) * 256]
            S.add("act", lambda e, bt_h=bt_h: e.activation(out=bt_h, in_=bt_h, func=AF.Exp), reads=["btab"], writes=["btab"])
            S.add("dve", lambda e, bt_h=bt_h, h_=h_: e.tensor_tensor(out=self.EBT[:, h_ * 256:(h_ + 1) * 256], in0=bt_h, in1=self.mtab, op=ALU.mult),
                  reads=["btab", "mtab"], writes=["EBT"])
            bm_h = self.bmeta[0:16, h_ * 160:(h_ + 1) * 160]
            S.add("act", lambda e, bm_h=bm_h: e.activation(out=bm_h[:, 0:144], in_=bm_h[:, 0:144], func=AF.Exp), reads=["bmeta"], writes=["bmeta"])
            S.add("dve", lambda e, bm_h=bm_h, h_=h_: e.tensor_copy(out=self.EBm1[0:16, h_ * 128:(h_ + 1) * 128], in_=bm_h[:, 0:128]),
                  reads=["bmeta"], writes=["EBm1"])
            S.add("dve", lambda e, bm_h=bm_h, h_=h_: e.tensor_tensor(out=self.EBm0[0:16, h_ * 16:(h_ + 1) * 16], in0=bm_h[:, 128:144],
                                                                       in1=bm_h[:, 144:160], op=ALU.mult),
                  reads=["bmeta"], writes=["EBm0"])
        for ci in range(4):
            w, wk = self.half_slab()
            for t in range(5):
                c0, n = TT[t]
                b = self.proj_tile(w, wk, uT_ap, ukey, t)
                S.add("act", lambda e, b=b, c0=c0, n=n: e.activation(out=self.ucv[:, 16 + c0:16 + c0 + n], in_=self.ps[b][:, 0:n], func=AF.Copy),
                      reads=[("ps", b)], writes=[("ucv", t)])
            w, wk = self.half_slab()
            for t in range(5):
                c0, n = TT[t]
                b = self.proj_tile(w, wk, uT_ap, ukey, t)
                S.add("dve", lambda e, b=b, c0=c0, n=n: e.tensor_tensor(out=self.ucv[:, 16 + c0:16 + c0 + n], in0=self.ucv[:, 16 + c0:16 + c0 + n],
                                                                         in1=self.ps[b][:, 0:n], op=ALU.mult),
                      reads=[("ps", b), ("ucv", t)], writes=[("ucv", t)])
            w, wk = self.half_slab()
            for t in range(5):
                c0, n = TT[t]
                b = self.proj_tile(w, wk, uT_ap, ukey, t)
                ti = self.rot("tmp")
                tm = self.tmp[ti][:, 0:n]
                prevk = [("ucv", t - 1)] if t > 0 else ["ucvpad"]
                cwo = 104 + ci * 3
                S.add("act", lambda e, tm=tm, c0=c0, n=n, cwo=cwo: e.activation(out=tm, in_=self.ucv[:, 16 + c0:16 + c0 + n], func=AF.Copy,
                                                                                scale=V[:, cwo:cwo + 1]),
                      reads=[("ucv", t), "vecs"], writes=[("tmp", ti)])
                for j in (1, 2):
                    S.add("dve", lambda e, tm=tm, c0=c0, n=n, cwo=cwo, j=j: e.scalar_tensor_tensor(
                        out=tm, in0=self.ucv[:, 16 + c0 - j:16 + c0 - j + n], scalar=V[:, cwo + j:cwo + j + 1], in1=tm,
                        op0=ALU.mult, op1=ALU.add),
                        reads=[("ucv", t), ("tmp", ti), "vecs"] + prevk, writes=[("tmp", ti)])
                S.add("dve", lambda e, tm=tm, b=b, c0=c0, n=n, ci=ci: e.tensor_tensor(out=self.ymC[:, ci * T + c0:ci * T + c0 + n], in0=tm,
                                                                                       in1=self.ps[b][:, 0:n], op=ALU.mult),
                      reads=[("tmp", ti), ("ps", b)], writes=[("ymC", ci, t)])
        self.fence(["btab", "mtab", "bmeta"], [("q", c, t) for c in range(4) for t in range(5)])
        self.fence([("ucv", t) for t in range(5)] + ["ucvpad"],
                   [("Pt", i) for i in range(2)] + [("Pm", i) for i in range(2)] + [("d2", i) for i in range(2)])
        for kzc in range(4):
            r0 = 64 if kzc % 2 == 0 else 0
            for t in range(5):
                c0, n = TT[t]
                S.add("pool", lambda e, kzc=kzc, r0=r0, c0=c0, n=n: e.memset(self.kz[r0:r0 + 64, kzc * T + c0:kzc * T + c0 + n], 0.0),
                      writes=[("kz", kzc, t)])
        ev = 0
        for dst, name in ((self.qT, "q"),):
            for c in range(4):
                w, wk = self.half_slab()
                for t in range(5):
                    c0, n = TT[t]
                    b = self.proj_tile(w, wk, uT_ap, ukey, t)
                    o = c * T + c0
                    if ev % 2 == 0:
                        S.add("act", lambda e, b=b, o=o, n=n, dst=dst: e.activation(out=dst[:, o:o + n], in_=self.ps[b][:, 0:n], func=AF.Copy),
                              reads=[("ps", b)], writes=[(name, c, t)])
                    else:
                        S.add("dve", lambda e, b=b, o=o, n=n, dst=dst: e.tensor_copy(out=dst[:, o:o + n], in_=self.ps[b][:, 0:n]),
                              reads=[("ps", b)], writes=[(name, c, t)])
                    ev += 1
        for kv in range(2):
            w, wk = self.half_slab()
            for t in range(5):
                c0, n = TT[t]
                b = self.proj_tile(w, wk, uT_ap, ukey, t)
                oa = (kv * 2) * T + c0
                ob = (kv * 2 + 1) * T + c0
                S.add("act", lambda e, b=b, oa=oa, n=n: e.activation(out=self.kz[0:64, oa:oa + n], in_=self.ps[b][0:64, 0:n], func=AF.Copy),
                      reads=[("ps", b)], writes=[("kz", kv * 2, t)])
                S.add("dve", lambda e, b=b, ob=ob, n=n: e.tensor_copy(out=self.kz[64:128, ob:ob + n], in_=self.ps[b][64:128, 0:n]),
                      reads=[("ps", b)], writes=[("kz", kv * 2 + 1, t)])
        w, wk = self.half_slab()
        _pad, _ = self.half_slab()

        def blk(tb):
            return (0, 16, 0) if tb == 0 else (16 + 128 * (tb - 1), 128, 1 + (tb - 1) // 4)

        for tb in range(17):
            c0, n, t = blk(tb)
            b = self.nb()
            for kk in range(KC):
                S.add("pe", lambda e, kk=kk, b=b, c0=c0, n=n, w=w: e.matmul(self.ps[b][0:n, 0:128], lhsT=self.uT[:, kk * T + c0:kk * T + c0 + n],
                                                                        rhs=w[:, kk * 128:(kk + 1) * 128], start=(kk == 0), stop=(kk == KC - 1)),
                      reads=wk + [("uT", kk, t)], writes=[("ps", b)])
            S.add("act", lambda e, b=b, n=n, tb=tb: e.activation(out=self.Vt[0:n, tb * 128:(tb + 1) * 128], in_=self.ps[b][0:n, 0:128], func=AF.Copy),
                  reads=[("ps", b)], writes=["V"])
        self.fence([("uT", c, t) for c in range(KC) for t in range(5)],
                   [("ymA", c, t) for c in range(4) for t in range(5)] + ["woutb"])
        def attn_scores(bq, c):
            c0, n, tq = blk(bq)
            kv = c // 2
            qrhs = self.qT[:, c * T + c0:c * T + c0 + n]
            qkey = ("q", c, tq)
            nk_main = 0 if bq == 0 else (1 if bq == 1 else 2)
            pi = self.rot("Pt")
            pmi = self.rot("Pm")
            Pt = self.Pt[pi]
            Pm = self.Pm[pmi]
            if nk_main:
                bs = self.nb()
                for hh in range(2):
                    kzc = kv * 2 + hh
                    for k in range(nk_main):
                        kb = bq - k
                        kc0, kn, kt = blk(kb)
                        S.add("pe", lambda e, bs=bs, hh=hh, k=k, kzc=kzc, kc0=kc0, qrhs=qrhs: e.matmul(
                            self.ps[bs][:, (hh * 2 + k) * 128:(hh * 2 + k + 1) * 128],
                            lhsT=self.kz[:, kzc * T + kc0:kzc * T + kc0 + 128], rhs=qrhs, start=True, stop=True),
                            reads=[("kz", kzc, kt), qkey], writes=[("ps", bs)])
                ti = self.rot("tmp")
                if nk_main == 2:
                    src = self.ps[bs][:, 0:512]
                    tm = self.tmp[ti][:, 0:512]
                    pt = Pt[:, 0:512]
                    eb = self.EBT[:, 2 * c * 256:(2 * c + 2) * 256]
                else:
                    v3 = lambda ap: ap.rearrange("p (h x) -> p h x", h=2)[:, :, 0:128]
                    src = v3(self.ps[bs][:, 0:512])
                    tm = v3(self.tmp[ti][:, 0:512])
                    pt = v3(Pt[:, 0:512])
                    eb = v3(self.EBT[:, 2 * c * 256:(2 * c + 2) * 256])
                S.add("act", lambda e, src=src, tm=tm: e.activation(out=tm, in_=src, func=AF.Exp, scale=0.125),
                      reads=[("ps", bs)], writes=[("tmp", ti)])
                S.add("dve", lambda e, tm=tm, pt=pt, eb=eb: e.tensor_tensor(out=pt, in0=tm, in1=eb, op=ALU.mult),
                      reads=[("tmp", ti), "EBT"], writes=[("Pt", pi)])
            bm = self.nb()
            for hh in range(2):
                kzc = kv * 2 + hh
                S.add("pe", lambda e, bm=bm, hh=hh, kzc=kzc, qrhs=qrhs, n=n: e.matmul(
                    self.ps[bm][0:16, 256 + hh * n:256 + (hh + 1) * n], lhsT=self.kz[:, kzc * T:kzc * T + 16], rhs=qrhs, start=True, stop=True),
                    reads=[("kz", kzc, 0), qkey], writes=[("ps", bm)])
            ti2 = self.rot("tmp")
            tm2 = self.tmp[ti2][0:16, 0:2 * n]
            S.add("act", lambda e, bm=bm, tm2=tm2, n=n: e.activation(out=tm2, in_=self.ps[bm][0:16, 256:256 + 2 * n], func=AF.Exp, scale=0.125),
                  reads=[("ps", bm)], writes=[("tmp", ti2)])
            if bq == 0:
                S.add("dve", lambda e, tm2=tm2, Pm=Pm, c=c: e.tensor_tensor(out=Pm[0:16, 0:32], in0=tm2, in1=self.EBm0[0:16, 2 * c * 16:(2 * c + 2) * 16], op=ALU.mult),
                      reads=[("tmp", ti2), "EBm0"], writes=[("Pm", pmi)])
            elif bq == 1:
                S.add("dve", lambda e, tm2=tm2, Pm=Pm, c=c: e.tensor_tensor(out=Pm[0:16, 0:256], in0=tm2, in1=self.EBm1[0:16, 2 * c * 128:(2 * c + 2) * 128], op=ALU.mult),
                      reads=[("tmp", ti2), "EBm1"], writes=[("Pm", pmi)])
            else:
                e16 = self.eb31[0:16, 2 * c:2 * c + 2]
                eb_bc = bass.AP(e16.tensor, e16.offset, [list(e16.ap[0]), list(e16.ap[1]), [0, 128]])
                S.add("dve", lambda e, Pm=Pm, ti2=ti2, eb_bc=eb_bc: e.tensor_tensor(
                    out=Pm[0:16, 0:256].rearrange("p (h x) -> p h x", h=2),
                    in0=self.tmp[ti2][0:16, 0:256].rearrange("p (h x) -> p h x", h=2), in1=eb_bc, op=ALU.mult),
                    reads=[("tmp", ti2), "eb31"], writes=[("Pm", pmi)])
            return (bq, c, nk_main, pi, pmi, bm)

        def attn_pv(state):
            bq, c, nk_main, pi, pmi, bm = state
            c0, n, tq = blk(bq)
            kv = c // 2
            Pt = self.Pt[pi]
            Pm = self.Pm[pmi]
            bo = bm
            for hh in range(2):
                rows = slice(hh * 64, hh * 64 + 64)
                chunks = []
                for k in range(nk_main):
                    kb = bq - k
                    chunks.append((self.Vt[:, kb * 128 + kv * 64:kb * 128 + kv * 64 + 64], self.ones1[:, 0:64],
                                   Pt[:, (hh * 2 + k) * 128:(hh * 2 + k) * 128 + n], ("Pt", pi)))
                chunks.append((self.Vt[0:16, kv * 64:kv * 64 + 64], self.ones1[0:16, 0:64], Pm[0:16, hh * n:(hh + 1) * n], ("Pm", pmi)))
                for which in range(2):
                    for ci_, (vl, ol, pr, pk) in enumerate(chunks):
                        lhs = vl if which == 0 else ol
                        S.add("pe", lambda e, bo=bo, rows=rows, which=which, lhs=lhs, pr=pr, ci_=ci_, nch=len(chunks), n=n: e.matmul(
                            self.ps[bo][rows, which * 128:which * 128 + n], lhsT=lhs, rhs=pr, start=(ci_ == 0), stop=(ci_ == nch - 1)),
                            reads=["V", "ones1", pk], writes=[("ps", bo)])
            di = self.rot("d2")
            d2 = self.d2[di][:, 0:n]
            S.add("dve", lambda e, bo=bo, d2=d2, c=c, n=n: e.tensor_scalar(out=d2, in0=self.ps[bo][:, 128:128 + n], scalar1=self.esink[:, c:c + 1],
                                                                           scalar2=None, op0=ALU.add),
                  reads=[("ps", bo), "esink"], writes=[("d2", di)])
            S.add("dve", lambda e, d2=d2: e.reciprocal(out=d2, in_=d2), reads=[("d2", di)], writes=[("d2", di)])
            S.add("dve", lambda e, bo=bo, d2=d2, c=c, c0=c0, n=n: e.tensor_tensor(out=self.ymA[:, c * T + c0:c * T + c0 + n], in0=self.ps[bo][:, 0:n],
                                                                                  in1=d2, op=ALU.mult),
                  reads=[("ps", bo), ("d2", di)], writes=[("ymA", c, tq)])

        iters = [(bq, c) for bq in range(17) for c in range(4)]
        pend = None
        for it, (bq, c) in enumerate(iters):
            if it in (6, 18, 30, 42):
                self.wnext()
            st = attn_scores(bq, c)
            if pend is not None:
                attn_pv(pend)
            pend = st
        attn_pv(pend)
        if self.debug:
            dbg = self.nc.dram_tensor("dbg", [128, 8 * T], BF16, kind="ExternalOutput").ap()
            op = S.add("sp", lambda e: e.dma_start(out=dbg[:, 0:4 * T], in_=self.ymA[:, 0:4 * T]),
                       reads=[("ymA", c, t) for c in range(4) for t in range(5)], dma="dbgA")
            self.out_dmas.append(op)
            op = S.add("sp", lambda e: e.dma_start(out=dbg[:, 4 * T:8 * T], in_=self.ymC[:, 0:4 * T]),
                       reads=[("ymC", c, t) for c in range(4) for t in range(5)], dma="dbgC")
            self.out_dmas.append(op)
            dbg2 = self.nc.dram_tensor("dbg2", [128, 2176 + 2048 + 1024 + 512], BF16, kind="ExternalOutput").ap()
            for (o_, n_, src, key) in ((0, 2176, self.Vt, ["V"]), (2176, 2048, self.EBT, ["EBT"]),
                                        (4224, 512, self.Pt[0], [("Pt", 0)]), (4736, 512, self.Pt[1], [("Pt", 1)]),
                                        (5248, 256, self.Pm[0], [("Pm", 0)]), (5504, 256, self.Pm[1], [("Pm", 1)])):
                op = S.add("sp", lambda e, o_=o_, n_=n_, src=src: e.dma_start(out=dbg2[:, o_:o_ + n_], in_=src[:, 0:n_]), reads=key, dma=("dbg2", o_))
                self.out_dmas.append(op)
            dbg3 = self.nc.dram_tensor("dbg3", [128, 256 + 12], F32, kind="ExternalOutput").ap()
            for (o_, n_, src, key) in ((0, 128, self.d2[0], [("d2", 0)]), (128, 128, self.d2[1], [("d2", 1)]),
                                        (256, 4, self.esink, ["esink"]), (260, 8, self.eb31, ["eb31"])):
                op = S.add("sp", lambda e, o_=o_, n_=n_, src=src: e.dma_start(out=dbg3[:, o_:o_ + n_], in_=src[:, 0:n_]), reads=key, dma=("dbg3", o_))
                self.out_dmas.append(op)
        def ymix_ap(kk, t):
            c0, n = TT[t]
            if kk < 4:
                return self.ymA[:, kk * T + c0:kk * T + c0 + n]
            return self.ymC[:, (kk - 4) * T + c0:(kk - 4) * T + c0 + n]

        ymkey = lambda kk, t: ("ymA", kk, t) if kk < 4 else ("ymC", kk - 4, t)
        self.fence([("q", c, t) for c in range(4) for t in range(5)] + [("kz", c, t) for c in range(4) for t in range(5)],
                   [("ybM", i, m) for i in range(2) for m in range(KC)])
        ybM = [self.ybufM, self.ybufM2]

        def wout_post(t):
            n = TT[t][1]
            yb = ybM[t % 2]
            self.post_tile(t, lambda m, n=n, yb=yb: yb[:, m * 512:m * 512 + n], lambda m, t=t: ("ybM", t % 2, m), 0, 3, pool_ms=(1, 3, 5, 7))

        for t in range(5):
            c0, n = TT[t]
            yb = ybM[t % 2]
            for m in range(KC):
                b = self.proj_tile(self.woutb[:, m * 1024:(m + 1) * 1024], ["woutb"], ymix_ap, ymkey, t)
                S.add("act", lambda e, b=b, m=m, n=n, yb=yb: e.activation(out=yb[:, m * 512:m * 512 + n], in_=self.ps[b][:, 0:n], func=AF.Copy),
                      reads=[("ps", b)], writes=[("ybM", t % 2, m)])
                if m == 3 and t > 0:
                    wout_post(t - 1)
        wout_post(4)
        self.fence(self._m0_keys(), self._ffn_keys())

    def mixer1(self):
        S = self.S
        V = self.vecs
        self.fence(self._ffn_keys(), self._m1_keys())
        S.add("sp", lambda e: e.dma_start(out=self.bandf, in_=self.bands_d), writes=["bandf"], dma="bands")
        for i in range(3):
            S.add("dve", lambda e, i=i: e.tensor_copy(out=self.bandb[:, i * 512:(i + 1) * 512], in_=self.bandf[:, i * 512:(i + 1) * 512]),
                  reads=["bandf"], writes=["bandb"])
        w, wk = self.wnext()
        gidx = 6 + 2

        def uT_ap(c, t):
            c0, n = TT[t]
            return self.uT[:, c * T + c0:c * T + c0 + n]

        def blk(tb):
            return (0, 16, 0) if tb == 0 else (16 + 128 * (tb - 1), 128, 1 + (tb - 1) // 4)

        self._ev = 0

        def zproj(tb):
            c0, n, t = blk(tb)
            zi = tb % 3
            bz = [0, 1]
            for g in range(4):
                for kk in range(2):
                    ch = g * 2 + kk
                    S.add("pe", lambda e, g=g, kk=kk, ch=ch, c0=c0, n=n, b=bz[g // 2]: e.matmul(
                        self.ps[b][0:n, (g % 2) * 256:(g % 2) * 256 + 256], lhsT=self.uT[:, ch * T + c0:ch * T + c0 + n],
                        rhs=w[:, ch * 256:ch * 256 + 256], start=(kk == 0), stop=(kk == 1)),
                        reads=wk + [("uT", ch, t)], writes=[("ps", bz[g // 2])])
            for hb in range(2):
                if self._ev % 2 == 0:
                    S.add("act", lambda e, hb=hb, n=n, zi=zi, b=bz[hb]: e.activation(out=self.zb[zi][0:n, hb * 512:(hb + 1) * 512], in_=self.ps[b][0:n, 0:512], func=AF.Copy),
                          reads=[("ps", bz[hb])], writes=[("z", zi)])
                else:
                    S.add("dve", lambda e, hb=hb, n=n, zi=zi, b=bz[hb]: e.tensor_copy(out=self.zb[zi][0:n, hb * 512:(hb + 1) * 512], in_=self.ps[b][0:n, 0:512]),
                          reads=[("ps", bz[hb])], writes=[("z", zi)])
                self._ev += 1

        def tokmix(tb):
            c0, n, t = blk(tb)
            zi = tb % 3
            zprev = (tb - 1) % 3
            yi = t % 2
            by = [2, 3] if tb % 2 == 0 else [4, 5]
            for m in range(KC):
                g = m // 2
                bo = g * 384
                b = by[m // 4]
                oc0 = (m % 4) * 128
                S.add("pe", lambda e, m=m, n=n, zi=zi, bo=bo, b=b, oc0=oc0, last=(tb == 0): e.matmul(
                    self.ps[b][:, oc0:oc0 + n], lhsT=self.zb[zi][0:n, m * 128:(m + 1) * 128], rhs=self.bandb[0:n, bo:bo + n],
                    start=True, stop=last),
                    reads=[("z", zi), "bandb"], writes=[("ps", b)])
                if tb == 1:
                    S.add("pe", lambda e, m=m, zprev=zprev, bo=bo, b=b, oc0=oc0: e.matmul(
                        self.ps[b][:, oc0:oc0 + 128], lhsT=self.zb[zprev][0:16, m * 128:(m + 1) * 128], rhs=self.bandb[0:16, bo + 256:bo + 384],
                        start=False, stop=True),
                        reads=[("z", zprev), "bandb"], writes=[("ps", b)])
                elif tb >= 2:
                    S.add("pe", lambda e, m=m, zprev=zprev, bo=bo, b=b, oc0=oc0: e.matmul(
                        self.ps[b][:, oc0:oc0 + 128], lhsT=self.zb[zprev][:, m * 128:(m + 1) * 128], rhs=self.bandb[:, bo + 128:bo + 256],
                        start=False, stop=True),
                        reads=[("z", zprev), "bandb"], writes=[("ps", b)])
            lc0 = 0 if tb == 0 else ((tb - 1) % 4) * 128
            for m in range(KC):
                b = by[m // 4]
                oc0 = (m % 4) * 128
                S.add("act", lambda e, m=m, n=n, b=b, oc0=oc0, lc0=lc0, yi=yi: e.activation(
                    out=self.ybufP[yi][:, m * 512 + lc0:m * 512 + lc0 + n], in_=self.ps[b][:, oc0:oc0 + n], func=AF.Copy,
                    scale=V[:, 96 + m:96 + m + 1]),
                    reads=[("ps", b), "vecs"], writes=[("ybP", yi, m)])

        def post(t):
            yi = t % 2
            tn = TT[t][1]
            self.post_tile(t, lambda m, yi=yi, tn=tn: self.ybufP[yi][:, m * 512:m * 512 + tn], lambda m, yi=yi: ("ybP", yi, m), 1, 3)

        def post_parts1(t, nsplit):
            yi = t % 2
            tn = TT[t][1]
            return self.post_parts(t, lambda m, yi=yi, tn=tn: self.ybufP[yi][:, m * 512:m * 512 + tn],
                                   lambda m, yi=yi: ("ybP", yi, m), 1, 3, nsplit=nsplit, pool_ms=(1, 3, 5, 7))

        self.prenorm_tile(0, gidx, lambda c: uT_ap(c, 0), lambda c: ("uT", c, 0))
        self.prenorm_tile(1, gidx, lambda c: uT_ap(c, 1), lambda c: ("uT", c, 1))
        self._nb_override = [[6, 7], 0]
        sched = {}
        for t in range(1, 5):
            tb0 = 4 * (t - 1) + 1
            pre = None
            if t + 1 < 5:
                pre = self.prenorm_parts(t + 1, gidx, lambda c, t=t: uT_ap(c, t + 1), lambda c, t=t: ("uT", c, t + 1), nsplit=4)
            if t == 1:
                post = [lambda: [f() for f in post_parts1(0, 1)]]
            else:
                post = post_parts1(t - 1, 4)
            lst = [[] for _ in range(4)]
            if pre is not None:
                lst[0].append(pre[0])
            if len(post) > 1:
                lst[0].append(post[0])
            if pre is not None:
                lst[0].append(pre[1])
            if len(post) > 1:
                lst[0].append(post[1])
            else:
                lst[0].append(post[0])
            if pre is not None:
                lst[1] += [pre[2], pre[3]]
                lst[2].append(pre[4])
            if len(post) > 1:
                lst[1].append(post[2])
                lst[2].append(post[3])
                lst[3].append(post[4])
            for k in range(4):
                sched[tb0 + k] = lst[k]
        zproj(0)
        for tb in range(17):
            if tb + 1 < 17:
                zproj(tb + 1)
            tokmix(tb)
            for f in sched.get(tb, []):
                f()
        for f in post_parts1(4, 1):
            f()
        self._nb_override = None
        self.fence(self._m1_keys(), self._ffn_keys())

    def build(self, stages):
        items = []

        def ffn_items(f):
            for p in range(2):
                for j in range(FC):
                    items.append((self.w1s[f, j], 2048, None, None))
                for mh in range(16):
                    items.append((self.w2s[f, mh], 1408, None, None))

        for st in stages:
            if st.startswith("ffn"):
                ffn_items(int(st[3]) * 2 + int(st[4]))
            elif st == "mixer0":
                for i in range(10):
                    items.append((self.wins[i], 2048, None, None))
                for i in range(4):
                    items.append((self.wouts[i], 2048, self.woutb[:, i * 2048:(i + 1) * 2048], ["woutb"]))
            elif st == "mixer1":
                items.append((self.pws, 2048, None, None))
        self.wplan(items)
        self.setup()
        last = stages[-1]
        for si, st in enumerate(stages):
            if st.startswith("ffn"):
                nxt = stages[si + 1] if si + 1 < len(stages) else None
                next_ffn = (int(nxt[3]), int(nxt[4])) if (nxt is not None and nxt.startswith("ffn")) else None
                self.ffn(int(st[3]), int(st[4]), final=(st == last), next_ffn=next_ffn)
            elif st == "mixer0":
                self.mixer0()
            elif st == "mixer1":
                self.mixer1()
        S = self.S
        if not last.startswith("ffn"):
            for t in range(1, 5):
                c0, n = TT[t]
                oc0 = c0 - NMETA
                op = S.add("sp", lambda e, c0=c0, n=n, oc0=oc0: e.dma_start(
                    out=self.outT.rearrange("(c p) t -> p c t", p=128)[:, :, oc0:oc0 + n],
                    in_=self.hT[:, :].rearrange("p (c t) -> p c t", c=KC)[:, :, c0:c0 + n]),
                    reads=[("h", m, t) for m in range(KC)], dma=("out", t))
                self.out_dmas.append(op)
        S.finalize(force_signal=self.out_dmas)
        S.emit(self.nc, final_waits=[("sp", p) for p in self.out_dmas])


ALL_STAGES = ["ffn00", "mixer0", "ffn01", "ffn10", "mixer1", "ffn11"]


def build_nc(stages=None, same_engine_sync=True, debug=False):
    stages = stages or ALL_STAGES
    nc = bass.Bass("TRN2", target_bir_lowering=False)
    with contextlib.ExitStack() as es:
        b = Builder(nc, es, same_engine_sync=same_engine_sync, debug=debug)
        b.build(stages)
    return nc


def run(inputs, stages=None, trace=False, n_cores=8, debug=False):
    common = prep_common(inputs)
    x = np.asarray(inputs["x"], dtype=np.float32)
    in_maps = []
    for i in range(n_cores):
        m = dict(common)
        m["xT"] = np.ascontiguousarray(x[i].T)
        in_maps.append(m)
    nc = build_nc(stages, debug=debug)
    res = run_bass_kernel_spmd(nc, in_maps, core_ids=list(range(n_cores)), trace=trace)
    out = np.stack([np.ascontiguousarray(r["outT"].T) for r in res.results], axis=0)
    return out.astype(np.float32), res


def kernel(**inputs):
    out, _ = run(inputs)
    return out
```

```python
import contextlib
import numpy as np
import concourse.bass as bass
import concourse.mybir as mybir
from concourse.bass_utils import run_bass_kernel_spmd

F32 = mybir.dt.float32
BF16 = mybir.dt.bfloat16
AF = mybir.ActivationFunctionType
ALU = mybir.AluOpType

D = 1024
KC = 8
NMETA = 16
SEQ = 2048
T = NMETA + SEQ
DFF = 2816
FC = 22
EPS = 1e-6
TT = [(0, 16), (16, 512), (528, 512), (1040, 512), (1552, 512)]
PASSES = [[0, 1, 2], [3, 4]]
PLO = [0, 1040]
PW = 1040
POOL_SIZES = (2, 4, 8, 16)
SLAB = 2048
NS = 2
NB = 3


class _Op:
    __slots__ = ("eng", "fn", "reads", "writes", "dma", "deps", "raw", "signal", "count", "idx")


class Sched:
    def __init__(self, same_engine_sync=True):
        self.ops = []
        self.last_w = {}
        self.readers = {}
        self.same_engine_sync = same_engine_sync

    def add(self, eng, fn, reads=(), writes=(), dma=None):
        op = _Op()
        op.eng = eng
        op.fn = fn
        op.dma = dma
        op.reads = tuple(reads)
        op.writes = tuple(writes)
        op.signal = False
        op.count = None
        op.idx = len(self.ops)
        deps = set()
        for r in op.reads:
            w = self.last_w.get(r)
            if w is not None:
                deps.add(w)
        op.raw = set(deps)
        for w_ in op.writes:
            w = self.last_w.get(w_)
            if w is not None:
                deps.add(w)
            for rd in self.readers.get(w_, ()):
                deps.add(rd)
        deps.discard(op.idx)
        for r in op.reads:
            self.readers.setdefault(r, []).append(op.idx)
        for w_ in op.writes:
            self.last_w[w_] = op.idx
            self.readers[w_] = []
        op.deps = deps
        self.ops.append(op)
        return op

    @staticmethod
    def _key(p):
        return ("dma", p.dma) if p.dma is not None else ("eng", p.eng)

    def finalize(self, force_signal=()):
        ops = self.ops
        for op in ops:
            best = {}
            for d in op.deps:
                p = ops[d]
                key = self._key(p)
                if p.dma is None and p.eng == op.eng and op.dma is None:
                    if p.eng == "pe" or not self.same_engine_sync:
                        continue
                if key not in best or best[key] < d:
                    best[key] = d
            op.deps = set(best.values())
            for d in op.deps:
                ops[d].signal = True
        for p in force_signal:
            p.signal = True
        cnt = {}
        for op in ops:
            if op.signal:
                key = self._key(op)
                cnt[key] = cnt.get(key, 0) + (16 if op.dma is not None else 1)
                op.count = cnt[key]
        self.sem_keys = sorted(cnt.keys(), key=str)

    def emit(self, nc, final_waits=()):
        ops = self.ops
        with contextlib.ExitStack() as es:
            sems = {}
            for i, k in enumerate(self.sem_keys):
                sems[k] = es.enter_context(nc.semaphore("s%d" % i))
            block = es.enter_context(nc.Block())

            def run(engname):
                def body(e):
                    waited = {}
                    for op in ops:
                        if op.eng != engname:
                            continue
                        for d in sorted(op.deps):
                            p = ops[d]
                            key = self._key(p)
                            if waited.get(key, 0) >= p.count:
                                continue
                            waited[key] = p.count
                            e.wait_ge(sems[key], p.count)
                        ins = op.fn(e)
                        if op.signal:
                            ins.then_inc(sems[self._key(op)], 16 if op.dma is not None else 1)
                    for (en, p) in final_waits:
                        if en == engname:
                            e.wait_ge(sems[self._key(p)], p.count)
                return body

            block.tensor(run("pe"))
            block.scalar(run("act"))
            block.vector(run("dve"))
            block.gpsimd(run("pool"))
            block.sync(run("sp"))


def _t5_bucket_np(dist):
    n = np.maximum(dist, 0)
    nf = np.maximum(n, 1).astype(np.float32)
    large = 16 + (np.log(nf / np.float32(16)) / np.float32(np.log(128 / 16)) * np.float32(16)).astype(np.int32)
    large = np.minimum(large, 31)
    return np.where(n < 16, n, large)


def _slabs_w1(w1):
    g = w1[:, :DFF].reshape(KC, 128, FC, 128)
    u = w1[:, DFF:].reshape(KC, 128, FC, 128)
    s = np.concatenate([g, u], axis=0)
    return np.ascontiguousarray(s.transpose(2, 1, 0, 3)).reshape(FC, 128, 2048)


def _slabs_w2(w2):
    s = w2.reshape(2, 11, 128, KC, 128)
    return np.ascontiguousarray(s.transpose(3, 0, 2, 1, 4)).reshape(16, 128, 1408)


def _colslab(w):
    return np.ascontiguousarray(w.reshape(KC, 128, 128).transpose(1, 0, 2)).reshape(128, 1024)


def prep_common(inp):
    f = lambda a: np.ascontiguousarray(np.asarray(a, dtype=np.float32))
    norm_g = f(inp["norm_g"])
    w1 = f(inp["ffn_w1"])
    w2 = f(inp["ffn_w2"])
    w_in = f(inp["mix_w_in"])[0]
    w_out = f(inp["mix_w_out"])[0]
    rel_bias = f(inp["rel_bias"])
    sinks = f(inp["attn_sinks"])[0]
    conv_w = f(inp["conv_w"])[0]
    pool_w = f(inp["pool_w"])[0]
    pool_scale = f(inp["pool_scale"])[0]
    meta = f(inp["meta_tokens"])
    out = {}
    out["metaT"] = np.ascontiguousarray(meta.T)
    gT = norm_g.reshape(12, KC, 128).transpose(2, 0, 1).reshape(128, 96)
    psc = pool_scale.reshape(KC, 128).T
    cw = conv_w.reshape(3, 4, 128).transpose(2, 1, 0).reshape(128, 12)
    sinkP = np.repeat(sinks.reshape(4, 2), 64, axis=1).T
    b31 = np.broadcast_to(rel_bias[31][None, :], (128, 8))
    out["vecs"] = np.ascontiguousarray(np.concatenate([gT, psc, cw, sinkP, b31], axis=1))
    out["w1s"] = np.stack([_slabs_w1(w1[l, s]) for l in range(2) for s in range(2)])
    out["w2s"] = np.stack([_slabs_w2(w2[l, s]) for l in range(2) for s in range(2)])
    q0, k0, v0, b0, c0, x0 = 0, 512, 640, 768, 1280, 1792
    z64 = np.zeros((D, 64), np.float32)
    slabs = []
    for ci in range(4):
        slabs.append(_colslab(w_in[:, c0 + ci * 128:c0 + (ci + 1) * 128]))
        slabs.append(_colslab(w_in[:, x0 + ci * 128:x0 + (ci + 1) * 128]))
        slabs.append(_colslab(w_in[:, b0 + ci * 128:b0 + (ci + 1) * 128]))
    for c in range(4):
        slabs.append(_colslab(w_in[:, q0 + c * 128:q0 + (c + 1) * 128]))
    for kv in range(2):
        kk_ = w_in[:, k0 + kv * 64:k0 + (kv + 1) * 64]
        slabs.append(_colslab(np.concatenate([kk_, kk_], axis=1)))
    slabs.append(_colslab(w_in[:, v0:v0 + 128]))
    slabs.append(np.zeros((128, 1024), np.float32))
    sl = np.stack(slabs)
    out["wins"] = np.ascontiguousarray(sl.reshape(10, 2, 128, 1024).transpose(0, 2, 1, 3)).reshape(10, 128, 2048)
    wo = np.stack([_colslab(w_out[:, m * 128:(m + 1) * 128]) for m in range(8)])
    out["wouts"] = np.ascontiguousarray(wo.reshape(4, 2, 128, 1024).transpose(0, 2, 1, 3)).reshape(4, 128, 2048)
    pw = pool_w.reshape(4, 2, 128, 2, 128)
    out["pws"] = np.ascontiguousarray(pw.transpose(2, 0, 1, 3, 4)).reshape(128, 2048)
    kk_ = np.arange(128)[:, None]
    nn_ = np.arange(128)[None, :]
    bands = np.zeros((128, 4, 3, 128), np.float32)
    for gi, wsz in enumerate(POOL_SIZES):
        dcur = nn_ - kk_
        bands[:, gi, 0] = np.where((dcur >= 0) & (dcur < wsz), 1.0 / wsz, 0.0) - (dcur == 0)
        bands[:, gi, 1] = np.where(128 + dcur < wsz, 1.0 / wsz, 0.0)
        bands[:16, gi, 2] = np.where(16 + nn_ - kk_[:16] < wsz, 1.0 / wsz, 0.0)
    out["bands"] = np.ascontiguousarray(bands.reshape(128, 1536))
    s_ = np.arange(128)[:, None]
    r_ = np.arange(128)[None, :]
    bk_cur = _t5_bucket_np(r_ - s_)
    bk_prev = _t5_bucket_np(128 + r_ - s_)
    bt = np.stack([rel_bias[bk_cur], rel_bias[bk_prev]], axis=0)
    out["btab"] = np.ascontiguousarray(bt.transpose(1, 3, 0, 2)).reshape(128, 8 * 256)
    mcur = (r_ - s_ >= 0).astype(np.float32)
    mprev = (s_ > r_).astype(np.float32)
    out["mtab"] = np.ascontiguousarray(np.stack([mcur, mprev], axis=1)).reshape(128, 256)
    m_ = np.arange(16)[:, None]
    bk_m1 = _t5_bucket_np(16 + r_ - m_)
    rq = np.arange(16)[None, :]
    bk_m0 = _t5_bucket_np(rq - m_)
    bm = np.concatenate([rel_bias[bk_m1].transpose(0, 2, 1), rel_bias[bk_m0].transpose(0, 2, 1)], axis=2)
    mm0 = np.broadcast_to((rq - m_ >= 0).astype(np.float32)[:, None, :], (16, 8, 16))
    out["bmeta"] = np.ascontiguousarray(np.concatenate([bm, mm0], axis=2)).reshape(16, 8 * 160)
    return out


class Builder:
    def __init__(self, nc, es, stop_after=None, debug=False, same_engine_sync=True):
        self.nc = nc
        self.S = Sched(same_engine_sync)
        self.stop_after = stop_after
        self.debug = debug
        S = self.S
        dt = nc.dram_tensor
        self.xT = dt("xT", [D, SEQ], F32, kind="ExternalInput").ap()
        self.metaT = dt("metaT", [D, NMETA], F32, kind="ExternalInput").ap()
        self.vecs_d = dt("vecs", [128, 128], F32, kind="ExternalInput").ap()
        self.w1s = dt("w1s", [4, FC, 128, 2048], F32, kind="ExternalInput").ap()
        self.w2s = dt("w2s", [4, 16, 128, 1408], F32, kind="ExternalInput").ap()
        self.wins = dt("wins", [10, 128, 2048], F32, kind="ExternalInput").ap()
        self.wouts = dt("wouts", [4, 128, 2048], F32, kind="ExternalInput").ap()
        self.pws = dt("pws", [128, 2048], F32, kind="ExternalInput").ap()
        self.bands_d = dt("bands", [128, 1536], F32, kind="ExternalInput").ap()
        self.btab_d = dt("btab", [128, 2048], F32, kind="ExternalInput").ap()
        self.mtab_d = dt("mtab", [128, 256], F32, kind="ExternalInput").ap()
        self.bmeta_d = dt("bmeta", [16, 1280], F32, kind="ExternalInput").ap()
        self.outT = dt("outT", [D, SEQ], F32, kind="ExternalOutput").ap()

        sb = lambda name, n, dtp: es.enter_context(nc.sbuf_tensor(name, [128, n], dtp))
        self.hT = sb("hT", KC * T, F32)
        self.vecs = sb("vecs_sb", 128, F32)
        self.gc = sb("gc", 96, F32)
        self.ones = sb("ones", 128, BF16)
        self.esink = sb("esink", 4, F32)
        self.eb31 = sb("eb31", 8, F32)
        self.stage = [sb("stage%d" % i, SLAB, F32) for i in range(NS)]
        self.wb = [sb("wb%d" % i, SLAB, BF16) for i in range(NB)]
        self.rstd = [sb("rstd%d" % i, 512, F32) for i in range(2)]
        self.sq = [sb("sq%d" % i, 512, BF16) for i in range(2)]
        self.tmp = [sb("tmp%d" % i, 512, F32) for i in range(2)]
        ARENA_F32 = 25408
        self.arena = sb("arena", ARENA_F32, F32)
        self.junk = sb("junk", 8, F32)
        self.ones1 = sb("ones1", 64, BF16)
        self.ps = [es.enter_context(nc.psum_tensor("ps%d" % i, [128, 512], F32)) for i in range(8)]
        self.bank = 0
        self.cnt = {"rstd": 0, "sq": 0, "tmp": 0, "Pt": 0, "Pm": 0, "d2": 0}
        a = self.arena
        self.xn = a[:, 0:4160].bitcast(BF16)
        self.ybuf = a[:, 4160:4160 + 8320]
        self.act = a[:, 12480:12480 + 11440].bitcast(BF16)
        o = 0
        o_u = o
        self.uT = a[:, o:o + 8256].bitcast(BF16); o += 8256
        o_q = o
        self.qT = a[:, o:o + 4128].bitcast(BF16); o += 4128
        self.kz = a[:, o:o + 4128].bitcast(BF16); o += 4128
        self.Vt = a[:, o:o + 1088].bitcast(BF16); o += 1088
        o_c = o
        self.ucv = a[:, o:o + 2080]; o += 2080
        self.ymC = a[:, o:o + 4128].bitcast(BF16); o += 4128
        self.EBT = a[:, o:o + 1024].bitcast(BF16); o += 1024
        self.EBm1 = a[:, o:o + 512].bitcast(BF16); o += 512
        self.EBm0 = a[:, o:o + 64].bitcast(BF16); o += 64
        assert o <= ARENA_F32, o
        self.ymA = a[:, o_u:o_u + 4128].bitcast(BF16)
        self.woutb = a[:, o_u + 4128:o_u + 4128 + 4096].bitcast(BF16)
        self.ybufM = a[:, o_q + 4128:o_q + 4128 + 4096]
        self.ybufM2 = a[:, o_q:o_q + 4096]
        self.btab = a[:, o_q:o_q + 2048]
        self.mtab = a[:, o_q + 2048:o_q + 2304]
        self.bmeta = a[:, o_q + 2304:o_q + 2304 + 1280]
        oc = o_c
        self.Pt = [a[:, oc + i * 256:oc + (i + 1) * 256].bitcast(BF16) for i in range(2)]; oc += 512
        self.Pm = [a[:, oc + i * 128:oc + (i + 1) * 128].bitcast(BF16) for i in range(2)]; oc += 256
        self.d2 = [a[:, oc + i * 128:oc + (i + 1) * 128] for i in range(2)]; oc += 256
        assert oc <= o_c + 2080
        o = 8256
        self.zb = [a[:, o + i * 512:o + (i + 1) * 512].bitcast(BF16) for i in range(3)]; o += 1536
        self.ybufP = [a[:, o + i * 4096:o + (i + 1) * 4096] for i in range(2)]; o += 8192
        self.bandf = a[:, o:o + 1536]; o += 1536
        self.bandb = a[:, o:o + 768].bitcast(BF16); o += 768
        assert o <= ARENA_F32, o
        self.out_dmas = []
        self.dbg_out = []

    def nb(self):
        ov = getattr(self, "_nb_override", None)
        if ov is not None:
            b = ov[0][ov[1] % len(ov[0])]
            ov[1] += 1
            return b
        b = self.bank
        self.bank = (self.bank + 1) % 8
        return b

    def rot(self, name):
        i = self.cnt[name]
        self.cnt[name] = (i + 1) % 2
        return i

    def hsl(self, c, c0, n):
        return self.hT[:, c * T + c0:c * T + c0 + n]

    def fence(self, reads, writes):
        self.S.add("pool", lambda e: e.memset(self.junk[0:1, 0:1], 0.0), reads=list(reads), writes=list(reads) + list(writes) + ["junk"])

    def wplan(self, items):
        self.witems = items
        self.w_dma = 0
        self.w_cast = 0
        self.w_use = 0

    def _pump(self, i):
        S = self.S
        n_items = len(self.witems)
        while True:
            prog = False
            if self.w_dma < n_items and self.w_dma - NS < self.w_cast:
                k = self.w_dma
                ap, n, dest, dkeys = self.witems[k]
                sl = k % NS
                S.add("sp", lambda e, sl=sl, n=n, ap=ap: e.dma_start(out=self.stage[sl][:, 0:n], in_=ap),
                      writes=[("stage", sl)], dma=("stage", sl))
                self.w_dma += 1
                prog = True
            if (self.w_cast < n_items and self.w_cast < self.w_dma and self.w_cast <= i + NB - 1
                    and (self.witems[self.w_cast][2] is None or self.w_cast <= i)):
                k = self.w_cast
                ap, n, dest, dkeys = self.witems[k]
                sl = k % NS
                if dest is None:
                    dst = self.wb[k % NB][:, 0:n]
                    wk = [("wb", k % NB)]
                else:
                    dst = dest
                    wk = dkeys
                for p0 in range(0, n, 512):
                    p1 = min(n, p0 + 512)
                    if p0 == 0:
                        S.add("pool", lambda e, dst=dst, sl=sl, p0=p0, p1=p1: e.tensor_copy(out=dst[:, p0:p1], in_=self.stage[sl][:, p0:p1]),
                              reads=[("stage", sl)], writes=wk)
                    else:
                        S.add("act", lambda e, dst=dst, sl=sl, p0=p0, p1=p1: e.activation(out=dst[:, p0:p1], in_=self.stage[sl][:, p0:p1], func=AF.Copy),
                              reads=[("stage", sl)], writes=wk)
                self.w_cast += 1
                prog = True
            if not prog:
                break

    def wnext(self):
        i = self.w_use
        self.w_use += 1
        self._pump(i)
        ap, n, dest, dkeys = self.witems[i]
        if dest is None:
            return self.wb[i % NB][:, 0:n], [("wb", i % NB)]
        return dest, dkeys

    def setup(self):
        S = self.S
        S.add("sp", lambda e: e.dma_start(out=self.vecs[:, :], in_=self.vecs_d), writes=["vecs"], dma="vecs")
        S.add("sp", lambda e: e.dma_start(out=self.hT[:, :].rearrange("p (c t) -> p c t", c=KC)[:, :, 0:NMETA],
                                           in_=self.metaT.rearrange("(c p) t -> p c t", p=128)),
              writes=[("h", c, 0) for c in range(KC)], dma="meta")
        for t in range(1, 5):
            c0, n = TT[t]
            S.add("sp", lambda e, c0=c0, n=n: e.dma_start(out=self.hT[:, :].rearrange("p (c t) -> p c t", c=KC)[:, :, c0:c0 + n],
                                                           in_=self.xT.rearrange("(c p) t -> p c t", p=128)[:, :, c0 - NMETA:c0 - NMETA + n]),
                  writes=[("h", c, t) for c in range(KC)], dma=("x", t))
        S.add("pool", lambda e: e.memset(self.ones[:, :], 1.0 / 1024), writes=["ones"])
        S.add("pool", lambda e: e.memset(self.ones1[:, :], 1.0), writes=["ones1"])
        for l in range(2):
            for i, coef in ((1, 0.5), (3, 1.0), (5, 0.5)):
                o = (l * 6 + i) * 8
                S.add("dve", lambda e, o=o, coef=coef: e.tensor_scalar(out=self.gc[:, o:o + 8], in0=self.vecs[:, o:o + 8],
                                                                       scalar1=coef, scalar2=None, op0=ALU.mult),
                      reads=["vecs"], writes=[("gc", l, i)])

    def rms_stats(self, src_fn, src_keys_fn, t, nchunks=8):
        S = self.S
        c0, n = TT[t]
        b = self.nb()
        for c in range(nchunks):
            si = self.rot("sq")
            S.add("act", lambda e, c=c, si=si: e.activation(out=self.sq[si][:, 0:n], in_=src_fn(c), func=AF.Square),
                  reads=src_keys_fn(c), writes=[("sq", si)])
            S.add("pe", lambda e, c=c, si=si, b=b: e.matmul(self.ps[b][:, 0:n], lhsT=self.ones[:, :], rhs=self.sq[si][:, 0:n],
                                                            start=(c == 0), stop=(c == nchunks - 1)),
                  reads=[("sq", si), "ones"], writes=[("ps", b)])
        ri = self.rot("rstd")
        r = self.rstd[ri]
        S.add("dve", lambda e, b=b: e.tensor_scalar(out=r[:, 0:n], in0=self.ps[b][:, 0:n], scalar1=EPS, scalar2=None, op0=ALU.add),
              reads=[("ps", b)], writes=[("rstd", ri)])
        S.add("dve", lambda e: e.reciprocal(out=r[:, 0:n], in_=r[:, 0:n]), reads=[("rstd", ri)], writes=[("rstd", ri)])
        S.add("act", lambda e: e.activation(out=r[:, 0:n], in_=r[:, 0:n], func=AF.Sqrt), reads=[("rstd", ri)], writes=[("rstd", ri)])
        return ri

    def prenorm_tile(self, t, gidx, dst_fn, dst_key_fn):
        S = self.S
        c0, n = TT[t]
        ri = self.rms_stats(lambda c: self.hsl(c, c0, n), lambda c: [("h", c, t)], t)
        for c in range(KC):
            S.add("dve", lambda e, c=c: e.scalar_tensor_tensor(out=dst_fn(c), in0=self.hsl(c, c0, n),
                                                               scalar=self.vecs[:, gidx * 8 + c:gidx * 8 + c + 1],
                                                               in1=self.rstd[ri][:, 0:n], op0=ALU.mult, op1=ALU.mult),
                  reads=[("h", c, t), ("rstd", ri), "vecs"], writes=[dst_key_fn(c)])

    def prenorm_parts(self, t, gidx, dst_fn, dst_key_fn, nsplit=4):
        S = self.S
        c0, n = TT[t]
        st = {}

        def stats():
            st["ri"] = self.rms_stats(lambda c: self.hsl(c, c0, n), lambda c: [("h", c, t)], t)

        def apply(cs):
            ri = st["ri"]
            for c in cs:
                S.add("dve", lambda e, c=c: e.scalar_tensor_tensor(out=dst_fn(c), in0=self.hsl(c, c0, n),
                                                                   scalar=self.vecs[:, gidx * 8 + c:gidx * 8 + c + 1],
                                                                   in1=self.rstd[ri][:, 0:n], op0=ALU.mult, op1=ALU.mult),
                      reads=[("h", c, t), ("rstd", ri), "vecs"], writes=[dst_key_fn(c)])

        per = KC // nsplit
        return [stats] + [(lambda cs=list(range(k * per, (k + 1) * per)): apply(cs)) for k in range(nsplit)]

    def post_parts(self, t, ysrc_fn, ykey_fn, l, i, final=False, nsplit=2, add_eng="dve", pool_ms=()):
        S = self.S
        c0, n = TT[t]
        go = (l * 6 + i) * 8
        st = {}

        def stats():
            st["ri"] = self.rms_stats(ysrc_fn, lambda m: [ykey_fn(m)], t)

        def apply(ms, last):
            ri = st["ri"]
            for m in ms:
                if m in pool_ms:
                    S.add("dve", lambda e, m=m: e.scalar_tensor_tensor(out=ysrc_fn(m), in0=ysrc_fn(m),
                                                                        scalar=self.gc[:, go + m:go + m + 1],
                                                                        in1=self.rstd[ri][:, 0:n], op0=ALU.mult, op1=ALU.mult),
                          reads=[ykey_fn(m), ("rstd", ri), ("gc", l, i)], writes=[ykey_fn(m)])
                    S.add("pool", lambda e, m=m: e.tensor_tensor(out=self.hsl(m, c0, n), in0=self.hsl(m, c0, n),
                                                                 in1=ysrc_fn(m), op=ALU.add),
                          reads=[ykey_fn(m), ("h", m, t)], writes=[("h", m, t)])
                    continue
                ti = self.rot("tmp")
                S.add("dve", lambda e, m=m, ti=ti: e.scalar_tensor_tensor(out=self.tmp[ti][:, 0:n], in0=ysrc_fn(m),
                                                                          scalar=self.gc[:, go + m:go + m + 1],
                                                                          in1=self.rstd[ri][:, 0:n], op0=ALU.mult, op1=ALU.mult),
                      reads=[ykey_fn(m), ("rstd", ri), ("gc", l, i)], writes=[("tmp", ti)])
                S.add(add_eng, lambda e, m=m, ti=ti: e.tensor_tensor(out=self.hsl(m, c0, n), in0=self.hsl(m, c0, n),
                                                                     in1=self.tmp[ti][:, 0:n], op=ALU.add),
                      reads=[("tmp", ti), ("h", m, t)], writes=[("h", m, t)])
            if last and final and t > 0:
                oc0 = c0 - NMETA
                op = S.add("sp", lambda e: e.dma_start(out=self.outT.rearrange("(c p) t -> p c t", p=128)[:, :, oc0:oc0 + n],
                                                       in_=self.hT[:, :].rearrange("p (c t) -> p c t", c=KC)[:, :, c0:c0 + n]),
                           reads=[("h", m, t) for m in range(KC)], dma=("out", t))
                self.out_dmas.append(op)

        per = KC // nsplit
        parts = [stats]
        for k in range(nsplit):
            ms = list(range(k * per, (k + 1) * per))
            parts.append(lambda ms=ms, last=(k == nsplit - 1): apply(ms, last))
        return parts

    def post_tile(self, t, ysrc_fn, ykey_fn, l, i, final=False, add_eng="dve", pool_ms=()):
        for f in self.post_parts(t, ysrc_fn, ykey_fn, l, i, final=final, nsplit=1, add_eng=add_eng, pool_ms=pool_ms):
            f()

    def ffn(self, l, s, final=False, next_ffn=None):
        S = self.S
        f = l * 2 + s
        gpre = l * 6 + (0 if s == 0 else 4)
        ipost = 1 if s == 0 else 5

        def xn_ap(p, c, t):
            c0, n = TT[t]
            off = c * PW + (c0 - PLO[p])
            return self.xn[:, off:off + n]

        def act_ap(p, j, t):
            c0, n = TT[t]
            off = j * PW + (c0 - PLO[p])
            return self.act[:, off:off + n]

        def y_ap(p, m, t):
            c0, n = TT[t]
            off = m * PW + (c0 - PLO[p])
            return self.ybuf[:, off:off + n]

        PASSES_ = [[t for t in ps_ if not (final and t == 0)] for ps_ in PASSES]

        def prenorm(p):
            for t in PASSES_[p]:
                self.prenorm_tile(t, gpre, lambda c, t=t: xn_ap(p, c, t), lambda c, t=t: ("xn", c, t))

        def w1phase(p, hooks=None):
            for j in range(FC):
                w, wk = self.wnext()
                for t in PASSES_[p]:
                    c0, n = TT[t]
                    ba = self.nb()
                    bb = self.nb()
                    for half, bk in ((0, ba), (1, bb)):
                        for kk in range(KC):
                            S.add("pe", lambda e, kk=kk, bk=bk, half=half, t=t, n=n, w=w: e.matmul(
                                self.ps[bk][:, 0:n], lhsT=w[:, (half * 8 + kk) * 128:(half * 8 + kk + 1) * 128],
                                rhs=xn_ap(p, kk, t), start=(kk == 0), stop=(kk == KC - 1)),
                                reads=wk + [("xn", kk, t)], writes=[("ps", bk)])
                    ti = self.rot("tmp")
                    S.add("act", lambda e, ba=ba, ti=ti, n=n: e.activation(out=self.tmp[ti][:, 0:n], in_=self.ps[ba][:, 0:n], func=AF.Silu),
                          reads=[("ps", ba)], writes=[("tmp", ti)])
                    S.add("dve", lambda e, bb=bb, ti=ti, n=n, j=j, t=t: e.tensor_tensor(out=act_ap(p, j, t), in0=self.tmp[ti][:, 0:n],
                                                                                         in1=self.ps[bb][:, 0:n], op=ALU.mult),
                          reads=[("tmp", ti), ("ps", bb)], writes=[("act", j, t)])
                if hooks and j in hooks:
                    hooks[j]()

        def w2phase(p):
            for m in range(KC):
                banks = {t: self.nb() for t in PASSES_[p]}
                for half in range(2):
                    w, wk = self.wnext()
                    for t in PASSES_[p]:
                        c0, n = TT[t]
                        for k2 in range(11):
                            kk = half * 11 + k2
                            S.add("pe", lambda e, kk=kk, k2=k2, t=t, n=n, w=w, bk=banks[t]: e.matmul(
                                self.ps[bk][:, 0:n], lhsT=w[:, k2 * 128:(k2 + 1) * 128], rhs=act_ap(p, kk, t),
                                start=(kk == 0), stop=(kk == FC - 1)),
                                reads=wk + [("act", kk, t)], writes=[("ps", banks[t])])
                for t in PASSES_[p]:
                    c0, n = TT[t]
                    S.add("act", lambda e, m=m, t=t, n=n, bk=banks[t]: e.activation(out=y_ap(p, m, t), in_=self.ps[bk][:, 0:n], func=AF.Copy),
                          reads=[("ps", banks[t])], writes=[("y", m, t)])

        def post_t(p, t):
            self.post_tile(t, lambda m, t=t: y_ap(p, m, t), lambda m, t=t: ("y", m, t), l, ipost, final=final)

        def post_hooks(p, tiles, j0):
            hk = {}
            j = j0
            for t in tiles:
                if TT[t][1] <= 16:
                    hk[j] = (lambda t=t: post_t(p, t))
                    j += 1
                    continue
                parts = self.post_parts(t, lambda m, t=t: y_ap(p, m, t), lambda m, t=t: ("y", m, t), l, ipost, final=final, nsplit=4)
                for f in parts:
                    hk[j] = f
                    j += 1
            return hk

        pend = getattr(self, "_pending_post", None)
        self._pending_post = None
        if not getattr(self, "_prenorm0_done", False):
            prenorm(0)
        self._prenorm0_done = False
        pre1 = [self.prenorm_parts(t, gpre, lambda c, t=t: xn_ap(1, c, t), lambda c, t=t: ("xn", c, t), nsplit=1)
                for t in PASSES_[1]]
        hooksA = dict(pend or {})
        assert (FC - 4) not in hooksA and (FC - 2) not in hooksA
        hooksA[FC - 4] = pre1[0][0]
        hooksA[FC - 2] = pre1[1][0]
        w1phase(0, hooks=hooksA)
        for p_ in pre1:
            p_[1]()
        w2phase(0)
        hooksB = post_hooks(0, PASSES_[0], 1)
        npre = None
        if next_ffn is not None:
            nl, ns_ = next_ffn
            ngpre = nl * 6 + (0 if ns_ == 0 else 4)
            npre = [self.prenorm_parts(t, ngpre, lambda c, t=t: xn_ap(0, c, t), lambda c, t=t: ("xn", c, t), nsplit=1)
                    for t in (1, 2)]
            assert (FC - 5) not in hooksB and (FC - 3) not in hooksB and max(hooksB) < FC - 5
            hooksB[FC - 5] = npre[0][0]
            hooksB[FC - 3] = npre[1][0]
        w1phase(1, hooks=hooksB)
        if next_ffn is not None:
            for p_ in npre:
                p_[1]()
            self.prenorm_tile(0, ngpre, lambda c: xn_ap(0, c, 0), lambda c: ("xn", c, 0))
            self._prenorm0_done = True
        w2phase(1)
        if next_ffn is not None:
            self._pending_post = post_hooks(1, [3, 4], 1)
        else:
            post_t(1, 3)
            post_t(1, 4)

    def _ffn_keys(self):
        return ([("xn", c, t) for c in range(KC) for t in range(5)] + [("y", m, t) for m in range(KC) for t in range(5)]
                + [("act", j, t) for j in range(FC) for t in range(5)])

    def _m0_keys(self):
        return (["btab", "mtab", "bmeta", "EBT", "EBm1", "EBm0", "V", "ucvpad", "woutb"]
                + [("uT", c, t) for c in range(KC) for t in range(5)]
                + [("q", c, t) for c in range(4) for t in range(5)] + [("kz", c, t) for c in range(4) for t in range(5)]
                + [("ucv", t) for t in range(5)] + [("ymC", c, t) for c in range(4) for t in range(5)]
                + [("ymA", c, t) for c in range(4) for t in range(5)] + [("ybM", i, m) for i in range(2) for m in range(KC)]
                + [("Pt", i) for i in range(2)] + [("Pm", i) for i in range(2)] + [("d2", i) for i in range(2)])

    def _m1_keys(self):
        return (["bandf", "bandb"] + [("uT", c, t) for c in range(KC) for t in range(5)] + [("z", i) for i in range(3)]
                + [("ybP", i, m) for i in range(2) for m in range(KC)])

    def half_slab(self):
        if getattr(self, "_hs", None) is None:
            w, wk = self.wnext()
            self._hs = (w, wk)
            return w[:, 0:1024], wk
        w, wk = self._hs
        self._hs = None
        return w[:, 1024:2048], wk

    def proj_tile(self, w, wk, src_fn, src_key_fn, t):
        S = self.S
        c0, n = TT[t]
        b = self.nb()
        for kk in range(KC):
            S.add("pe", lambda e, kk=kk, b=b, n=n, t=t: e.matmul(self.ps[b][:, 0:n], lhsT=w[:, kk * 128:(kk + 1) * 128],
                                                                  rhs=src_fn(kk, t), start=(kk == 0), stop=(kk == KC - 1)),
                  reads=wk + [src_key_fn(kk, t)], writes=[("ps", b)])
        return b

    def mixer0(self):
        S = self.S
        V = self.vecs
        self.fence(self._ffn_keys(), self._m0_keys())
        S.add("pool", lambda e: e.memset(self.ucv[:, 0:16], 0.0), writes=["ucvpad"])
        gidx = 2

        def uT_ap(c, t):
            c0, n = TT[t]
            return self.uT[:, c * T + c0:c * T + c0 + n]

        for t in range(5):
            self.prenorm_tile(t, gidx, lambda c, t=t: uT_ap(c, t), lambda c, t=t: ("uT", c, t))
        ukey = lambda kk, t: ("uT", kk, t)
        S.add("sp", lambda e: e.dma_start(out=self.btab, in_=self.btab_d), writes=["btab"], dma="btab")
        S.add("sp", lambda e: e.dma_start(out=self.mtab, in_=self.mtab_d), writes=["mtab"], dma="mtab")
        S.add("sp", lambda e: e.dma_start(out=self.bmeta[0:16, :], in_=self.bmeta_d), writes=["bmeta"], dma="bmeta")
        S.add("act", lambda e: e.activation(out=self.esink[:, :], in_=V[:, 116:120], func=AF.Exp), reads=["vecs"], writes=["esink"])
        S.add("act", lambda e: e.activation(out=self.eb31[:, :], in_=V[:, 120:128], func=AF.Exp), reads=["vecs"], writes=["eb31"])
        for h_ in range(8):
            bt_h = self.btab[:, h_ * 256:(h_ + 1) * 256]
            S.add("act", lambda e, bt_h=bt_h: e.activation(out=bt_h, in_=bt_h, func=AF.Exp), reads=["btab"], writes=["btab"])
            S.add("dve", lambda e, bt_h=bt_h, h_=h_: e.tensor_tensor(out=self.EBT[:, h_ * 256:(h_ + 1) * 256], in0=bt_h, in1=self.mtab, op=ALU.mult),
                  reads=["btab", "mtab"], writes=["EBT"])
            bm_h = self.bmeta[0:16, h_ * 160:(h_ + 1) * 160]
            S.add("act", lambda e, bm_h=bm_h: e.activation(out=bm_h[:, 0:144], in_=bm_h[:, 0:144], func=AF.Exp), reads=["bmeta"], writes=["bmeta"])
            S.add("dve", lambda e, bm_h=bm_h, h_=h_: e.tensor_copy(out=self.EBm1[0:16, h_ * 128:(h_ + 1) * 128], in_=bm_h[:, 0:128]),
                  reads=["bmeta"], writes=["EBm1"])
            S.add("dve", lambda e, bm_h=bm_h, h_=h_: e.tensor_tensor(out=self.EBm0[0:16, h_ * 16:(h_ + 1) * 16], in0=bm_h[:, 128:144],
                                                                       in1=bm_h[:, 144:160], op=ALU.mult),
                  reads=["bmeta"], writes=["EBm0"])
        for ci in range(4):
            w, wk = self.half_slab()
            for t in range(5):
                c0, n = TT[t]
                b = self.proj_tile(w, wk, uT_ap, ukey, t)
                S.add("act", lambda e, b=b, c0=c0, n=n: e.activation(out=self.ucv[:, 16 + c0:16 + c0 + n], in_=self.ps[b][:, 0:n], func=AF.Copy),
                      reads=[("ps", b)], writes=[("ucv", t)])
            w, wk = self.half_slab()
            for t in range(5):
                c0, n = TT[t]
                b = self.proj_tile(w, wk, uT_ap, ukey, t)
                S.add("dve", lambda e, b=b, c0=c0, n=n: e.tensor_tensor(out=self.ucv[:, 16 + c0:16 + c0 + n], in0=self.ucv[:, 16 + c0:16 + c0 + n],
                                                                         in1=self.ps[b][:, 0:n], op=ALU.mult),
                      reads=[("ps", b), ("ucv", t)], writes=[("ucv", t)])
            w, wk = self.half_slab()
            for t in range(5):
                c0, n = TT[t]
                b = self.proj_tile(w, wk, uT_ap, ukey, t)
                ti = self.rot("tmp")
                tm = self.tmp[ti][:, 0:n]
                prevk = [("ucv", t - 1)] if t > 0 else ["ucvpad"]
                cwo = 104 + ci * 3
                S.add("act", lambda e, tm=tm, c0=c0, n=n, cwo=cwo: e.activation(out=tm, in_=self.ucv[:, 16 + c0:16 + c0 + n], func=AF.Copy,
                                                                                scale=V[:, cwo:cwo + 1]),
                      reads=[("ucv", t), "vecs"], writes=[("tmp", ti)])
                for j in (1, 2):
                    S.add("dve", lambda e, tm=tm, c0=c0, n=n, cwo=cwo, j=j: e.scalar_tensor_tensor(
                        out=tm, in0=self.ucv[:, 16 + c0 - j:16 + c0 - j + n], scalar=V[:, cwo + j:cwo + j + 1], in1=tm,
                        op0=ALU.mult, op1=ALU.add),
                        reads=[("ucv", t), ("tmp", ti), "vecs"] + prevk, writes=[("tmp", ti)])
                S.add("dve", lambda e, tm=tm, b=b, c0=c0, n=n, ci=ci: e.tensor_tensor(out=self.ymC[:, ci * T + c0:ci * T + c0 + n], in0=tm,
                                                                                       in1=self.ps[b][:, 0:n], op=ALU.mult),
                      reads=[("tmp", ti), ("ps", b)], writes=[("ymC", ci, t)])
        self.fence(["btab", "mtab", "bmeta"], [("q", c, t) for c in range(4) for t in range(5)])
        self.fence([("ucv", t) for t in range(5)] + ["ucvpad"],
                   [("Pt", i) for i in range(2)] + [("Pm", i) for i in range(2)] + [("d2", i) for i in range(2)])
        for kzc in range(4):
            r0 = 64 if kzc % 2 == 0 else 0
            for t in range(5):
                c0, n = TT[t]
                S.add("pool", lambda e, kzc=kzc, r0=r0, c0=c0, n=n: e.memset(self.kz[r0:r0 + 64, kzc * T + c0:kzc * T + c0 + n], 0.0),
                      writes=[("kz", kzc, t)])
        ev = 0
        for dst, name in ((self.qT, "q"),):
            for c in range(4):
                w, wk = self.half_slab()
                for t in range(5):
                    c0, n = TT[t]
                    b = self.proj_tile(w, wk, uT_ap, ukey, t)
                    o = c * T + c0
                    if ev % 2 == 0:
                        S.add("act", lambda e, b=b, o=o, n=n, dst=dst: e.activation(out=dst[:, o:o + n], in_=self.ps[b][:, 0:n], func=AF.Copy),
                              reads=[("ps", b)], writes=[(name, c, t)])
                    else:
                        S.add("dve", lambda e, b=b, o=o, n=n, dst=dst: e.tensor_copy(out=dst[:, o:o + n], in_=self.ps[b][:, 0:n]),
                              reads=[("ps", b)], writes=[(name, c, t)])
                    ev += 1
        for kv in range(2):
            w, wk = self.half_slab()
            for t in range(5):
                c0, n = TT[t]
                b = self.proj_tile(w, wk, uT_ap, ukey, t)
                oa = (kv * 2) * T + c0
                ob = (kv * 2 + 1) * T + c0
                S.add("act", lambda e, b=b, oa=oa, n=n: e.activation(out=self.kz[0:64, oa:oa + n], in_=self.ps[b][0:64, 0:n], func=AF.Copy),
                      reads=[("ps", b)], writes=[("kz", kv * 2, t)])
                S.add("dve", lambda e, b=b, ob=ob, n=n: e.tensor_copy(out=self.kz[64:128, ob:ob + n], in_=self.ps[b][64:128, 0:n]),
                      reads=[("ps", b)], writes=[("kz", kv * 2 + 1, t)])
        w, wk = self.half_slab()
        _pad, _ = self.half_slab()

        def blk(tb):
            return (0, 16, 0) if tb == 0 else (16 + 128 * (tb - 1), 128, 1 + (tb - 1) // 4)

        for tb in range(17):
            c0, n, t = blk(tb)
            b = self.nb()
            for kk in range(KC):
                S.add("pe", lambda e, kk=kk, b=b, c0=c0, n=n, w=w: e.matmul(self.ps[b][0:n, 0:128], lhsT=self.uT[:, kk * T + c0:kk * T + c0 + n],
                                                                        rhs=w[:, kk * 128:(kk + 1) * 128], start=(kk == 0), stop=(kk == KC - 1)),
                      reads=wk + [("uT", kk, t)], writes=[("ps", b)])
            S.add("act", lambda e, b=b, n=n, tb=tb: e.activation(out=self.Vt[0:n, tb * 128:(tb + 1) * 128], in_=self.ps[b][0:n, 0:128], func=AF.Copy),
                  reads=[("ps", b)], writes=["V"])
        self.fence([("uT", c, t) for c in range(KC) for t in range(5)],
                   [("ymA", c, t) for c in range(4) for t in range(5)] + ["woutb"])
        def attn_scores(bq, c):
            c0, n, tq = blk(bq)
            kv = c // 2
            qrhs = self.qT[:, c * T + c0:c * T + c0 + n]
            qkey = ("q", c, tq)
            nk_main = 0 if bq == 0 else (1 if bq == 1 else 2)
            pi = self.rot("Pt")
            pmi = self.rot("Pm")
            Pt = self.Pt[pi]
            Pm = self.Pm[pmi]
            if nk_main:
                bs = self.nb()
                for hh in range(2):
                    kzc = kv * 2 + hh
                    for k in range(nk_main):
                        kb = bq - k
                        kc0, kn, kt = blk(kb)
                        S.add("pe", lambda e, bs=bs, hh=hh, k=k, kzc=kzc, kc0=kc0, qrhs=qrhs: e.matmul(
                            self.ps[bs][:, (hh * 2 + k) * 128:(hh * 2 + k + 1) * 128],
                            lhsT=self.kz[:, kzc * T + kc0:kzc * T + kc0 + 128], rhs=qrhs, start=True, stop=True),
                            reads=[("kz", kzc, kt), qkey], writes=[("ps", bs)])
                ti = self.rot("tmp")
                if nk_main == 2:
                    src = self.ps[bs][:, 0:512]
                    tm = self.tmp[ti][:, 0:512]
                    pt = Pt[:, 0:512]
                    eb = self.EBT[:, 2 * c * 256:(2 * c + 2) * 256]
                else:
                    v3 = lambda ap: ap.rearrange("p (h x) -> p h x", h=2)[:, :, 0:128]
                    src = v3(self.ps[bs][:, 0:512])
                    tm = v3(self.tmp[ti][:, 0:512])
                    pt = v3(Pt[:, 0:512])
                    eb = v3(self.EBT[:, 2 * c * 256:(2 * c + 2) * 256])
                S.add("act", lambda e, src=src, tm=tm: e.activation(out=tm, in_=src, func=AF.Exp, scale=0.125),
                      reads=[("ps", bs)], writes=[("tmp", ti)])
                S.add("dve", lambda e, tm=tm, pt=pt, eb=eb: e.tensor_tensor(out=pt, in0=tm, in1=eb, op=ALU.mult),
                      reads=[("tmp", ti), "EBT"], writes=[("Pt", pi)])
            bm = self.nb()
            for hh in range(2):
                kzc = kv * 2 + hh
                S.add("pe", lambda e, bm=bm, hh=hh, kzc=kzc, qrhs=qrhs, n=n: e.matmul(
                    self.ps[bm][0:16, hh * n:(hh + 1) * n], lhsT=self.kz[:, kzc * T:kzc * T + 16], rhs=qrhs, start=True, stop=True),
                    reads=[("kz", kzc, 0), qkey], writes=[("ps", bm)])
            ti2 = self.rot("tmp")
            tm2 = self.tmp[ti2][0:16, 0:2 * n]
            S.add("act", lambda e, bm=bm, tm2=tm2, n=n: e.activation(out=tm2, in_=self.ps[bm][0:16, 0:2 * n], func=AF.Exp, scale=0.125),
                  reads=[("ps", bm)], writes=[("tmp", ti2)])
            if bq == 0:
                S.add("dve", lambda e, tm2=tm2, Pm=Pm, c=c: e.tensor_tensor(out=Pm[0:16, 0:32], in0=tm2, in1=self.EBm0[0:16, 2 * c * 16:(2 * c + 2) * 16], op=ALU.mult),
                      reads=[("tmp", ti2), "EBm0"], writes=[("Pm", pmi)])
            elif bq == 1:
                S.add("dve", lambda e, tm2=tm2, Pm=Pm, c=c: e.tensor_tensor(out=Pm[0:16, 0:256], in0=tm2, in1=self.EBm1[0:16, 2 * c * 128:(2 * c + 2) * 128], op=ALU.mult),
                      reads=[("tmp", ti2), "EBm1"], writes=[("Pm", pmi)])
            else:
                e16 = self.eb31[0:16, 2 * c:2 * c + 2]
                eb_bc = bass.AP(e16.tensor, e16.offset, [list(e16.ap[0]), list(e16.ap[1]), [0, 128]])
                S.add("dve", lambda e, Pm=Pm, ti2=ti2, eb_bc=eb_bc: e.tensor_tensor(
                    out=Pm[0:16, 0:256].rearrange("p (h x) -> p h x", h=2),
                    in0=self.tmp[ti2][0:16, 0:256].rearrange("p (h x) -> p h x", h=2), in1=eb_bc, op=ALU.mult),
                    reads=[("tmp", ti2), "eb31"], writes=[("Pm", pmi)])
            return (bq, c, nk_main, pi, pmi)

        def attn_pv(state):
            bq, c, nk_main, pi, pmi = state
            c0, n, tq = blk(bq)
            kv = c // 2
            Pt = self.Pt[pi]
            Pm = self.Pm[pmi]
            bo = self.nb()
            for hh in range(2):
                rows = slice(hh * 64, hh * 64 + 64)
                chunks = []
                for k in range(nk_main):
                    kb = bq - k
                    chunks.append((self.Vt[:, kb * 128 + kv * 64:kb * 128 + kv * 64 + 64], self.ones1[:, 0:64],
                                   Pt[:, (hh * 2 + k) * 128:(hh * 2 + k) * 128 + n], ("Pt", pi)))
                chunks.append((self.Vt[0:16, kv * 64:kv * 64 + 64], self.ones1[0:16, 0:64], Pm[0:16, hh * n:(hh + 1) * n], ("Pm", pmi)))
                for which in range(2):
                    for ci_, (vl, ol, pr, pk) in enumerate(chunks):
                        lhs = vl if which == 0 else ol
                        S.add("pe", lambda e, bo=bo, rows=rows, which=which, lhs=lhs, pr=pr, ci_=ci_, nch=len(chunks), n=n: e.matmul(
                            self.ps[bo][rows, which * 128:which * 128 + n], lhsT=lhs, rhs=pr, start=(ci_ == 0), stop=(ci_ == nch - 1)),
                            reads=["V", "ones1", pk], writes=[("ps", bo)])
            di = self.rot("d2")
            d2 = self.d2[di][:, 0:n]
            S.add("dve", lambda e, bo=bo, d2=d2, c=c, n=n: e.tensor_scalar(out=d2, in0=self.ps[bo][:, 128:128 + n], scalar1=self.esink[:, c:c + 1],
                                                                           scalar2=None, op0=ALU.add),
                  reads=[("ps", bo), "esink"], writes=[("d2", di)])
            S.add("dve", lambda e, d2=d2: e.reciprocal(out=d2, in_=d2), reads=[("d2", di)], writes=[("d2", di)])
            S.add("dve", lambda e, bo=bo, d2=d2, c=c, c0=c0, n=n: e.tensor_tensor(out=self.ymA[:, c * T + c0:c * T + c0 + n], in0=self.ps[bo][:, 0:n],
                                                                                  in1=d2, op=ALU.mult),
                  reads=[("ps", bo), ("d2", di)], writes=[("ymA", c, tq)])

        iters = [(bq, c) for bq in range(17) for c in range(4)]
        pend = None
        for it, (bq, c) in enumerate(iters):
            if it in (6, 18, 30, 42):
                self.wnext()
            st = attn_scores(bq, c)
            if pend is not None:
                attn_pv(pend)
            pend = st
        attn_pv(pend)
        if self.debug:
            dbg = self.nc.dram_tensor("dbg", [128, 8 * T], BF16, kind="ExternalOutput").ap()
            op = S.add("sp", lambda e: e.dma_start(out=dbg[:, 0:4 * T], in_=self.ymA[:, 0:4 * T]),
                       reads=[("ymA", c, t) for c in range(4) for t in range(5)], dma="dbgA")
            self.out_dmas.append(op)
            op = S.add("sp", lambda e: e.dma_start(out=dbg[:, 4 * T:8 * T], in_=self.ymC[:, 0:4 * T]),
                       reads=[("ymC", c, t) for c in range(4) for t in range(5)], dma="dbgC")
            self.out_dmas.append(op)
            dbg2 = self.nc.dram_tensor("dbg2", [128, 2176 + 2048 + 1024 + 512], BF16, kind="ExternalOutput").ap()
            for (o_, n_, src, key) in ((0, 2176, self.Vt, ["V"]), (2176, 2048, self.EBT, ["EBT"]),
                                        (4224, 512, self.Pt[0], [("Pt", 0)]), (4736, 512, self.Pt[1], [("Pt", 1)]),
                                        (5248, 256, self.Pm[0], [("Pm", 0)]), (5504, 256, self.Pm[1], [("Pm", 1)])):
                op = S.add("sp", lambda e, o_=o_, n_=n_, src=src: e.dma_start(out=dbg2[:, o_:o_ + n_], in_=src[:, 0:n_]), reads=key, dma=("dbg2", o_))
                self.out_dmas.append(op)
            dbg3 = self.nc.dram_tensor("dbg3", [128, 256 + 12], F32, kind="ExternalOutput").ap()
            for (o_, n_, src, key) in ((0, 128, self.d2[0], [("d2", 0)]), (128, 128, self.d2[1], [("d2", 1)]),
                                        (256, 4, self.esink, ["esink"]), (260, 8, self.eb31, ["eb31"])):
                op = S.add("sp", lambda e, o_=o_, n_=n_, src=src: e.dma_start(out=dbg3[:, o_:o_ + n_], in_=src[:, 0:n_]), reads=key, dma=("dbg3", o_))
                self.out_dmas.append(op)
        def ymix_ap(kk, t):
            c0, n = TT[t]
            if kk < 4:
                return self.ymA[:, kk * T + c0:kk * T + c0 + n]
            return self.ymC[:, (kk - 4) * T + c0:(kk - 4) * T + c0 + n]

        ymkey = lambda kk, t: ("ymA", kk, t) if kk < 4 else ("ymC", kk - 4, t)
        self.fence([("q", c, t) for c in range(4) for t in range(5)] + [("kz", c, t) for c in range(4) for t in range(5)],
                   [("ybM", i, m) for i in range(2) for m in range(KC)])
        ybM = [self.ybufM, self.ybufM2]

        def wout_post(t):
            n = TT[t][1]
            yb = ybM[t % 2]
            self.post_tile(t, lambda m, n=n, yb=yb: yb[:, m * 512:m * 512 + n], lambda m, t=t: ("ybM", t % 2, m), 0, 3, pool_ms=(1, 3, 5, 7))

        for t in range(5):
            c0, n = TT[t]
            yb = ybM[t % 2]
            for m in range(KC):
                b = self.proj_tile(self.woutb[:, m * 1024:(m + 1) * 1024], ["woutb"], ymix_ap, ymkey, t)
                S.add("act", lambda e, b=b, m=m, n=n, yb=yb: e.activation(out=yb[:, m * 512:m * 512 + n], in_=self.ps[b][:, 0:n], func=AF.Copy),
                      reads=[("ps", b)], writes=[("ybM", t % 2, m)])
                if m == 3 and t > 0:
                    wout_post(t - 1)
        wout_post(4)
        self.fence(self._m0_keys(), self._ffn_keys())

    def mixer1(self):
        S = self.S
        V = self.vecs
        self.fence(self._ffn_keys(), self._m1_keys())
        S.add("sp", lambda e: e.dma_start(out=self.bandf, in_=self.bands_d), writes=["bandf"], dma="bands")
        for i in range(3):
            S.add("dve", lambda e, i=i: e.tensor_copy(out=self.bandb[:, i * 512:(i + 1) * 512], in_=self.bandf[:, i * 512:(i + 1) * 512]),
                  reads=["bandf"], writes=["bandb"])
        w, wk = self.wnext()
        gidx = 6 + 2

        def uT_ap(c, t):
            c0, n = TT[t]
            return self.uT[:, c * T + c0:c * T + c0 + n]

        def blk(tb):
            return (0, 16, 0) if tb == 0 else (16 + 128 * (tb - 1), 128, 1 + (tb - 1) // 4)

        self._ev = 0

        def zproj(tb):
            c0, n, t = blk(tb)
            zi = tb % 3
            bz = [0, 1]
            for g in range(4):
                for kk in range(2):
                    ch = g * 2 + kk
                    S.add("pe", lambda e, g=g, kk=kk, ch=ch, c0=c0, n=n, b=bz[g // 2]: e.matmul(
                        self.ps[b][0:n, (g % 2) * 256:(g % 2) * 256 + 256], lhsT=self.uT[:, ch * T + c0:ch * T + c0 + n],
                        rhs=w[:, ch * 256:ch * 256 + 256], start=(kk == 0), stop=(kk == 1)),
                        reads=wk + [("uT", ch, t)], writes=[("ps", bz[g // 2])])
            for hb in range(2):
                if self._ev % 2 == 0:
                    S.add("act", lambda e, hb=hb, n=n, zi=zi, b=bz[hb]: e.activation(out=self.zb[zi][0:n, hb * 512:(hb + 1) * 512], in_=self.ps[b][0:n, 0:512], func=AF.Copy),
                          reads=[("ps", bz[hb])], writes=[("z", zi)])
                else:
                    S.add("dve", lambda e, hb=hb, n=n, zi=zi, b=bz[hb]: e.tensor_copy(out=self.zb[zi][0:n, hb * 512:(hb + 1) * 512], in_=self.ps[b][0:n, 0:512]),
                          reads=[("ps", bz[hb])], writes=[("z", zi)])
                self._ev += 1

        def tokmix(tb):
            c0, n, t = blk(tb)
            zi = tb % 3
            zprev = (tb - 1) % 3
            yi = t % 2
            by = [2, 3] if tb % 2 == 0 else [4, 5]
            for m in range(KC):
                g = m // 2
                bo = g * 384
                b = by[m // 4]
                oc0 = (m % 4) * 128
                S.add("pe", lambda e, m=m, n=n, zi=zi, bo=bo, b=b, oc0=oc0, last=(tb == 0): e.matmul(
                    self.ps[b][:, oc0:oc0 + n], lhsT=self.zb[zi][0:n, m * 128:(m + 1) * 128], rhs=self.bandb[0:n, bo:bo + n],
                    start=True, stop=last),
                    reads=[("z", zi), "bandb"], writes=[("ps", b)])
                if tb == 1:
                    S.add("pe", lambda e, m=m, zprev=zprev, bo=bo, b=b, oc0=oc0: e.matmul(
                        self.ps[b][:, oc0:oc0 + 128], lhsT=self.zb[zprev][0:16, m * 128:(m + 1) * 128], rhs=self.bandb[0:16, bo + 256:bo + 384],
                        start=False, stop=True),
                        reads=[("z", zprev), "bandb"], writes=[("ps", b)])
                elif tb >= 2:
                    S.add("pe", lambda e, m=m, zprev=zprev, bo=bo, b=b, oc0=oc0: e.matmul(
                        self.ps[b][:, oc0:oc0 + 128], lhsT=self.zb[zprev][:, m * 128:(m + 1) * 128], rhs=self.bandb[:, bo + 128:bo + 256],
                        start=False, stop=True),
                        reads=[("z", zprev), "bandb"], writes=[("ps", b)])
            lc0 = 0 if tb == 0 else ((tb - 1) % 4) * 128
            for m in range(KC):
                b = by[m // 4]
                oc0 = (m % 4) * 128
                S.add("act", lambda e, m=m, n=n, b=b, oc0=oc0, lc0=lc0, yi=yi: e.activation(
                    out=self.ybufP[yi][:, m * 512 + lc0:m * 512 + lc0 + n], in_=self.ps[b][:, oc0:oc0 + n], func=AF.Copy,
                    scale=V[:, 96 + m:96 + m + 1]),
                    reads=[("ps", b), "vecs"], writes=[("ybP", yi, m)])

        def post(t):
            yi = t % 2
            tn = TT[t][1]
            self.post_tile(t, lambda m, yi=yi, tn=tn: self.ybufP[yi][:, m * 512:m * 512 + tn], lambda m, yi=yi: ("ybP", yi, m), 1, 3)

        def post_parts1(t, nsplit):
            yi = t % 2
            tn = TT[t][1]
            return self.post_parts(t, lambda m, yi=yi, tn=tn: self.ybufP[yi][:, m * 512:m * 512 + tn],
                                   lambda m, yi=yi: ("ybP", yi, m), 1, 3, nsplit=nsplit, pool_ms=(1, 3, 5, 7))

        self.prenorm_tile(0, gidx, lambda c: uT_ap(c, 0), lambda c: ("uT", c, 0))
        self.prenorm_tile(1, gidx, lambda c: uT_ap(c, 1), lambda c: ("uT", c, 1))
        self._nb_override = [[6, 7], 0]
        sched = {}
        for t in range(1, 5):
            tb0 = 4 * (t - 1) + 1
            pre = None
            if t + 1 < 5:
                pre = self.prenorm_parts(t + 1, gidx, lambda c, t=t: uT_ap(c, t + 1), lambda c, t=t: ("uT", c, t + 1), nsplit=4)
            if t == 1:
                post = [lambda: None] if self._skip_meta_m1 else [lambda: [f() for f in post_parts1(0, 1)]]
            else:
                post = post_parts1(t - 1, 4)
            lst = [[] for _ in range(4)]
            if pre is not None:
                lst[0].append(pre[0])
            if len(post) > 1:
                lst[0].append(post[0])
            if pre is not None:
                lst[0].append(pre[1])
            if len(post) > 1:
                lst[0].append(post[1])
            else:
                lst[0].append(post[0])
            if pre is not None:
                lst[1] += [pre[2], pre[3]]
                lst[2].append(pre[4])
            if len(post) > 1:
                lst[1].append(post[2])
                lst[2].append(post[3])
                lst[3].append(post[4])
            for k in range(4):
                sched[tb0 + k] = lst[k]
        zproj(0)
        for tb in range(17):
            if tb + 1 < 17:
                zproj(tb + 1)
            if not (tb == 0 and self._skip_meta_m1):
                tokmix(tb)
            for f in sched.get(tb, []):
                f()
        for f in post_parts1(4, 1):
            f()
        self._nb_override = None
        self.fence(self._m1_keys(), self._ffn_keys())

    def build(self, stages):
        items = []

        def ffn_items(f):
            for p in range(2):
                for j in range(FC):
                    items.append((self.w1s[f, j], 2048, None, None))
                for mh in range(16):
                    items.append((self.w2s[f, mh], 1408, None, None))

        for st in stages:
            if st.startswith("ffn"):
                ffn_items(int(st[3]) * 2 + int(st[4]))
            elif st == "mixer0":
                for i in range(10):
                    items.append((self.wins[i], 2048, None, None))
                for i in range(4):
                    items.append((self.wouts[i], 2048, self.woutb[:, i * 2048:(i + 1) * 2048], ["woutb"]))
            elif st == "mixer1":
                items.append((self.pws, 2048, None, None))
        self.wplan(items)
        self.setup()
        last = stages[-1]
        for si, st in enumerate(stages):
            if st.startswith("ffn"):
                nxt = stages[si + 1] if si + 1 < len(stages) else None
                next_ffn = (int(nxt[3]), int(nxt[4])) if (nxt is not None and nxt.startswith("ffn")) else None
                self.ffn(int(st[3]), int(st[4]), final=(st == last), next_ffn=next_ffn)
            elif st == "mixer0":
                self.mixer0()
            elif st == "mixer1":
                self._skip_meta_m1 = (stages[si + 1:] == ["ffn11"])
                self.mixer1()
        S = self.S
        if not last.startswith("ffn"):
            for t in range(1, 5):
                c0, n = TT[t]
                oc0 = c0 - NMETA
                op = S.add("sp", lambda e, c0=c0, n=n, oc0=oc0: e.dma_start(
                    out=self.outT.rearrange("(c p) t -> p c t", p=128)[:, :, oc0:oc0 + n],
                    in_=self.hT[:, :].rearrange("p (c t) -> p c t", c=KC)[:, :, c0:c0 + n]),
                    reads=[("h", m, t) for m in range(KC)], dma=("out", t))
                self.out_dmas.append(op)
        S.finalize(force_signal=self.out_dmas)
        S.emit(self.nc, final_waits=[("sp", p) for p in self.out_dmas])


ALL_STAGES = ["ffn00", "mixer0", "ffn01", "ffn10", "mixer1", "ffn11"]


def build_nc(stages=None, same_engine_sync=True, debug=False):
    stages = stages or ALL_STAGES
    nc = bass.Bass("TRN2", target_bir_lowering=False)
    with contextlib.ExitStack() as es:
        b = Builder(nc, es, same_engine_sync=same_engine_sync, debug=debug)
        b.build(stages)
    return nc


def run(inputs, stages=None, trace=False, n_cores=8, debug=False):
    common = prep_common(inputs)
    x = np.asarray(inputs["x"], dtype=np.float32)
    in_maps = []
    for i in range(n_cores):
        m = dict(common)
        m["xT"] = np.ascontiguousarray(x[i].T)
        in_maps.append(m)
    nc = build_nc(stages, debug=debug)
    res = run_bass_kernel_spmd(nc, in_maps, core_ids=list(range(n_cores)), trace=trace)
    out = np.stack([np.ascontiguousarray(r["outT"].T) for r in res.results], axis=0)
    return out.astype(np.float32), res


def kernel(**inputs):
    out, _ = run(inputs)
    return out
```
